# Optimizing a Trainium2 kernel written in Bass

```python
import math
import jax
import jax.numpy as jnp
from jax import lax
import numpy as np

D_MODEL = 1024
BATCH = 8
SEQ = 4096
DEPTH = 2

HEAD_DIM = 64
D_FF = 2816
ROPE_THETA = 10000.0
NORM_EPS = 1e-6
NEG_INF = -1e30

A_HEADS = 4
A_VDIM = 2 * HEAD_DIM
A_OUT = A_HEADS * A_VDIM
A_QBLOCK = 128
A_SIZES = (2 * A_HEADS * HEAD_DIM, 2 * A_HEADS * HEAD_DIM, A_OUT)
A_COLS = sum(A_SIZES)

B_HEADS = 8
B_DIM = B_HEADS * HEAD_DIM
B_W_RANK = 64
B_A_RANK = 64
B_G_RANK = 128
B_GN_EPS = 64e-5
B_SIZES = (B_DIM, B_DIM, B_DIM, B_W_RANK, B_A_RANK, B_G_RANK)
B_COLS = sum(B_SIZES)
AB_COLS = A_COLS + B_COLS

C_HEADS = 8
C_KV_GROUPS = 2
C_HPG = C_HEADS // C_KV_GROUPS
C_OUT = C_HEADS * HEAD_DIM
C_KVW = C_KV_GROUPS * HEAD_DIM
CMP_BLOCK = 32
CMP_STRIDE = 16
CMP_HIDDEN = 256
SLC_BLOCK = 64
N_SELECT = 16
WINDOW = 512
C_QBLOCK = 64
FORCE_BONUS = 1e4
C_SIZES = (C_OUT, C_KVW, C_KVW, C_KVW, C_KVW, C_KVW, C_KVW, 3 * C_HEADS)
C_COLS = sum(C_SIZES)

D_HEADS = 4
D_QK = 64
D_V = 128
D_OUT = D_HEADS * D_V
D_CHUNK = 64
D_CONV = 4
D_SIZES = (D_HEADS * D_QK, D_HEADS * D_QK, D_OUT, D_HEADS, D_HEADS, D_OUT)
D_COLS = sum(D_SIZES)
CD_COLS = C_COLS + D_COLS

kernel_name = 'hybrid_diffattn_rwkv7_nsa_mlstm_macaron'


def rmsnorm(x, w, eps=NORM_EPS):
    xf = x.astype(jnp.float32)
    y = xf * lax.rsqrt(jnp.mean(xf * xf, axis=-1, keepdims=True) + eps)
    return (y * w.astype(jnp.float32)).astype(x.dtype)


def split_cols(p, sizes):
    return jnp.split(p, [int(c) for c in np.cumsum(sizes)[:-1]], axis=-1)


def rope_tables(pos):
    inv = jnp.asarray(ROPE_THETA ** (-np.arange(0, HEAD_DIM, 2, dtype=np.float32) / HEAD_DIM), jnp.float32)
    ang = pos[:, None] * inv[None, :]
    ang = jnp.concatenate([ang, ang], axis=-1)
    return jnp.cos(ang), jnp.sin(ang)


def apply_rope(x, cos, sin):
    x1, x2 = jnp.split(x, 2, axis=-1)
    rot = jnp.concatenate([-x2, x1], axis=-1)
    return (x * cos + rot * sin).astype(x.dtype)


def swiglu(x, wg, wu, wd):
    return (jax.nn.silu(x @ wg) * (x @ wu)) @ wd


def lambda_init(layer):
    return 0.8 - 0.6 * math.exp(-0.3 * layer)


def diff_attention(qa, ka, va, lam, subln_w, lam_init, cos, sin):
    B, T, _ = qa.shape
    q = apply_rope(qa.reshape(B, T, A_HEADS, 2, HEAD_DIM).transpose(0, 2, 3, 1, 4), cos, sin) * (HEAD_DIM ** -0.5)
    k = apply_rope(ka.reshape(B, T, A_HEADS, 2, HEAD_DIM).transpose(0, 2, 3, 1, 4), cos, sin)
    v = va.reshape(B, T, A_HEADS, A_VDIM).transpose(0, 2, 1, 3)
    lam32 = lam.astype(jnp.float32)
    lam_full = jnp.exp(jnp.sum(lam32[0] * lam32[1])) - jnp.exp(jnp.sum(lam32[2] * lam32[3])) + lam_init
    nb = T // A_QBLOCK
    qb = q.reshape(B, A_HEADS, 2, nb, A_QBLOCK, HEAD_DIM).transpose(3, 0, 1, 2, 4, 5)
    kpos = jnp.arange(T)

    def block(args):
        qi, i = args
        s = jnp.einsum('bhcqd,bhckd->bhcqk', qi, k).astype(jnp.float32)
        qpos = i * A_QBLOCK + jnp.arange(A_QBLOCK)
        mask = kpos[None, :] <= qpos[:, None]
        p = jax.nn.softmax(jnp.where(mask, s, NEG_INF), axis=-1)
        attn = p[:, :, 0] - lam_full * p[:, :, 1]
        return jnp.einsum('bhqk,bhke->bhqe', attn.astype(v.dtype), v)

    o = lax.map(block, (qb, jnp.arange(nb)))
    o = o.transpose(1, 0, 3, 2, 4).reshape(B, T, A_HEADS, A_VDIM)
    o = rmsnorm(o, subln_w) * (1.0 - lam_init)
    return o.reshape(B, T, A_OUT)


def rwkv7_time_mix(p, mu, w0, w2, a0, a2, g2, k_k, k_a, r_k, ln_w, ln_b):
    B, T, _ = p.shape
    prev = jnp.pad(p[:, :-1], ((0, 0), (1, 0), (0, 0)))
    xm = p + (prev - p) * mu
    r, k, v, wl, al, gl = split_cols(xm, B_SIZES)
    w = -jax.nn.softplus(-(w0 + jnp.tanh(wl) @ w2)) - 0.5
    a = jax.nn.sigmoid(a0 + al @ a2)
    g = jax.nn.sigmoid(gl) @ g2

    def heads(z):
        return z.reshape(B, T, B_HEADS, HEAD_DIM).astype(jnp.float32)

    kk = heads(k * k_k)
    kk = kk * lax.rsqrt(jnp.sum(kk * kk, axis=-1, keepdims=True) + 1e-12)
    k = k * (1.0 + (a - 1.0) * k_a)
    r_h, k_h, v_h, a_h = heads(r), heads(k), heads(v), heads(a)
    decay = jnp.exp(-jnp.exp(heads(w)))

    def step(S, inp):
        r_t, w_t, k_t, v_t, kk_t, a_t = inp
        S = (S * w_t[:, :, None, :]
             - jnp.einsum('bhvk,bhk->bhv', S, kk_t)[..., None] * (kk_t * a_t)[:, :, None, :]
             + v_t[..., None] * k_t[:, :, None, :])
        return S, jnp.einsum('bhvk,bhk->bhv', S, r_t)

    def tm(z):
        return z.transpose(1, 0, 2, 3)

    S0 = jnp.zeros((B, B_HEADS, HEAD_DIM, HEAD_DIM), jnp.float32)
    _, y = lax.scan(step, S0, (tm(r_h), tm(decay), tm(k_h), tm(v_h), tm(kk), tm(a_h)))
    y = y.transpose(1, 0, 2, 3)
    mean = jnp.mean(y, axis=-1, keepdims=True)
    var = jnp.mean(jnp.square(y - mean), axis=-1, keepdims=True)
    y = (y - mean) * lax.rsqrt(var + B_GN_EPS) * ln_w.reshape(B_HEADS, HEAD_DIM) + ln_b.reshape(B_HEADS, HEAD_DIM)
    bonus = jnp.sum(r_h * k_h * r_k, axis=-1, keepdims=True) * v_h
    out = (y + bonus).reshape(B, T, B_DIM).astype(p.dtype)
    return out * g


def compress_blocks(z, pe, w1, w2):
    B, G, T, d = z.shape
    n_cmp = (T - CMP_BLOCK) // CMP_STRIDE + 1
    idx = CMP_STRIDE * np.arange(n_cmp)[:, None] + np.arange(CMP_BLOCK)[None, :]
    blocks = (z[:, :, idx] + pe).reshape(B, G, n_cmp, CMP_BLOCK * d)
    return jax.nn.gelu(blocks @ w1) @ w2


def selection_map(n_cmp, n_slc):
    c0 = CMP_STRIDE * np.arange(n_cmp)[:, None]
    s0 = SLC_BLOCK * np.arange(n_slc)[None, :]
    shared = np.clip(np.minimum(c0 + CMP_BLOCK, s0 + SLC_BLOCK) - np.maximum(c0, s0), 0, None)
    return jnp.asarray(shared / CMP_BLOCK, jnp.float32)


def native_sparse_attention(q, kc, vc, ks, vs, kw, vw, gate_logits,
                            pe_k, w1_k, w2_k, pe_v, w1_v, w2_v, cos, sin):
    B, T, _ = q.shape
    dt = q.dtype
    q = apply_rope(q.reshape(B, T, C_KV_GROUPS, C_HPG, HEAD_DIM).transpose(0, 2, 3, 1, 4), cos, sin) * (HEAD_DIM ** -0.5)

    def groups(z):
        return z.reshape(B, T, C_KV_GROUPS, HEAD_DIM).transpose(0, 2, 1, 3)

    kc, vc, ks, vs, kw, vw = [groups(z) for z in (kc, vc, ks, vs, kw, vw)]
    ks = apply_rope(ks, cos, sin)
    kw = apply_rope(kw, cos, sin)
    n_cmp = (T - CMP_BLOCK) // CMP_STRIDE + 1
    cmp_end = CMP_STRIDE * np.arange(n_cmp) + CMP_BLOCK - 1
    ccos, csin = rope_tables(jnp.asarray(cmp_end, jnp.float32))
    k_cmp = apply_rope(compress_blocks(kc, pe_k, w1_k, w2_k), ccos, csin)
    v_cmp = compress_blocks(vc, pe_v, w1_v, w2_v)
    n_slc = T // SLC_BLOCK
    n_sel = min(N_SELECT, n_slc)
    sel_map = selection_map(n_cmp, n_slc)
    ks_blocks = ks.reshape(B, C_KV_GROUPS, n_slc, SLC_BLOCK, HEAD_DIM)
    vs_blocks = vs.reshape(B, C_KV_GROUPS, n_slc, SLC_BLOCK, HEAD_DIM)
    kw_pad = jnp.pad(kw, ((0, 0), (0, 0), (WINDOW, 0), (0, 0)))
    vw_pad = jnp.pad(vw, ((0, 0), (0, 0), (WINDOW, 0), (0, 0)))
    gates = jax.nn.sigmoid(gate_logits).reshape(B, T, C_KV_GROUPS, C_HPG, 3).transpose(0, 2, 3, 1, 4)
    b_idx = jnp.arange(B)[:, None, None, None]
    g_idx = jnp.arange(C_KV_GROUPS)[None, :, None, None]
    blk = jnp.arange(n_slc)
    cmp_end_j = jnp.asarray(cmp_end)

    def block(i):
        s0 = i * C_QBLOCK
        t = s0 + jnp.arange(C_QBLOCK)
        qi = lax.dynamic_slice_in_dim(q, s0, C_QBLOCK, axis=3)
        mc = cmp_end_j[None, :] <= t[:, None]
        sc = jnp.einsum('bghqd,bgnd->bghqn', qi, k_cmp).astype(jnp.float32)
        pc = jnp.where(mc, jax.nn.softmax(jnp.where(mc, sc, NEG_INF), axis=-1), 0.0)
        o_cmp = jnp.einsum('bghqn,bgne->bghqe', pc.astype(dt), v_cmp)
        imp = jnp.einsum('bghqn,nj->bgqj', pc, sel_map)
        cur = t // SLC_BLOCK
        forced = (blk[None, :] == 0) | (blk[None, :] == cur[:, None]) | (blk[None, :] == cur[:, None] - 1)
        valid = blk[None, :] * SLC_BLOCK <= t[:, None]
        score = jnp.where(valid, imp + jnp.where(forced, FORCE_BONUS, 0.0), NEG_INF)
        _, sel = lax.top_k(score, n_sel)
        kg = ks_blocks[b_idx, g_idx, sel]
        vg = vs_blocks[b_idx, g_idx, sel]
        pos = sel[..., None] * SLC_BLOCK + jnp.arange(SLC_BLOCK)
        ms = (pos <= t[:, None, None])[:, :, None]
        ss = jnp.where(ms, jnp.einsum('bghqd,bgqnkd->bghqnk', qi, kg).astype(jnp.float32), NEG_INF)
        ps = jax.nn.softmax(ss.reshape(B, C_KV_GROUPS, C_HPG, C_QBLOCK, -1), axis=-1).reshape(ss.shape)
        o_slc = jnp.einsum('bghqnk,bgqnke->bghqe', ps.astype(dt), vg)
        kwi = lax.dynamic_slice_in_dim(kw_pad, s0, C_QBLOCK + WINDOW, axis=2)
        vwi = lax.dynamic_slice_in_dim(vw_pad, s0, C_QBLOCK + WINDOW, axis=2)
        wpos = s0 - WINDOW + jnp.arange(C_QBLOCK + WINDOW)
        mw = (wpos[None, :] <= t[:, None]) & (wpos[None, :] > t[:, None] - WINDOW) & (wpos[None, :] >= 0)
        sw = jnp.einsum('bghqd,bgkd->bghqk', qi, kwi).astype(jnp.float32)
        pw = jax.nn.softmax(jnp.where(mw, sw, NEG_INF), axis=-1)
        o_win = jnp.einsum('bghqk,bgke->bghqe', pw.astype(dt), vwi)
        gi = lax.dynamic_slice_in_dim(gates, s0, C_QBLOCK, axis=3)
        return gi[..., 0:1] * o_cmp + gi[..., 1:2] * o_slc + gi[..., 2:3] * o_win

    o = lax.map(block, jnp.arange(T // C_QBLOCK))
    return o.transpose(1, 0, 4, 2, 3, 5).reshape(B, T, C_OUT)


def causal_conv(x, w, b):
    K, T = w.shape[0], x.shape[1]
    xp = jnp.pad(x, ((0, 0), (K - 1, 0), (0, 0)))
    return sum(xp[:, j:j + T] * w[j] for j in range(K)) + b


def mlstm(q, k, v, ig, fg, og, conv_w, conv_b, ig_b, fg_b, norm_w):
    B, T, _ = q.shape
    qk = jax.nn.silu(causal_conv(jnp.concatenate([q, k], axis=-1), conv_w, conv_b))
    q, k = qk[..., :D_HEADS * D_QK], qk[..., D_HEADS * D_QK:]
    nc = T // D_CHUNK

    def chunks(z, dim):
        return z.astype(jnp.float32).reshape(B, nc, D_CHUNK, D_HEADS, dim).transpose(1, 0, 3, 2, 4)

    def gate_chunks(z):
        return z.reshape(B, nc, D_CHUNK, D_HEADS).transpose(1, 0, 3, 2)

    qc = chunks(q, D_QK) * (D_QK ** -0.5)
    kc = chunks(k, D_QK)
    vc = chunks(v, D_V)
    li = gate_chunks((ig + ig_b).astype(jnp.float32))
    lf = gate_chunks(jax.nn.log_sigmoid((fg + fg_b).astype(jnp.float32)))
    causal = jnp.tril(jnp.ones((D_CHUNK, D_CHUNK), bool))

    def step(carry, inp):
        C, n, m = carry
        qs, ks, vs, lis, lfs = inp
        b = jnp.cumsum(lfs, axis=-1)
        dmat = jnp.where(causal, b[..., :, None] - b[..., None, :] + lis[..., None, :], -jnp.inf)
        m_s = jnp.maximum(b + m[..., None], jnp.max(dmat, axis=-1))
        carry_w = jnp.exp(b + m[..., None] - m_s)
        sqk = jnp.einsum('bhsd,bhrd->bhsr', qs, ks) * jnp.exp(dmat - m_s[..., None])
        num = carry_w[..., None] * jnp.einsum('bhsd,bhde->bhse', qs, C) + jnp.einsum('bhsr,bhre->bhse', sqk, vs)
        den = carry_w * jnp.einsum('bhsd,bhd->bhs', qs, n) + jnp.sum(sqk, axis=-1)
        h = num / jnp.maximum(jnp.abs(den), jnp.exp(-m_s))[..., None]
        b_last = b[..., -1]
        g = b_last[..., None] - b + lis
        m_new = jnp.maximum(b_last + m, jnp.max(g, axis=-1))
        decay_c = jnp.exp(b_last + m - m_new)
        wr = jnp.exp(g - m_new[..., None])
        C = decay_c[..., None, None] * C + jnp.einsum('bhr,bhrd,bhre->bhde', wr, ks, vs)
        n = decay_c[..., None] * n + jnp.einsum('bhr,bhrd->bhd', wr, ks)
        return (C, n, m_new), h

    init = (jnp.zeros((B, D_HEADS, D_QK, D_V), jnp.float32),
            jnp.zeros((B, D_HEADS, D_QK), jnp.float32),
            jnp.zeros((B, D_HEADS), jnp.float32))
    _, h = lax.scan(step, init, (qc, kc, vc, li, lf))
    h = h.transpose(1, 0, 3, 2, 4).reshape(B, T, D_HEADS, D_V)
    h = rmsnorm(h, norm_w.reshape(D_HEADS, D_V)).reshape(B, T, D_OUT)
    return h.astype(og.dtype) * jax.nn.sigmoid(og)


def setup_inputs(seed: int = 0) -> dict:
    key = jax.random.key(seed)
    keys = iter(jax.random.split(key, 48))

    def nrm(shape, scale):
        return jax.random.normal(next(keys), shape, jnp.float32) * scale

    def gain(shape):
        return 1.0 + nrm(shape, 0.02)

    L, E, O = DEPTH, (DEPTH + 1) // 2, DEPTH // 2
    d, f = D_MODEL, D_FF
    return {
        'x': nrm((BATCH, SEQ, d), 1.0),
        'ffa_norm': gain((L, d)),
        'ffa_gate': nrm((L, d, f), d ** -0.5),
        'ffa_up': nrm((L, d, f), d ** -0.5),
        'ffa_down': nrm((L, f, d), f ** -0.5),
        'mix_norm': gain((L, d)),
        'ffb_norm': gain((L, d)),
        'ffb_gate': nrm((L, d, f), d ** -0.5),
        'ffb_up': nrm((L, d, f), d ** -0.5),
        'ffb_down': nrm((L, f, d), f ** -0.5),
        'ab_w_in': nrm((E, d, AB_COLS), d ** -0.5),
        'ab_w_out': nrm((E, A_OUT + B_DIM, d), (A_OUT + B_DIM) ** -0.5),
        'diff_lam': nrm((E, 4, HEAD_DIM), 0.1),
        'diff_subln': gain((E, A_VDIM)),
        'rwkv_mu': jax.random.uniform(next(keys), (E, B_COLS), jnp.float32),
        'rwkv_w0': jnp.linspace(-6.0, 1.0, B_DIM, dtype=jnp.float32)[None, :] + nrm((E, B_DIM), 0.1),
        'rwkv_w2': nrm((E, B_W_RANK, B_DIM), B_W_RANK ** -0.5),
        'rwkv_a0': nrm((E, B_DIM), 0.1),
        'rwkv_a2': nrm((E, B_A_RANK, B_DIM), B_A_RANK ** -0.5),
        'rwkv_g2': nrm((E, B_G_RANK, B_DIM), B_G_RANK ** -0.5),
        'rwkv_kk': 0.85 + nrm((E, B_DIM), 0.05),
        'rwkv_ka': 1.0 + nrm((E, B_DIM), 0.05),
        'rwkv_rk': nrm((E, B_HEADS, HEAD_DIM), 0.1),
        'rwkv_lnw': gain((E, B_DIM)),
        'rwkv_lnb': nrm((E, B_DIM), 0.02),
        'cd_w_in': nrm((O, d, CD_COLS), d ** -0.5),
        'cd_w_out': nrm((O, C_OUT + D_OUT, d), (C_OUT + D_OUT) ** -0.5),
        'nsa_pe_k': nrm((O, CMP_BLOCK, HEAD_DIM), 0.1),
        'nsa_w1_k': nrm((O, CMP_BLOCK * HEAD_DIM, CMP_HIDDEN), (CMP_BLOCK * HEAD_DIM) ** -0.5),
        'nsa_w2_k': nrm((O, CMP_HIDDEN, HEAD_DIM), CMP_HIDDEN ** -0.5),
        'nsa_pe_v': nrm((O, CMP_BLOCK, HEAD_DIM), 0.1),
        'nsa_w1_v': nrm((O, CMP_BLOCK * HEAD_DIM, CMP_HIDDEN), (CMP_BLOCK * HEAD_DIM) ** -0.5),
        'nsa_w2_v': nrm((O, CMP_HIDDEN, HEAD_DIM), CMP_HIDDEN ** -0.5),
        'mlstm_conv_w': nrm((O, D_CONV, 2 * D_HEADS * D_QK), 0.5),
        'mlstm_conv_b': nrm((O, 2 * D_HEADS * D_QK), 0.02),
        'mlstm_ig_b': -2.0 + nrm((O, D_HEADS), 0.1),
        'mlstm_fg_b': jnp.linspace(3.0, 6.0, D_HEADS, dtype=jnp.float32)[None, :] + nrm((O, D_HEADS), 0.1),
        'mlstm_norm': gain((O, D_OUT)),
        'final_norm': gain((d,)),
    }


def reference(x, ffa_norm, ffa_gate, ffa_up, ffa_down, mix_norm, ffb_norm, ffb_gate, ffb_up, ffb_down,
              ab_w_in, ab_w_out, diff_lam, diff_subln,
              rwkv_mu, rwkv_w0, rwkv_w2, rwkv_a0, rwkv_a2, rwkv_g2, rwkv_kk, rwkv_ka, rwkv_rk, rwkv_lnw, rwkv_lnb,
              cd_w_in, cd_w_out, nsa_pe_k, nsa_w1_k, nsa_w2_k, nsa_pe_v, nsa_w1_v, nsa_w2_v,
              mlstm_conv_w, mlstm_conv_b, mlstm_ig_b, mlstm_fg_b, mlstm_norm, final_norm):
    T = x.shape[1]
    cos, sin = rope_tables(jnp.arange(T, dtype=jnp.float32))
    for layer in range(DEPTH):
        j = layer // 2
        x = x + 0.5 * swiglu(rmsnorm(x, ffa_norm[layer]), ffa_gate[layer], ffa_up[layer], ffa_down[layer])
        h = rmsnorm(x, mix_norm[layer])
        if layer % 2 == 0:
            p = h @ ab_w_in[j]
            qa, ka, va = split_cols(p[..., :A_COLS], A_SIZES)
            oa = diff_attention(qa, ka, va, diff_lam[j], diff_subln[j], lambda_init(layer), cos, sin)
            ob = rwkv7_time_mix(p[..., A_COLS:], rwkv_mu[j], rwkv_w0[j], rwkv_w2[j], rwkv_a0[j], rwkv_a2[j],
                                rwkv_g2[j], rwkv_kk[j], rwkv_ka[j], rwkv_rk[j], rwkv_lnw[j], rwkv_lnb[j])
            mix = jnp.concatenate([oa, ob], axis=-1) @ ab_w_out[j]
        else:
            p = h @ cd_w_in[j]
            cq, ckc, cvc, cks, cvs, ckw, cvw, cg = split_cols(p[..., :C_COLS], C_SIZES)
            dq, dk, dv, di, df, dog = split_cols(p[..., C_COLS:], D_SIZES)
            oc = native_sparse_attention(cq, ckc, cvc, cks, cvs, ckw, cvw, cg,
                                         nsa_pe_k[j], nsa_w1_k[j], nsa_w2_k[j],
                                         nsa_pe_v[j], nsa_w1_v[j], nsa_w2_v[j], cos, sin)
            od = mlstm(dq, dk, dv, di, df, dog, mlstm_conv_w[j], mlstm_conv_b[j],
                       mlstm_ig_b[j], mlstm_fg_b[j], mlstm_norm[j])
            mix = jnp.concatenate([oc, od], axis=-1) @ cd_w_out[j]
        x = x + mix
        x = x + 0.5 * swiglu(rmsnorm(x, ffb_norm[layer]), ffb_gate[layer], ffb_up[layer], ffb_down[layer])
    return rmsnorm(x, final_norm)
```

```python
from contextlib import ExitStack
import numpy as np
import ml_dtypes
import concourse.bass as bass
import concourse.mybir as mybir
from concourse.bass_utils import run_bass_kernel_spmd

F32 = mybir.dt.float32
BF16 = mybir.dt.bfloat16
ALU = mybir.AluOpType
AF = mybir.ActivationFunctionType
AX = mybir.AxisListType

T = 4096
D = 1024
DFF = 2816
NFC = DFF // 128
EPS = 1e-6


class Buf:
    __slots__ = ("t", "w", "r", "name")

    def __init__(self, t, name=""):
        self.t = t
        self.w = None
        self.r = {}
        self.name = name

    def __getitem__(self, k):
        return self.t[k]


class Ctx:
    COMPUTE = ("pe", "act", "dve", "pool")
    NDQ = 6

    def __init__(self, nc, es):
        self.nc = nc
        self.es = es
        self.engs = {"pe": nc.tensor, "act": nc.scalar, "dve": nc.vector,
                     "pool": nc.gpsimd, "sp": nc.sync}
        self.sem = {}
        self.cnt = {}
        for e in self.COMPUTE:
            self.sem[e] = es.enter_context(nc.semaphore("s_" + e))
            self.cnt[e] = 0
        self.dq = {}
        self.dq_cnt = {}
        self.dq_i = {}
        for q in ("sp", "pool", "act"):
            self.dq[q] = [es.enter_context(nc.semaphore(f"d_{q}{i}")) for i in range(self.NDQ)]
            self.dq_cnt[q] = [0] * self.NDQ
            self.dq_i[q] = 0
        self.known = {e: {} for e in self.engs}
        self.nins = 0

    def _nm(self, name):
        self.uid = getattr(self, "uid", 0) + 1
        return f"{name}_{self.uid}"

    def sb(self, st, name, shape, dt):
        name = self._nm("sb_" + name)
        return Buf(st.enter_context(self.nc.sbuf_tensor(name, list(shape), dt)), name)

    def ps(self, st, name, shape, dt=F32):
        name = self._nm("ps_" + name)
        return Buf(st.enter_context(self.nc.psum_tensor(name, list(shape), dt)), name)

    def dram(self, name, shape, dt, kind=None):
        if kind is None:
            t = self.nc.dram_tensor(name, list(shape), dt)
        else:
            t = self.nc.dram_tensor(name, list(shape), dt, kind=kind)
        return Buf(t.ap(), name)

    def _need(self, r, w):
        need = {}

        def add(tok):
            if tok is None:
                return
            s, v = tok
            cur = need.get(s.num)
            if cur is None or cur[1] < v:
                need[s.num] = (s, v)
        for b in r:
            add(b.w)
        for b in w:
            add(b.w)
            for tk in b.r.values():
                add(tk)
        return need

    def _emit_waits(self, e, need):
        eng = self.engs[e]
        kn = self.known[e]
        for num, (s, v) in need.items():
            if e == "pe" and s is self.sem["pe"]:
                continue
            if kn.get(num, 0) >= v:
                continue
            eng.wait_ge(s, v)
            kn[num] = v
            self.nins += 1

    def _record(self, tok, r, w):
        for b in r:
            b.r[tok[0].num] = tok
        for b in w:
            b.w = tok
            b.r = {}

    def op(self, e, fn, r=(), w=(), rows=(0, 128)):
        need = self._need(r, w)
        self._emit_waits(e, need)
        if e == "pe":
            lw = getattr(self, "_pe_lw", None)
            if lw is None:
                lw = self._pe_lw = {}
            wnums = [id(b) for b in w]
            if rows[1] >= 128:
                lw.clear()
            else:
                conflict = any(rg != rows and (rows[0] >= rg[0] + rg[1] or rg[0] >= rows[0] + rows[1])
                               and any(x in bs for x in wnums) for rg, bs in lw.items())
                if conflict and self.cnt["pe"] > 0:
                    if self.known["pe"].get(self.sem["pe"].num, 0) < self.cnt["pe"]:
                        self.engs["pe"].wait_ge(self.sem["pe"], self.cnt["pe"])
                        self.known["pe"][self.sem["pe"].num] = self.cnt["pe"]
                        self.nins += 1
                    lw.clear()
                lw.setdefault(rows, set()).update(wnums)
        ins = fn(self.engs[e])
        self.cnt[e] += 1
        tok = (self.sem[e], self.cnt[e])
        ins.then_inc(self.sem[e], 1)
        self.nins += 1
        self._record(tok, r, w)
        return tok

    def dma(self, q, out_ap, in_ap, r=(), w=(), **kw):
        ring = self.dq[q]
        i = self.dq_i[q]
        self.dq_i[q] = (i + 1) % len(ring)
        s = ring[i]
        prev = self.dq_cnt[q][i]
        need = self._need(r, w)
        if prev > 0:
            need[s.num] = (s, prev)
        self._emit_waits(q, need)
        ins = self.engs[q].dma_start(out=out_ap, in_=in_ap, **kw)
        ins.then_inc(s, 16)
        self.nins += 1
        self.dq_cnt[q][i] = prev + 16
        tok = (s, prev + 16)
        self._record(tok, r, w)
        return tok

    def barrier(self):
        toks = []
        for e in self.COMPUTE:
            if self.cnt[e] > 0:
                toks.append((self.sem[e], self.cnt[e]))
        for q in self.dq:
            for i, s in enumerate(self.dq[q]):
                if self.dq_cnt[q][i] > 0:
                    toks.append((s, self.dq_cnt[q][i]))
        for e in self.engs:
            need = {s.num: (s, v) for s, v in toks}
            kn = self.known[e]
            for num, (s, v) in need.items():
                if kn.get(num, 0) >= v:
                    continue
                self.engs[e].wait_ge(s, v)
                kn[num] = v
                self.nins += 1


def load_col_vec(cx, st, name, src_ap, n):
    c = n // 128
    b = cx.sb(st, name, [128, c], F32)
    cx.dma("sp", b[:, :], src_ap.rearrange("(c p) -> p c", p=128), w=[b],
           allow_slow_non_contiguous=True)
    return b


def load_weight_bf16(cx, st, wsb, w_dram_ap, kc, ncols, stage, scale_col=None, cast_eng="pool"):
    half = stage[0].t.shape[1]
    si = 0
    for c in range(kc):
        for c0 in range(0, ncols, half):
            cw = min(half, ncols - c0)
            sg = stage[si % len(stage)]
            si += 1
            cx.dma("sp", sg[:, :cw], w_dram_ap[c * 128:(c + 1) * 128, c0:c0 + cw], w=[sg])
            if scale_col is not None:
                cx.op(cast_eng, lambda e, sg=sg, c=c, c0=c0, cw=cw: e.tensor_scalar(
                    out=wsb[:, c, c0:c0 + cw], in0=sg[:, :cw], scalar1=scale_col[:, c:c + 1],
                    scalar2=None, op0=ALU.mult), r=[sg, scale_col], w=[wsb])
            else:
                cx.op(cast_eng, lambda e, sg=sg, c=c, c0=c0, cw=cw: e.tensor_copy(
                    out=wsb[:, c, c0:c0 + cw], in_=sg[:, :cw]), r=[sg], w=[wsb])


def norm_transpose_tile(cx, xt, hT, col0, ident, junk, ss, xn, tp, nw_b):
    cx.op("act", lambda e: e.activation(out=junk[:, :], in_=xt[:, :], func=AF.Square,
                                        accum_out=ss[:, 0:1]), r=[xt], w=[junk, ss])
    cx.op("act", lambda e: e.activation(out=ss[:, 1:2], in_=ss[:, 0:1], func=AF.Sqrt,
                                        scale=1.0 / D, bias=ss[:, 3:4]), r=[ss], w=[ss])
    cx.op("dve", lambda e: e.reciprocal(out=ss[:, 2:3], in_=ss[:, 1:2]), r=[ss], w=[ss])
    cx.op("dve", lambda e: e.scalar_tensor_tensor(out=xn[:, :], in0=xt[:, :], scalar=ss[:, 2:3],
                                                  in1=nw_b[:, :], op0=ALU.mult, op1=ALU.mult),
          r=[xt, ss, nw_b], w=[xn])
    for c in range(8):
        cx.op("pe", lambda e, c=c: e.transpose(out=tp[:, c * 128:(c + 1) * 128],
                                                in_=xn[:, c * 128:(c + 1) * 128],
                                                identity=ident[:, :]), r=[xn, ident], w=[tp])
    cx.op("act", lambda e: e.copy(out=hT[:, :, col0:col0 + 128],
                                  in_=tp[:, :].rearrange("p (c t) -> p c t", c=8)),
          r=[tp], w=[hT])


class WStream:
    def __init__(self, cx, stage, jobs):
        self.cx, self.stage, self.jobs, self.n = cx, stage, list(jobs), 0

    def emit(self, k=1):
        cx = self.cx
        for _ in range(k):
            if self.n >= len(self.jobs):
                return
            n = self.n
            self.n += 1
            dst, src, buf = self.jobs[n]
            sg = self.stage[n % len(self.stage)]
            shp = list(dst.shape)
            if len(shp) == 2:
                view = sg[:, 0:shp[1]]
            else:
                view = sg[:, 0:shp[1] * shp[2]].rearrange("p (c n) -> p c n", c=shp[1])
            cx.dma("sp" if n % 2 == 0 else "pool", view, src, w=[sg])
            if n % 2 == 0:
                cx.op("dve", lambda e: e.tensor_copy(out=dst, in_=view), r=[sg], w=[buf])
            else:
                cx.op("act", lambda e: e.copy(out=dst, in_=view), r=[sg], w=[buf])

    def rest(self):
        self.emit(len(self.jobs))


def stream_weight_chunks(cx, jobs, stage):
    WStream(cx, stage, jobs).rest()


def ffn_stage(cx, x_in, x_out, nw_ap, wg_ap, wu_ap, wd_ap, ident, ntok=T):
    with ExitStack() as st:
        wg = cx.sb(st, "wg", [128, 8, DFF], BF16)
        wu = cx.sb(st, "wu", [128, 8, DFF], BF16)
        wd = cx.sb(st, "wd", [128, NFC, D], BF16)
        wgp = [Buf(wg.t, f"wg{i}") for i in range(NFC // 2)]
        wup = [Buf(wu.t, f"wu{i}") for i in range(NFC // 2)]
        wdp = [Buf(wd.t, f"wd{i}") for i in range(NFC // 2)]
        wgf = [wgp[f // 2] for f in range(NFC)]
        wuf = [wup[f // 2] for f in range(NFC)]
        wdf = [wdp[f // 2] for f in range(NFC)]
        stage = [cx.sb(st, f"wstage{i}", [128, 2048], F32) for i in range(2)]
        nw_b = bcast_load(cx, st, "nw_b", nw_ap, D)
        xts = [cx.sb(st, f"xt{i}", [128, D], F32) for i in range(2)]
        xrs = [cx.sb(st, f"xr{i}", [128, D], F32) for i in range(1)]
        xn = cx.sb(st, "xn", [128, D], BF16)
        junk = xn
        sss = [cx.sb(st, f"ss{i}", [128, 4], F32) for i in range(2)]
        hTs = [cx.sb(st, f"hT{i}", [128, 8, 512], BF16) for i in range(2)]
        actb = cx.sb(st, "actb", [128, NFC, 512], BF16)
        sg_sb = [cx.sb(st, f"sg{i}", [128, 512], BF16) for i in range(2)]
        tp = cx.ps(st, "tp", [128, 1024], BF16)
        pg = [cx.ps(st, f"pg{i}", [128, 512]) for i in range(2)]
        pu = [cx.ps(st, f"pu{i}", [128, 512]) for i in range(2)]
        py = [cx.ps(st, f"py{i}", [128, 512]) for i in range(2)]
        for s in sss:
            cx.op("dve", lambda e, s=s: e.memset(s[:, :], EPS), w=[s])
        wgv = wg_ap.rearrange("(c p) n -> p c n", p=128)
        wuv = wu_ap.rearrange("(c p) n -> p c n", p=128)
        wdv = wd_ap.rearrange("(f p) n -> p f n", p=128)
        jobs = []
        for i in range(NFC // 2):
            cs = slice(i * 256, (i + 1) * 256)
            jobs.append((wg[:, :, cs], wgv[:, :, cs], wgp[i]))
            jobs.append((wu[:, :, cs], wuv[:, :, cs], wup[i]))
            if i >= 1:
                jobs.append((wd[:, 2 * (i - 1):2 * i, :], wdv[:, 2 * (i - 1):2 * i, :], wdp[i - 1]))
        jobs.append((wd[:, NFC - 2:NFC, :], wdv[:, NFC - 2:NFC, :], wdp[NFC // 2 - 1]))
        ws = WStream(cx, stage, jobs)

        def norm_tile(b, j):
            hT_ = hTs[b % 2]
            t0_ = b * 512 + j * 128
            k_ = b * 4 + j
            xt = xts[k_ % 2]
            ss = sss[k_ % 2]
            cx.dma("sp", xt[:, :], x_in[t0_:t0_ + 128, :], r=[x_in], w=[xt])
            norm_transpose_tile(cx, xt, hT_, j * 128, ident, junk, ss, xn, tp, nw_b)

        nblk = ntok // 512
        for b in range(nblk):
            hT = hTs[b % 2]
            if b == 0:
                for j in range(4):
                    norm_tile(0, j)
                ws.emit(2)
            for f in range(NFC):
                if b == 0 and f % 2 == 0:
                    ws.emit(3)
                g = pg[f % 2]
                u = pu[f % 2]
                sg = sg_sb[f % 2]
                for c in range(8):
                    cx.op("pe", lambda e, c=c, f=f, g=g: e.matmul(
                        g[:, :], lhsT=wg[:, c, f * 128:(f + 1) * 128], rhs=hT[:, c, :],
                        start=(c == 0), stop=(c == 7)), r=[wgf[f], hT], w=[g])
                for c in range(8):
                    cx.op("pe", lambda e, c=c, f=f, u=u: e.matmul(
                        u[:, :], lhsT=wu[:, c, f * 128:(f + 1) * 128], rhs=hT[:, c, :],
                        start=(c == 0), stop=(c == 7)), r=[wuf[f], hT], w=[u])
                cx.op("act", lambda e, g=g, sg=sg: e.activation(out=sg[:, :], in_=g[:, :],
                                                                 func=AF.Silu), r=[g], w=[sg])
                cx.op("dve", lambda e, f=f, u=u, sg=sg: e.tensor_tensor(
                    out=actb[:, f, :], in0=u[:, :], in1=sg[:, :], op=ALU.mult),
                    r=[u, sg], w=[actb])
                if b + 1 < nblk and f in (3, 8, 13, 18):
                    norm_tile(b + 1, (3, 8, 13, 18).index(f))
            if b == 0:
                ws.rest()
            for j in range(4):
                t0 = b * 512 + j * 128
                xr = xrs[0]
                cx.dma("sp", xr[:, :], x_in[t0:t0 + 128, :], r=[x_in], w=[xr])
                for h in range(2):
                    y = py[h]
                    for f in range(NFC):
                        cx.op("pe", lambda e, f=f, h=h, y=y, j=j: e.matmul(
                            y[:, :], lhsT=actb[:, f, j * 128:(j + 1) * 128],
                            rhs=wd[:, f, h * 512:(h + 1) * 512],
                            start=(f == 0), stop=(f == NFC - 1)), r=[actb, wdf[f]], w=[y])
                    cx.op("dve", lambda e, h=h, y=y, xr=xr: e.scalar_tensor_tensor(
                        out=xr[:, h * 512:(h + 1) * 512], in0=y[:, :], scalar=0.5,
                        in1=xr[:, h * 512:(h + 1) * 512], op0=ALU.mult, op1=ALU.add),
                        r=[y, xr], w=[xr])
                cx.dma("pool", x_out[t0:t0 + 128, :], xr[:, :], r=[xr], w=[x_out])
    cx.barrier()


def make_ident(cx, st):
    identf = cx.sb(st, "identf", [128, 128], F32)
    ident = cx.sb(st, "ident", [128, 128], BF16)
    cx.op("pool", lambda e: e.memset(identf[:, :], 1.0), w=[identf])
    cx.op("pool", lambda e: e.affine_select(out=identf[:, :], in_=identf[:, :],
                                            pattern=[[-1, 128]], compare_op=ALU.is_equal,
                                            fill=0.0, base=0, channel_multiplier=1),
          r=[identf], w=[identf])
    cx.op("dve", lambda e: e.tensor_copy(out=ident[:, :], in_=identf[:, :]), r=[identf], w=[ident])
    return ident, identf


def proj_stage(cx, x_in, nw_ap, wsrcs, specs, ident, dst_buf, consts, ntok=T):
    ncols = sum(n for _, n in wsrcs)
    with ExitStack() as st:
        wsb = cx.sb(st, "wp", [128, 8, ncols], BF16)
        stage = [cx.sb(st, f"wstage{i}", [128, 1024], F32) for i in range(4)]
        nw_b = bcast_load(cx, st, "nw_b", nw_ap, D)
        pieces = []
        c0 = 0
        for ap, n in wsrcs:
            apv = ap.rearrange("(c p) n -> p c n", p=128)
            for lo in range(0, n, 128):
                hi = min(n, lo + 128)
                pieces.append((c0 + lo, c0 + hi, Buf(wsb.t, f"wp{c0 + lo}"), apv[:, :, lo:hi]))
            c0 += n

        def wb(a, n):
            return [p[2] for p in pieces if p[0] < a + n and p[1] > a]
        order = []
        for sp in specs:
            rng = [(sp["a"], sp.get("n", 128))]
            if sp["kind"] == "rope":
                rng.append((sp["b"], 128))
            for (a, n) in rng:
                for p in pieces:
                    if p[0] < a + n and p[1] > a and p not in order:
                        order.append(p)
        for p in pieces:
            if p not in order:
                order.append(p)
        jobs = [(wsb[:, :, p[0]:p[1]], p[3], p[2]) for p in order]
        ws = WStream(cx, stage, jobs)

        def need(sp):
            rng = [(sp["a"], sp.get("n", 128))]
            if sp["kind"] == "rope":
                rng.append((sp["b"], 128))
            idx = 0
            for (a_, n_) in rng:
                for k_, p in enumerate(order):
                    if p[0] < a_ + n_ and p[1] > a_:
                        idx = max(idx, k_ + 1)
            return idx
        needs = [need(sp) for sp in specs]
        xts = [cx.sb(st, f"xt{i}", [128, D], F32) for i in range(2)]
        junk = cx.sb(st, "junk", [128, D], BF16)
        xn = cx.sb(st, "xn", [128, D], BF16)
        sss = [cx.sb(st, f"ss{i}", [128, 4], F32) for i in range(2)]
        hTs = [cx.sb(st, f"hT{i}", [128, 8, 512], BF16) for i in range(2)]
        cosb = [cx.sb(st, f"cosb{i}", [128, 512], F32) for i in range(2)]
        sinb = [cx.sb(st, f"sinb{i}", [128, 512], F32) for i in range(2)]
        t1 = [cx.sb(st, f"t1_{i}", [128, 512], F32) for i in range(2)]
        t2 = [cx.sb(st, f"t2_{i}", [128, 512], F32) for i in range(2)]
        ob = [cx.sb(st, f"ob{i}", [128, 512], BF16) for i in range(3)]
        of = [cx.sb(st, f"of{i}", [128, 512], F32) for i in range(3)]
        tp = cx.ps(st, "tp", [128, 1024], BF16)
        pa = [cx.ps(st, f"pa{i}", [128, 512]) for i in range(2)]
        pb = [cx.ps(st, f"pb{i}", [128, 512]) for i in range(2)]
        pc = [cx.ps(st, f"pc{i}", [128, 512]) for i in range(2)]
        for s in sss:
            cx.op("dve", lambda e, s=s: e.memset(s[:, :], EPS), w=[s])
        nblk = ntok // 512
        it = 0
        k_ob = 0
        k_of = 0
        k_r = 0
        k_c = 0
        for b in range(nblk):
            hT = hTs[b % 2]
            tb = b * 512
            for j in range(4):
                t0 = tb + j * 128
                xt = xts[it % 2]
                ss = sss[it % 2]
                it += 1
                cx.dma("sp", xt[:, :], x_in[t0:t0 + 128, :], r=[x_in], w=[xt])
                norm_transpose_tile(cx, xt, hT, j * 128, ident, junk, ss, xn, tp, nw_b)
            cb = cosb[b % 2]
            sb_ = sinb[b % 2]
            if any(s["kind"] == "rope" for s in specs):
                cx.dma("sp", cb[:, :], consts["cos"][:, tb:tb + 512], r=[consts["cos"]], w=[cb])
                cx.dma("sp", sb_[:, :], consts["sin"][:, tb:tb + 512], r=[consts["sin"]], w=[sb_])
            for si, sp in enumerate(specs):
                kind = sp["kind"]
                if b == 0:
                    tgt = needs[min(si + 2, len(specs) - 1)] if si + 2 < len(specs) else len(jobs)
                    tgt = max(tgt, needs[si])
                    ws.emit(max(0, tgt - ws.n))
                if kind == "rope":
                    A = pa[k_r % 2]
                    B = pb[k_r % 2]
                    u1 = t1[k_r % 2]
                    u2 = t2[k_r % 2]
                    k_r += 1
                    for c in range(8):
                        cx.op("pe", lambda e, c=c, A=A, a=sp["a"]: e.matmul(
                            A[:, :], lhsT=wsb[:, c, a:a + 128], rhs=hT[:, c, :],
                            start=(c == 0), stop=(c == 7)), r=wb(sp["a"], 128) + [hT], w=[A])
                    for c in range(8):
                        cx.op("pe", lambda e, c=c, B=B, a=sp["b"]: e.matmul(
                            B[:, :], lhsT=wsb[:, c, a:a + 128], rhs=hT[:, c, :],
                            start=(c == 0), stop=(c == 7)), r=wb(sp["b"], 128) + [hT], w=[B])
                    cx.op("dve", lambda e, A=A, u1=u1: e.tensor_tensor(
                        out=u1[:, :], in0=A[:, :], in1=cb[:, :], op=ALU.mult), r=[A, cb], w=[u1])
                    cx.op("dve", lambda e, B=B, u2=u2: e.tensor_tensor(
                        out=u2[:, :], in0=B[:, :], in1=sb_[:, :], op=ALU.mult), r=[B, sb_], w=[u2])
                    o = ob[k_ob % 3]
                    k_ob += 1
                    cx.op("pool", lambda e, o=o, u1=u1, u2=u2: e.tensor_tensor(
                        out=o[:, :], in0=u1[:, :], in1=u2[:, :], op=ALU.add), r=[u1, u2], w=[o])
                    cx.dma("pool", sp["dst"][:, tb:tb + 512], o[:, :], r=[o], w=dst_buf)
                elif kind == "raw":
                    n = sp["n"]
                    C = pc[k_c % 2]
                    k_c += 1
                    for c in range(8):
                        cx.op("pe", lambda e, c=c, C=C, a=sp["a"], n=n: e.matmul(
                            C[:n, :], lhsT=wsb[:, c, a:a + n], rhs=hT[:, c, :],
                            start=(c == 0), stop=(c == 7)), r=wb(sp["a"], n) + [hT], w=[C])
                    o = of[k_of % 3]
                    k_of += 1
                    cx.op("act", lambda e, o=o, C=C, n=n: e.copy(out=o[:n, :], in_=C[:n, :]),
                          r=[C], w=[o])
                    cx.dma("pool", sp["dst"][:, tb:tb + 512], o[:n, :], r=[o], w=dst_buf)
                elif kind == "tm":
                    n = sp["n"]
                    isbf = sp["dst"].dtype == BF16
                    for j in range(4):
                        C = pc[k_c % 2]
                        k_c += 1
                        for c in range(8):
                            cx.op("pe", lambda e, c=c, C=C, a=sp["a"], n=n, j=j: e.matmul(
                                C[:, :n], lhsT=hT[:, c, j * 128:(j + 1) * 128],
                                rhs=wsb[:, c, a:a + n],
                                start=(c == 0), stop=(c == 7)), r=wb(sp["a"], n) + [hT], w=[C])
                        if isbf:
                            o = ob[k_ob % 3]
                            k_ob += 1
                        else:
                            o = of[k_of % 3]
                            k_of += 1
                        fn = AF.Sigmoid if sp.get("act") == "sigmoid" else AF.Copy
                        cx.op("act", lambda e, o=o, C=C, n=n, fn=fn: e.activation(
                            out=o[:, :n], in_=C[:, :n], func=fn), r=[C], w=[o])
                        cx.dma("pool", sp["dst"][tb + j * 128:tb + (j + 1) * 128, :], o[:, :n],
                               r=[o], w=dst_buf)
    cx.barrier()


class _ColView:
    def __init__(self, buf, c0):
        self.buf = buf
        self.c0 = c0
        self.t = buf.t

    def __getitem__(self, k):
        p, c, cols = k
        return self.buf.t[p, c, slice(cols.start + self.c0, cols.stop + self.c0)]

    @property
    def w(self):
        return self.buf.w

    @w.setter
    def w(self, v):
        self.buf.w = v

    @property
    def r(self):
        return self.buf.r

    @r.setter
    def r(self, v):
        self.buf.r = v


def attn_run(cx, st, units, E1, mode, scale=1.0, bufs=None):
    sS, Os, Pb = bufs["sS"], bufs["Os"], bufs["Pb"]
    npb = len(Pb)
    flat = []
    for ui, u in enumerate(units):
        nt = len(u["tiles"])
        first = [None] * 4
        last = [None] * 4
        for ti, t in enumerate(u["tiles"]):
            for s in range(4):
                if t["subs"][s]:
                    if first[s] is None:
                        first[s] = ti
                    last[s] = ti
        for ti, t in enumerate(u["tiles"]):
            flat.append((ui, ti, first, last))

    def emit_S(n):
        ui, ti, _, _ = flat[n]
        u = units[ui]
        t = u["tiles"][ti]
        S = sS[n % len(sS)]
        add = t.get("add")
        rows = u.get("rows", (0, 128))
        cx.op("pe", lambda e: e.matmul(S[:, :], lhsT=t["k"], rhs=u["q"], start=True,
                                       stop=(add is None)), r=list(u["qr"]) + list(t["r"]), w=[S],
              rows=rows)
        if add is not None:
            cx.op("pe", lambda e: e.matmul(S[:, :], lhsT=add[0], rhs=add[1], start=False,
                                           stop=True), r=list(add[2]), w=[S], rows=rows)

    def emit_rest(n):
        ui, ti, first, last = flat[n]
        u = units[ui]
        t = u["tiles"][ti]
        S = sS[n % len(sS)]
        P = Pb[n % npb]
        O = Os[ui % 2]
        if mode == "exp":
            cx.op("act", lambda e: e.activation(out=P[:, :], in_=S[:, :], func=AF.Exp,
                                                scale=scale), r=[S], w=[P])
        else:
            rf = t["rowfac"]
            cx.op("act", lambda e: e.activation(out=P[:, :], in_=S[:, :], func=AF.Copy,
                                                scale=rf[0]), r=[S, rf[1]], w=[P])
        if t.get("mask") is not None:
            m = t["mask"]
            cx.op("dve", lambda e: e.tensor_tensor(out=P[:, :], in0=P[:, :], in1=m[0],
                                                   op=ALU.mult), r=[P, m[1]], w=[P])
        for s in range(4):
            if not t["subs"][s]:
                continue
            bk = (ui, s // 2)
            st_flag = bk not in started
            started.add(bk)
            cx.op("pe", lambda e, s=s: e.matmul(
                O[s // 2][:, s % 2, :E1], lhsT=P[:, s * 128:(s + 1) * 128], rhs=t["v"],
                start=st_flag, stop=(ti == last[s]), skip_group_check=True),
                r=[P] + list(t["r"]), w=[O[s // 2]])
        if ti == len(u["tiles"]) - 1:
            u["fin"](O[0], O[1])

    n = len(flat)
    if n == 0:
        return
    started = set()
    depth = len(sS) - 1
    for k in range(min(depth, n)):
        emit_S(k)
    for i in range(n):
        if i + depth < n:
            emit_S(i + depth)
        emit_rest(i)


def causal_tiles(j):
    out = []
    for i in range(4 * j + 4):
        dlt = i - 4 * j
        if dlt < 0:
            out.append((i, None, [True] * 4))
        else:
            out.append((i, dlt, [s >= dlt for s in range(4)]))
    return out


def bcast_load(cx, st, name, src_ap, n, dt=F32, parts=128):
    b = cx.sb(st, name, [parts, n], dt)
    cx.dma("sp", b[:, :], src_ap.partition_broadcast(parts), w=[b])
    return b


def rstd_from_ss(cx, ss_ap, out_ap, bufs, n, eps_ap):
    cx.op("act", lambda e: e.activation(out=out_ap, in_=ss_ap, func=AF.Sqrt, scale=1.0 / n,
                                        bias=eps_ap), r=bufs, w=bufs)
    cx.op("dve", lambda e: e.reciprocal(out=out_ap, in_=out_ap), r=bufs, w=bufs)


def diffattn_stage(cx, qTd, kTd, Vd, lam_ap, subln_ap, oTd, ident, consts, lam_init, ntok=T):
    nb = ntok // 512
    nkt = ntok // 128
    with ExitStack() as st:
        kT = [cx.sb(st, f"kT{i}", [128, ntok], BF16) for i in range(2)]
        qT = [cx.sb(st, f"qT{i}", [128, ntok], BF16) for i in range(2)]
        Vs = [cx.sb(st, f"V{i}", [128, nkt, 129], BF16) for i in range(2)]
        masks = cx.sb(st, "masks", [128, 4, 512], BF16)
        cx.dma("sp", masks[:, :, :], consts["cmask"].t, r=[consts["cmask"]], w=[masks])
        for v in Vs:
            cx.op("dve", lambda e, v=v: e.memset(v[:, :, 128:129], 1.0), w=[v])
        lam = bcast_load(cx, st, "lam", lam_ap, 256)
        sub = bcast_load(cx, st, "subln", subln_ap, 128)
        cx.op("dve", lambda e: e.tensor_scalar(out=sub[:, :], in0=sub[:, :], scalar1=1.0 - lam_init,
                                               scalar2=None, op0=ALU.mult), r=[sub], w=[sub])
        sm = cx.sb(st, "sm", [128, 8], F32)
        lprod = cx.sb(st, "lprod", [128, 2, 64], F32)
        lv = lam[:, :].rearrange("p (a b d) -> p a b d", a=2, b=2)
        cx.op("dve", lambda e: e.tensor_tensor(out=lprod[:, :, :], in0=lv[:, :, 0, :],
                                               in1=lv[:, :, 1, :], op=ALU.mult), r=[lam], w=[lprod])
        cx.op("dve", lambda e: e.reduce_sum(out=sm[:, 0:2], in_=lprod[:, :, :], axis=AX.X),
              r=[lprod], w=[sm])
        cx.op("act", lambda e: e.activation(out=sm[:, 2:4], in_=sm[:, 0:2], func=AF.Exp),
              r=[sm], w=[sm])
        cx.op("dve", lambda e: e.tensor_tensor(out=sm[:, 4:5], in0=sm[:, 3:4], in1=sm[:, 2:3],
                                               op=ALU.subtract), r=[sm], w=[sm])
        cx.op("dve", lambda e: e.tensor_scalar(out=sm[:, 5:6], in0=sm[:, 4:5], scalar1=-lam_init,
                                               scalar2=None, op0=ALU.add), r=[sm], w=[sm])
        cx.op("dve", lambda e: e.memset(sm[:, 6:7], EPS), r=[], w=[sm])
        oc0 = [cx.sb(st, f"oc0_{i}", [128, 4, 128], F32) for i in range(2)]
        dd4 = cx.sb(st, "dd4", [128, 4, 128], F32)
        sq4 = cx.sb(st, "sq4", [128, 4, 128], F32)
        yb4 = cx.sb(st, "yb4", [128, 4, 128], BF16)
        fs = cx.sb(st, "fs", [128, 12], F32)
        mhalf = cx.sb(st, "mhalf", [128, 4], F32)
        cx.op("dve", lambda e: e.memset(mhalf[:, :], -0.5), w=[mhalf])
        oTs = [cx.sb(st, f"oTs{i}", [128, 512], BF16) for i in range(2)]
        tpf = cx.ps(st, "tpf", [128, 512], BF16)

        for h in range(4):
            k_sb = kT[h % 2]
            q_sb = qT[h % 2]
            v_sb = Vs[h % 2]
            cx.dma("sp", k_sb[:, :], kTd[h, :, :ntok], r=[kTd], w=[k_sb])
            cx.dma("sp", q_sb[:, :], qTd[h, :, :ntok], r=[qTd], w=[q_sb])
            cx.dma("sp", v_sb[:, :, 0:128],
                   Vd[:ntok, h * 128:(h + 1) * 128].rearrange("(i p) e -> p i e", p=128),
                   r=[Vd], w=[v_sb])
            units = []
            for j in range(nb):
                for c in range(2):
                    pr = slice(c * 64, (c + 1) * 64)
                    tiles = []
                    for (i, mid, subs) in causal_tiles(j):
                        tiles.append(dict(k=k_sb[pr, i * 128:(i + 1) * 128], v=v_sb[:, i, :],
                                          r=[k_sb, v_sb],
                                          mask=(masks[:, mid, :], masks) if mid is not None else None,
                                          subs=subs))

                    def fin(O0, O1, c=c, j=j, h=h):
                        oc = oc0[j % 2]
                        oT_sb = oTs[j % 2]
                        tgt = oc if c == 0 else dd4
                        for bi, O in enumerate((O0, O1)):
                            cx.op("dve", lambda e: e.reciprocal(out=fs[:, bi * 2:bi * 2 + 2], in_=O[:, :, 128]),
                                  r=[O], w=[fs])
                            cx.op("dve", lambda e: e.tensor_tensor(
                                out=tgt[:, bi * 2:bi * 2 + 2, :], in0=O[:, :, 0:128],
                                in1=fs[:, bi * 2:bi * 2 + 2].unsqueeze(2).to_broadcast([128, 2, 128]),
                                op=ALU.mult), r=[O, fs], w=[tgt])
                        if c == 1:
                            cx.op("dve", lambda e: e.scalar_tensor_tensor(
                                out=dd4[:, :, :], in0=dd4[:, :, :], scalar=sm[:, 5:6], in1=oc[:, :, :],
                                op0=ALU.mult, op1=ALU.add), r=[dd4, sm, oc], w=[dd4])
                            cx.op("pool", lambda e: e.tensor_tensor(out=sq4[:, :, :], in0=dd4[:, :, :],
                                                                    in1=dd4[:, :, :], op=ALU.mult),
                                  r=[dd4], w=[sq4])
                            cx.op("dve", lambda e: e.reduce_sum(out=fs[:, 4:8], in_=sq4[:, :, :], axis=AX.X),
                                  r=[sq4], w=[fs])
                            cx.op("dve", lambda e: e.tensor_scalar(out=fs[:, 4:8], in0=fs[:, 4:8], scalar1=1.0 / 128,
                                                                   scalar2=EPS, op0=ALU.mult, op1=ALU.add),
                                  r=[fs], w=[fs])
                            cx.op("pool", lambda e: e.tensor_tensor(out=fs[:, 8:12], in0=fs[:, 4:8], in1=mhalf[:, 0:4],
                                                                    op=ALU.pow), r=[fs, mhalf], w=[fs])
                            cx.op("dve", lambda e: e.tensor_tensor(
                                out=dd4[:, :, :], in0=dd4[:, :, :],
                                in1=fs[:, 8:12].unsqueeze(2).to_broadcast([128, 4, 128]), op=ALU.mult),
                                r=[dd4, fs], w=[dd4])
                            cx.op("pool", lambda e: e.tensor_tensor(
                                out=yb4[:, :, :], in0=dd4[:, :, :],
                                in1=sub[:, :].unsqueeze(1).to_broadcast([128, 4, 128]), op=ALU.mult),
                                r=[dd4, sub], w=[yb4])
                            for s in range(4):
                                cx.op("pe", lambda e: e.transpose(
                                    out=tpf[:, s * 128:(s + 1) * 128], in_=yb4[:, s, :], identity=ident[:, :]),
                                    r=[yb4, ident], w=[tpf])
                            cx.op("dve", lambda e: e.tensor_copy(out=oT_sb[:, :], in_=tpf[:, :]),
                                  r=[tpf], w=[oT_sb])
                            cx.dma("pool", oTd[h, :, j * 512:(j + 1) * 512], oT_sb[:, :],
                                   r=[oT_sb], w=[oTd])
                    units.append(dict(q=q_sb[pr, j * 512:(j + 1) * 512], qr=[q_sb], tiles=tiles, fin=fin))
            attn_run_shared(cx, st, units, 129, "exp", scale=0.125)
    cx.barrier()


def attn_run_shared(cx, st, units, E1, mode, scale=1.0):
    key = "_attn_bufs"
    if not hasattr(st, key):
        setattr(st, key, dict(
            sS=[cx.ps(st, f"sS{i}", [128, 512]) for i in range(3)],
            Os=[[cx.ps(st, f"O{i}_{h}", [128, 2, 256]) for h in range(2)] for i in range(2)],
            Pb=[cx.sb(st, f"Pb{i}", [128, 512], BF16) for i in range(4)]))
    attn_run(cx, st, units, E1, mode, scale=scale, bufs=getattr(st, key))


def rot_cols(w):
    n = w.shape[1]
    idx = np.arange(n)
    g = idx // 64
    d = idx % 64
    src = g * 64 + (d + 32) % 64
    return np.ascontiguousarray(w[:, src])


def host_consts(ntok=T):
    pos = np.arange(ntok, dtype=np.float64)
    inv = 10000.0 ** (-np.arange(0, 64, 2, dtype=np.float64) / 64)
    d = np.arange(128) % 64
    ang = pos[None, :] * inv[d % 32][:, None]
    cos = np.cos(ang).astype(np.float32)
    sgn = np.where(d < 32, -1.0, 1.0)[:, None]
    sin = (np.sin(ang) * sgn).astype(np.float32)
    key = np.arange(128)[:, None, None]
    dl = np.arange(4)[None, :, None]
    q = np.arange(512)[None, None, :]
    cmask = ((128 * dl + key) <= q).astype(np.float32).astype(ml_dtypes.bfloat16)
    return {"c_cos": cos, "c_sin": sin, "c_cmask": cmask}


def declare_consts(cx, ntok=T):
    return {
        "cos": cx.dram("c_cos", [128, ntok], F32, kind="ExternalInput"),
        "sin": cx.dram("c_sin", [128, ntok], F32, kind="ExternalInput"),
        "cmask": cx.dram("c_cmask", [128, 4, 512], BF16, kind="ExternalInput"),
    }


def outproj_stage(cx, oTd, w_ap, x_in, x_out, ntok=T):
    with ExitStack() as st:
        wsb = cx.sb(st, "wo", [128, 8, D], BF16)
        stage = [cx.sb(st, f"wstage{i}", [128, 1024], F32) for i in range(2)]
        load_weight_bf16(cx, st, wsb, w_ap, 8, D, stage)
        oTs = [cx.sb(st, f"oTb{i}", [128, 8, 512], BF16) for i in range(2)]
        xrs = [cx.sb(st, f"xr{i}", [128, D], F32) for i in range(2)]
        py = [cx.ps(st, f"py{i}", [128, 512]) for i in range(2)]
        k = 0
        for b in range(ntok // 512):
            tb = b * 512
            o = oTs[b % 2]
            cx.dma("sp", o[:, :, :], oTd[:, :, tb:tb + 512].rearrange("c p t -> p c t"),
                   r=[oTd], w=[o])
            for j in range(4):
                t0 = tb + j * 128
                xr = xrs[j % 2]
                cx.dma("sp", xr[:, :], x_in[t0:t0 + 128, :], r=[x_in], w=[xr])
                for h in range(2):
                    y = py[k % 2]
                    k += 1
                    for c in range(8):
                        cx.op("pe", lambda e, c=c: e.matmul(
                            y[:, :], lhsT=o[:, c, j * 128:(j + 1) * 128],
                            rhs=wsb[:, c, h * 512:(h + 1) * 512], start=(c == 0), stop=(c == 7)),
                            r=[o, wsb], w=[y])
                    cx.op("dve", lambda e: e.tensor_tensor(
                        out=xr[:, h * 512:(h + 1) * 512], in0=y[:, :],
                        in1=xr[:, h * 512:(h + 1) * 512], op=ALU.add), r=[y, xr], w=[xr])
                cx.dma("pool", x_out[t0:t0 + 128, :], xr[:, :], r=[xr], w=[x_out])
    cx.barrier()


def rwkv_stage(cx, pTb, prm, oTd, ident, identf, ntok=T):
    NB = ntok // 512
    with ExitStack() as st:
        G = [cx.ps(st, f"G{i}", [128, 512]) for i in range(8)]

        def g3(i, n=64):
            return G[i].t[0:64, :].rearrange("p (h t) -> p h t", h=8)[:, :, 0:n]

        mu_c = load_col_vec(cx, st, "mu", prm["mu"], 1792)
        w0c = load_col_vec(cx, st, "w0", prm["w0"], 512)
        a0c = load_col_vec(cx, st, "a0", prm["a0"], 512)
        kkc = load_col_vec(cx, st, "kk", prm["kk"], 512)
        kac = load_col_vec(cx, st, "ka", prm["ka"], 512)
        rkc = load_col_vec(cx, st, "rk", prm["rk"], 512)
        lnw_b = bcast_load(cx, st, "lnw", prm["lnw"], 512, parts=64)
        lnb_b = bcast_load(cx, st, "lnb", prm["lnb"], 512, parts=64)
        wst = cx.sb(st, "wst", [128, 512], F32)
        w2_sb = cx.sb(st, "w2", [64, 512], BF16)
        a2_sb = cx.sb(st, "a2", [128, 512], BF16)
        g2_sb = cx.sb(st, "g2", [128, 512], BF16)
        cx.dma("sp", wst[0:64, :], prm["w2"], w=[wst])
        cx.op("dve", lambda e: e.tensor_copy(out=w2_sb[:, :], in_=wst[0:64, :]), r=[wst], w=[w2_sb])
        cx.dma("sp", wst[64:128, :], prm["a2"], r=[], w=[wst])
        cx.op("dve", lambda e: e.tensor_copy(out=a2_sb[64:128, :], in_=wst[64:128, :]), r=[wst], w=[a2_sb])
        cx.dma("sp", wst[:, :], prm["g2"], w=[wst])
        cx.op("dve", lambda e: e.tensor_copy(out=g2_sb[:, :], in_=wst[:, :]), r=[wst], w=[g2_sb])
        bones = cx.sb(st, "bones", [128, 128], BF16)
        cx.op("dve", lambda e: e.memset(bones[:, :], 0.0), w=[bones])
        cx.op("dve", lambda e: e.memset(bones[0:64, 0:64], 1.0), w=[bones])
        cx.op("dve", lambda e: e.memset(bones[64:128, 64:128], 1.0), w=[bones])
        hsel = cx.sb(st, "hsel", [128, 2], BF16)
        cx.op("dve", lambda e: e.memset(hsel[:, :], 0.0), w=[hsel])
        cx.op("dve", lambda e: e.memset(hsel[0:64, 0:1], 1.0), w=[hsel])
        cx.op("dve", lambda e: e.memset(hsel[64:128, 1:2], 1.0), w=[hsel])
        rmask = cx.sb(st, "rmask", [128, 512], F32)
        cx.op("dve", lambda e: e.memset(rmask[:, :], 1.0), w=[rmask])
        cx.op("dve", lambda e: e.memset(
            rmask[:, :].rearrange("p (c t) -> p c t", t=64)[:, :, 0:1], 0.0), w=[rmask])
        epsc = cx.sb(st, "epsc", [128, 2], F32)
        cx.op("dve", lambda e: e.memset(epsc[:, 0:1], 1e-12), w=[epsc])
        cx.op("dve", lambda e: e.memset(epsc[:, 1:2], 64e-5), w=[epsc])

        def mk_mask(name, step, cm, cmp, val):
            m = cx.sb(st, name, [64, 8, 64], F32)
            cx.op("pool", lambda e: e.memset(m[:, :, :], val), w=[m])
            cx.op("pool", lambda e: e.affine_select(out=m[:, :, :], in_=m[:, :, :],
                                                    pattern=[[0, 8], [step, 64]], compare_op=cmp,
                                                    fill=0.0, base=0, channel_multiplier=cm),
                  r=[m], w=[m])
            return m
        Mlt = mk_mask("Mlt", 1, -1, ALU.is_gt, 1.0)
        Mle = mk_mask("Mle", 1, -1, ALU.is_ge, 1.0)
        nMlt = mk_mask("nMlt", 1, -1, ALU.is_gt, -1.0)
        nMle = mk_mask("nMle", 1, -1, ALU.is_ge, -1.0)
        nMltT = mk_mask("nMltT", -1, 1, ALU.is_gt, -1.0)
        Ieye = mk_mask("Ieye", 1, -1, ALU.is_equal, 1.0)

        S32 = cx.sb(st, "S32", [128, 4, 64], F32)
        Sbf = cx.sb(st, "Sbf", [128, 4, 64], BF16)
        cx.op("dve", lambda e: e.memset(S32[:, :, :], 0.0), w=[S32])
        cx.op("dve", lambda e: e.memset(Sbf[:, :, :], 0.0), w=[Sbf])

        def dbl(name, shape, dt):
            return [cx.sb(st, f"{name}{i}", shape, dt) for i in range(2)]
        kb = dbl("kb", [128, 4, 8, 128], BF16)
        kr = dbl("kr", [128, 4, 8, 128], BF16)
        khT = dbl("khT", [64, 4, 8, 128], BF16)
        nbhT = dbl("nbhT", [64, 4, 8, 128], BF16)
        vT = dbl("vT", [64, 4, 8, 128], BF16)
        WLt = dbl("WLt", [128, 4, 8], F32)
        bon = dbl("bon", [64, 8, 8], F32)
        sgT = dbl("sgT", [128, 512], BF16)
        oTblk = dbl("oTblk", [128, 4, 512], BF16)

        raws = [cx.sb(st, f"raw{i}", [128, 513], F32) for i in range(3)]
        dtmp = [cx.sb(st, f"dtmp{i}", [128, 512], F32) for i in range(2)]

        def f32t(name):
            return cx.sb(st, name, [128, 512], F32)
        xm12, xm13, xr, xk, xv = f32t("xm12"), f32t("xm13"), f32t("xr"), f32t("xk"), f32t("xv")
        tw_bf = cx.sb(st, "tw_bf", [64, 512], BF16)
        al_bf = cx.sb(st, "al_bf", [128, 512], BF16)
        sigw, av, kkv, kap, k2, bv, cum = (f32t("sigw"), f32t("av"), f32t("kkv"), f32t("kap"),
                                           f32t("k2"), f32t("bv"), f32t("cum"))
        epos, eneg, eprev, ehat, tt = f32t("epos"), f32t("eneg"), f32t("eprev"), f32t("ehat"), f32t("tt")
        sq_bf = cx.sb(st, "sq_bf", [128, 512], BF16)
        khat_bf = cx.sb(st, "khat_bf", [128, 512], BF16)
        nbhat_bf = cx.sb(st, "nbhat_bf", [128, 512], BF16)
        v_bf = cx.sb(st, "v_bf", [128, 512], BF16)
        rkk_bf = cx.sb(st, "rkk_bf", [128, 512], BF16)
        kctr = [0]

        def load_xm(c, b, dst):
            raw = raws[kctr[0] % 3]
            d = dtmp[kctr[0] % 2]
            kctr[0] += 1
            tb = b * 512
            if b == 0:
                cx.op("dve", lambda e: e.memset(raw[:, 0:1], 0.0), w=[raw])
                cx.dma("sp", raw[:, 1:513], pTb[c, :, 0:512], r=[pTb], w=[raw])
            else:
                cx.dma("sp", raw[:, 0:513], pTb[c, :, tb - 1:tb + 512], r=[pTb], w=[raw])
            cx.op("pool", lambda e: e.tensor_tensor(out=d[:, :], in0=raw[:, 0:512], in1=raw[:, 1:513],
                                                    op=ALU.subtract), r=[raw], w=[d])
            cx.op("dve", lambda e: e.scalar_tensor_tensor(
                out=dst[:, :], in0=d[:, :], scalar=mu_c[:, c:c + 1], in1=raw[:, 1:513],
                op0=ALU.mult, op1=ALU.add), r=[d, raw, mu_c], w=[dst])

        def v3(buf):
            return buf[:, :].rearrange("p (c t) -> p c t", t=64)

        def phase1(b):
            par = b % 2
            load_xm(12, b, xm12)
            load_xm(13, b, xm13)
            cx.op("act", lambda e: e.activation(out=tw_bf[:, :], in_=xm12[0:64, :], func=AF.Tanh),
                  r=[xm12], w=[tw_bf])
            cx.op("dve", lambda e: e.tensor_copy(out=al_bf[64:128, :], in_=xm12[64:128, :]),
                  r=[xm12], w=[al_bf])
            cx.op("act", lambda e: e.activation(out=sgT[par][:, :], in_=xm13[:, :], func=AF.Sigmoid),
                  r=[xm13], w=[sgT[par]])
            for hp in range(4):
                yield
                cs = slice(hp * 128, (hp + 1) * 128)
                load_xm(hp, b, xr)
                load_xm(4 + hp, b, xk)
                load_xm(8 + hp, b, xv)
                pw, pa, pss = G[5], G[6], G[7]
                cx.op("pe", lambda e: e.matmul(pw[:, :], lhsT=w2_sb[0:64, cs], rhs=tw_bf[0:64, :],
                                               start=True, stop=True), r=[w2_sb, tw_bf], w=[pw])
                cx.op("pe", lambda e: e.matmul(pa[:, :], lhsT=a2_sb[64:128, cs], rhs=al_bf[64:128, :],
                                               start=True, stop=True), r=[a2_sb, al_bf], w=[pa])
                cx.op("act", lambda e: e.activation(out=sigw[:, :], in_=pw[:, :], func=AF.Sigmoid,
                                                    bias=w0c[:, hp:hp + 1]), r=[pw, w0c], w=[sigw])
                cx.op("act", lambda e: e.activation(out=av[:, :], in_=pa[:, :], func=AF.Sigmoid,
                                                    bias=a0c[:, hp:hp + 1]), r=[pa, a0c], w=[av])
                cx.op("dve", lambda e: e.tensor_scalar(out=sigw[:, :], in0=sigw[:, :],
                                                       scalar1=-0.6065306597126334, scalar2=None,
                                                       op0=ALU.mult), r=[sigw], w=[sigw])
                cx.op("dve", lambda e: e.tensor_tensor_scan(out=cum[:, :], data0=rmask[:, :],
                                                            data1=sigw[:, :], initial=0.0,
                                                            op0=ALU.mult, op1=ALU.add),
                      r=[rmask, sigw], w=[cum])
                yield
                cx.op("dve", lambda e: e.tensor_scalar(out=kkv[:, :], in0=xk[:, :],
                                                       scalar1=kkc[:, hp:hp + 1], scalar2=None,
                                                       op0=ALU.mult), r=[xk, kkc], w=[kkv])
                cx.op("act", lambda e: e.activation(out=sq_bf[:, :], in_=kkv[:, :], func=AF.Square),
                      r=[kkv], w=[sq_bf])
                cx.op("pe", lambda e: e.matmul(pss[:, :], lhsT=bones[:, :], rhs=sq_bf[:, :],
                                               start=True, stop=True), r=[bones, sq_bf], w=[pss])
                cx.op("act", lambda e: e.activation(out=tt[:, :], in_=pss[:, :], func=AF.Sqrt,
                                                    bias=epsc[:, 0:1]), r=[pss, epsc], w=[tt])
                cx.op("dve", lambda e: e.reciprocal(out=tt[:, :], in_=tt[:, :]), r=[tt], w=[tt])
                cx.op("dve", lambda e: e.tensor_tensor(out=kap[:, :], in0=kkv[:, :], in1=tt[:, :],
                                                       op=ALU.mult), r=[kkv, tt], w=[kap])
                yield
                cx.op("dve", lambda e: e.tensor_scalar(out=tt[:, :], in0=av[:, :], scalar1=-1.0,
                                                       scalar2=kac[:, hp:hp + 1], op0=ALU.add,
                                                       op1=ALU.mult), r=[av, kac], w=[tt])
                cx.op("dve", lambda e: e.scalar_tensor_tensor(out=k2[:, :], in0=tt[:, :], scalar=1.0,
                                                              in1=xk[:, :], op0=ALU.add, op1=ALU.mult),
                      r=[tt, xk], w=[k2])
                cx.op("pool", lambda e: e.tensor_tensor(out=bv[:, :], in0=kap[:, :], in1=av[:, :],
                                                        op=ALU.mult), r=[kap, av], w=[bv])
                yield
                cx.op("act", lambda e: e.activation(out=epos[:, :], in_=cum[:, :], func=AF.Exp),
                      r=[cum], w=[epos])
                cx.op("act", lambda e: e.activation(out=eneg[:, :], in_=cum[:, :], func=AF.Exp,
                                                    scale=-1.0), r=[cum], w=[eneg])
                cx.op("pool", lambda e: e.tensor_tensor(out=tt[:, :], in0=cum[:, :], in1=sigw[:, :],
                                                        op=ALU.subtract), r=[cum, sigw], w=[tt])
                cx.op("act", lambda e: e.activation(out=eprev[:, :], in_=tt[:, :], func=AF.Exp),
                      r=[tt], w=[eprev])
                cx.op("pool", lambda e: e.tensor_tensor(
                    out=v3(tt), in0=v3(cum)[:, :, 63:64].to_broadcast([128, 8, 64]), in1=v3(cum),
                    op=ALU.subtract), r=[cum], w=[tt])
                cx.op("act", lambda e: e.activation(out=ehat[:, :], in_=tt[:, :], func=AF.Exp),
                      r=[tt], w=[ehat])
                cx.op("act", lambda e: e.activation(out=WLt[par][:, hp, :], in_=v3(cum)[:, :, 63],
                                                    func=AF.Exp), r=[cum], w=[WLt[par]])
                yield
                cx.op("dve", lambda e: e.tensor_tensor(out=kb[par][:, hp, :, 0:64], in0=v3(k2),
                                                       in1=v3(eneg), op=ALU.mult), r=[k2, eneg], w=[kb[par]])
                cx.op("dve", lambda e: e.tensor_tensor(out=kb[par][:, hp, :, 64:128], in0=v3(bv),
                                                       in1=v3(eneg), op=ALU.mult), r=[bv, eneg], w=[kb[par]])
                cx.op("dve", lambda e: e.tensor_tensor(out=kr[par][:, hp, :, 0:64], in0=v3(kap),
                                                       in1=v3(eprev), op=ALU.mult), r=[kap, eprev], w=[kr[par]])
                cx.op("dve", lambda e: e.tensor_tensor(out=kr[par][:, hp, :, 64:128], in0=v3(xr),
                                                       in1=v3(epos), op=ALU.mult), r=[xr, epos], w=[kr[par]])
                cx.op("pool", lambda e: e.tensor_tensor(out=khat_bf[:, :], in0=k2[:, :], in1=ehat[:, :],
                                                        op=ALU.mult), r=[k2, ehat], w=[khat_bf])
                cx.op("dve", lambda e: e.scalar_tensor_tensor(
                    out=nbhat_bf[:, :], in0=bv[:, :], scalar=-1.0, in1=ehat[:, :], op0=ALU.mult,
                    op1=ALU.mult), r=[bv, ehat], w=[nbhat_bf])
                cx.op("act", lambda e: e.copy(out=v_bf[:, :], in_=xv[:, :]), r=[xv], w=[v_bf])
                cx.op("dve", lambda e: e.scalar_tensor_tensor(
                    out=rkk_bf[:, :], in0=xr[:, :], scalar=rkc[:, hp:hp + 1], in1=k2[:, :],
                    op0=ALU.mult, op1=ALU.mult), r=[xr, rkc, k2], w=[rkk_bf])
                yield
                for src, dst, bank in ((khat_bf, khT[par], G[0]), (nbhat_bf, nbhT[par], G[1]),
                                       (v_bf, vT[par], G[2])):
                    pt = bank.t[0:64, :].bitcast(BF16)
                    for ch in range(8):
                        cx.op("pe", lambda e, ch=ch: e.transpose(
                            out=pt[:, ch * 128:(ch + 1) * 128], in_=src[:, ch * 64:(ch + 1) * 64],
                            identity=ident[:, :]), r=[src, ident], w=[bank])
                    cx.op("act", lambda e: e.copy(out=dst[:, hp, :, :],
                                                  in_=pt.rearrange("p (c k) -> p c k", c=8)),
                          r=[bank], w=[dst])
                yield
                pbn = G[3]
                for ch in range(8):
                    cx.op("pe", lambda e, ch=ch: e.matmul(
                        pbn[0:64, ch * 2:(ch + 1) * 2], lhsT=rkk_bf[:, ch * 64:(ch + 1) * 64],
                        rhs=hsel[:, :], start=True, stop=True), r=[rkk_bf, hsel], w=[pbn])
                cx.op("act", lambda e: e.copy(
                    out=bon[par][:, :, hp * 2:(hp + 1) * 2],
                    in_=pbn[0:64, 0:16].rearrange("p (c e) -> p c e", e=2)), r=[pbn], w=[bon[par]])

        def t64(name, dt):
            return cx.sb(st, name, [64, 8, 64], dt)
        Avk_sb, Bkr_sb, nBbr_sb, Z_sb = t64("Avk", BF16), t64("Bkr", BF16), t64("nBbr", BF16), t64("Zsb", BF16)
        N_sb, NT_sb, RZ_sb = t64("Nsb", BF16), t64("NTsb", BF16), t64("RZsb", F32)
        Pbf = t64("Pbf", BF16)
        Ysbs = [t64("Ysb0", F32), t64("Ysb1", F32)]
        yc, ysq = t64("yc", F32), t64("ysq", F32)
        bvv = t64("bvv", F32)
        st8 = cx.sb(st, "st8", [64, 4, 8], F32)
        out_bf = cx.sb(st, "out_bf", [64, 512], BF16)
        Stmp = cx.sb(st, "Stmp", [128, 4, 64], F32)

        Pinv = [t64("Pinv0", F32), t64("Pinv1", F32)]
        HS = [(h, h // 2, slice((h % 2) * 64, (h % 2) * 64 + 64)) for h in (0, 2, 4, 6, 1, 3, 5, 7)]

        def opsof(par, ch, h, hp, pr):
            return (kb[par][pr, hp, ch, 0:64], kb[par][pr, hp, ch, 64:128],
                    kr[par][pr, hp, ch, 0:64], kr[par][pr, hp, ch, 64:128])

        def inv_gen(b, ch):
            par = b % 2
            P_sb = Pinv[ch % 2]
            rd = [kb[par], kr[par]]
            for half in (HS[0:4], HS[4:8]):
                for (bank, li, ri) in ((5, 1, 2), (6, 2, 1)):
                    for (h, hp, pr) in half:
                        o = opsof(par, ch, h, hp, pr)
                        cx.op("pe", lambda e: e.matmul(g3(bank)[:, h, :], lhsT=o[li], rhs=o[ri],
                                                       start=True, stop=True), r=rd, w=[G[bank]],
                              rows=(pr.start, 64))
            cx.op("dve", lambda e: e.tensor_tensor(out=N_sb[:, :, :], in0=g3(5), in1=nMlt[:, :, :],
                                                   op=ALU.mult), r=[G[5], nMlt], w=[N_sb])
            cx.op("dve", lambda e: e.tensor_tensor(out=NT_sb[:, :, :], in0=g3(6), in1=nMltT[:, :, :],
                                                   op=ALU.mult), r=[G[6], nMltT], w=[NT_sb])
            cx.op("pool", lambda e: e.tensor_tensor(out=P_sb[:, :, :], in0=N_sb[:, :, :],
                                                    in1=Ieye[:, :, :], op=ALU.add), r=[N_sb, Ieye], w=[P_sb])
            cx.op("pool", lambda e: e.tensor_copy(out=Pbf[:, :, :], in_=P_sb[:, :, :]), r=[P_sb], w=[Pbf])
            yield
            for lev in range(5):
                for (h, hp, pr) in (HS if lev < 4 else []):
                    cx.op("pe", lambda e: e.matmul(g3(5)[:, h, :], lhsT=NT_sb[:, h, :], rhs=N_sb[:, h, :],
                                                   start=True, stop=True), r=[NT_sb, N_sb], w=[G[5]], rows=(0, 64))
                for (h, hp, pr) in HS:
                    cx.op("pe", lambda e: e.matmul(g3(6)[:, h, :], lhsT=N_sb[:, h, :], rhs=NT_sb[:, h, :],
                                                   start=True, stop=True), r=[NT_sb, N_sb], w=[G[6]], rows=(0, 64))
                if lev < 4:
                    cx.op("act", lambda e: e.copy(out=N_sb[:, :, :], in_=g3(5)), r=[G[5]], w=[N_sb])
                cx.op("dve", lambda e: e.tensor_copy(out=NT_sb[:, :, :], in_=g3(6)), r=[G[6]], w=[NT_sb])
                yield
                for (h, hp, pr) in HS:
                    cx.op("pe", lambda e: e.matmul(g3(7)[:, h, :], lhsT=NT_sb[:, h, :], rhs=Pbf[:, h, :],
                                                   start=True, stop=True), r=[NT_sb, Pbf], w=[G[7]], rows=(0, 64))
                cx.op("dve", lambda e: e.tensor_tensor(out=P_sb[:, :, :], in0=g3(7), in1=P_sb[:, :, :],
                                                       op=ALU.add), r=[G[7], P_sb], w=[P_sb])
                if lev < 4:
                    cx.op("act", lambda e: e.copy(out=Pbf[:, :, :], in_=P_sb[:, :, :]), r=[P_sb], w=[Pbf])
                yield

        def state_gen(b, ch):
            par = b % 2
            P_sb = Pinv[ch % 2]
            rd = [kb[par], kr[par]]
            for half in (HS[0:4], HS[4:8]):
                for (bank, li, ri) in ((0, 0, 2), (2, 0, 3), (3, 1, 3)):
                    for (h, hp, pr) in half:
                        o = opsof(par, ch, h, hp, pr)
                        cx.op("pe", lambda e: e.matmul(g3(bank)[:, h, :], lhsT=o[li], rhs=o[ri],
                                                       start=True, stop=True), r=rd, w=[G[bank]],
                              rows=(pr.start, 64))
            for (bank, m, dst) in ((0, Mlt, Avk_sb), (2, Mle, Bkr_sb), (3, nMle, nBbr_sb)):
                cx.op("dve", lambda e: e.tensor_tensor(out=dst[:, :, :], in0=g3(bank), in1=m[:, :, :],
                                                       op=ALU.mult), r=[G[bank], m], w=[dst])
            yield
            for (h, hp, pr) in HS[4:8] + HS[0:4]:
                o = opsof(par, ch, h, hp, pr)
                cx.op("pe", lambda e: e.matmul(g3(0)[:, h, :], lhsT=o[2], rhs=Sbf[pr, hp, :],
                                               start=(h == 1), stop=False, skip_group_check=True),
                      r=[kr[par], Sbf], w=[G[0]], rows=(pr.start, 64))
            for (h, hp, pr) in HS:
                e_ = h % 2
                cx.op("pe", lambda e: e.matmul(g3(0)[:, h, :], lhsT=Avk_sb[:, h, :],
                                               rhs=vT[par][:, hp, ch, e_ * 64:(e_ + 1) * 64],
                                               start=False, stop=True, skip_group_check=True),
                      r=[Avk_sb, vT[par]], w=[G[0]], rows=(0, 64))
            cx.op("act", lambda e: e.copy(out=RZ_sb[:, :, :], in_=g3(0)), r=[G[0]], w=[RZ_sb])
            yield
            for (h, hp, pr) in HS:
                cx.op("pe", lambda e: e.matmul(g3(1)[:, h, :], lhsT=P_sb[:, h, :], rhs=RZ_sb[:, h, :],
                                               start=True, stop=True), r=[P_sb, RZ_sb], w=[G[1]], rows=(0, 64))
            cx.op("act", lambda e: e.copy(out=Z_sb[:, :, :], in_=g3(1)), r=[G[1]], w=[Z_sb])
            yield
            for (h, hp, pr) in HS[4:8] + HS[0:4]:
                o = opsof(par, ch, h, hp, pr)
                cx.op("pe", lambda e: e.matmul(g3(2)[:, h, :], lhsT=o[3], rhs=Sbf[pr, hp, :],
                                               start=(h == 1), stop=False, skip_group_check=True),
                      r=[kr[par], Sbf], w=[G[2]], rows=(pr.start, 64))
            for (h, hp, pr) in HS:
                e_ = h % 2
                vh = vT[par][:, hp, ch, e_ * 64:(e_ + 1) * 64]
                cx.op("pe", lambda e: e.matmul(g3(2)[:, h, :], lhsT=Bkr_sb[:, h, :], rhs=vh,
                                               start=False, stop=False, skip_group_check=True),
                      r=[Bkr_sb, vT[par]], w=[G[2]], rows=(0, 64))
            for (h, hp, pr) in HS:
                cx.op("pe", lambda e: e.matmul(g3(2)[:, h, :], lhsT=nBbr_sb[:, h, :], rhs=Z_sb[:, h, :],
                                               start=False, stop=True, skip_group_check=True),
                      r=[nBbr_sb, Z_sb], w=[G[2]], rows=(0, 64))
            SU = G[3].t[:, :].rearrange("p (a c) -> p a c", a=4)
            for hp in range(4):
                cx.op("pe", lambda e: e.matmul(SU[:, hp, :], lhsT=khT[par][:, hp, ch, :],
                                               rhs=vT[par][:, hp, ch, :], start=(hp == 0), stop=False,
                                               skip_group_check=True), r=[khT[par], vT[par]], w=[G[3]],
                      rows=(0, 64))
                cx.op("pe", lambda e: e.matmul(
                    SU[:, hp, :], lhsT=nbhT[par][:, hp, ch, :],
                    rhs=Z_sb[:, 2 * hp:2 * hp + 2, :].rearrange("p a v -> p (a v)"),
                    start=False, stop=True, skip_group_check=True), r=[nbhT[par], Z_sb], w=[G[3]],
                    rows=(0, 64))
            cx.op("pool", lambda e: e.tensor_tensor(
                out=Stmp[:, :, :], in0=S32[:, :, :],
                in1=WLt[par][:, :, ch:ch + 1].to_broadcast([128, 4, 64]), op=ALU.mult),
                r=[S32, WLt[par]], w=[Stmp])
            for e_ in range(2):
                pr = slice(e_ * 64, (e_ + 1) * 64)
                cx.op("dve", lambda e: e.tensor_tensor(
                    out=S32[pr, :, :], in0=SU[pr, :, e_ * 64:(e_ + 1) * 64], in1=Stmp[pr, :, :],
                    op=ALU.add), r=[G[3], Stmp], w=[S32])
            cx.op("act", lambda e: e.copy(out=Sbf[:, :, :], in_=S32[:, :, :]), r=[S32], w=[Sbf])
            Ysb = Ysbs[ch % 2]
            cx.op("act", lambda e: e.copy(out=Ysb[:, :, :], in_=g3(2)), r=[G[2]], w=[Ysb])

        def out_gen(b, ch):
            par = b % 2
            Ysb = Ysbs[ch % 2]
            cx.op("pe", lambda e: e.matmul(G[4].t[0:64, :], lhsT=sgT[par][:, ch * 64:(ch + 1) * 64],
                                           rhs=g2_sb[:, :], start=True, stop=True),
                  r=[sgT[par], g2_sb], w=[G[4]])
            cx.op("dve", lambda e: e.reduce_sum(out=st8[:, 0, :], in_=Ysb[:, :, :], axis=AX.X),
                  r=[Ysb], w=[st8])
            cx.op("dve", lambda e: e.tensor_scalar(out=st8[:, 1, :], in0=st8[:, 0, :], scalar1=1.0 / 64,
                                                   scalar2=None, op0=ALU.mult), r=[st8], w=[st8])
            cx.op("dve", lambda e: e.tensor_tensor(
                out=yc[:, :, :], in0=Ysb[:, :, :],
                in1=st8[:, 1, :].unsqueeze(2).to_broadcast([64, 8, 64]), op=ALU.subtract),
                r=[Ysb, st8], w=[yc])
            cx.op("pool", lambda e: e.tensor_tensor(out=ysq[:, :, :], in0=yc[:, :, :], in1=yc[:, :, :],
                                                    op=ALU.mult), r=[yc], w=[ysq])
            yield
            cx.op("dve", lambda e: e.reduce_sum(out=st8[:, 2, :], in_=ysq[:, :, :], axis=AX.X),
                  r=[ysq], w=[st8])
            cx.op("dve", lambda e: e.tensor_scalar(out=st8[:, 2, :], in0=st8[:, 2, :], scalar1=1.0 / 64,
                                                   scalar2=64e-5, op0=ALU.mult, op1=ALU.add), r=[st8], w=[st8])
            cx.op("pool", lambda e: e.tensor_tensor(out=st8[:, 3, :], in0=st8[:, 2, :], in1=mhalf[:, :],
                                                    op=ALU.pow), r=[st8, mhalf], w=[st8])
            cx.op("dve", lambda e: e.tensor_tensor(
                out=yc[:, :, :], in0=yc[:, :, :],
                in1=st8[:, 3, :].unsqueeze(2).to_broadcast([64, 8, 64]), op=ALU.mult),
                r=[yc, st8], w=[yc])
            ycf = yc[:, :, :].rearrange("p h v -> p (h v)")
            cx.op("pool", lambda e: e.tensor_tensor(out=ycf, in0=ycf, in1=lnw_b[:, :], op=ALU.mult),
                  r=[yc, lnw_b], w=[yc])
            yield
            cx.op("pool", lambda e: e.tensor_tensor(out=ycf, in0=ycf, in1=lnb_b[:, :], op=ALU.add),
                  r=[yc, lnb_b], w=[yc])
            vview = vT[par][:, :, ch, :].rearrange("p a (e v) -> p a e v", e=2)
            cx.op("dve", lambda e: e.tensor_tensor(
                out=bvv[:, :, :].rearrange("p (a e) v -> p a e v", e=2), in0=vview,
                in1=bon[par][:, ch, :].rearrange("p (a e) -> p a e", e=2).unsqueeze(3).to_broadcast([64, 4, 2, 64]),
                op=ALU.mult), r=[vT[par], bon[par]], w=[bvv])
            cx.op("pool", lambda e: e.tensor_tensor(out=yc[:, :, :], in0=yc[:, :, :], in1=bvv[:, :, :],
                                                    op=ALU.add), r=[yc, bvv], w=[yc])
            cx.op("dve", lambda e: e.tensor_tensor(out=out_bf[:, :], in0=G[4].t[0:64, :], in1=ycf,
                                                   op=ALU.mult), r=[G[4], yc], w=[out_bf])
            yield
            ptT = G[4].t[:, :].bitcast(BF16)
            for hp in range(4):
                cx.op("pe", lambda e: e.transpose(out=ptT[:, hp * 64:(hp + 1) * 64],
                                                  in_=out_bf[:, hp * 128:(hp + 1) * 128],
                                                  identity=ident[0:64, 0:64]), r=[out_bf, ident], w=[G[4]])
            cx.op("act", lambda e: e.copy(
                out=oTblk[par][:, :, ch * 64:(ch + 1) * 64],
                in_=ptT[:, 0:256].rearrange("p (a t) -> p a t", a=4)), r=[G[4]], w=[oTblk[par]])

        def run_interleaved(gens):
            gens = [g for g in gens if g is not None]
            while gens:
                for g in list(gens):
                    try:
                        next(g)
                    except StopIteration:
                        gens.remove(g)

        mhalf = cx.sb(st, "mhalf", [64, 8], F32)
        cx.op("dve", lambda e: e.memset(mhalf[:, :], -0.5), w=[mhalf])

        def step(g, k):
            if g is None:
                return None
            for _ in range(k):
                try:
                    next(g)
                except StopIteration:
                    return None
            return g

        run_interleaved([phase1(0)])
        for b in range(NB):
            p1 = phase1(b + 1) if b + 1 < NB else None
            run_interleaved([inv_gen(b, 0)])
            for ch in range(8):
                gens = [state_gen(b, ch), inv_gen(b, ch + 1) if ch < 7 else None,
                        out_gen(b, ch - 1) if ch > 0 else None]
                gens = [g for g in gens if g is not None]
                while gens:
                    for g in list(gens):
                        try:
                            next(g)
                        except StopIteration:
                            gens.remove(g)
                    p1 = step(p1, 1)
            run_interleaved([out_gen(b, 7)])
            if p1 is not None:
                run_interleaved([p1])
            for hp in range(4):
                cx.dma("pool", oTd[4 + hp, :, b * 512:(b + 1) * 512], oTblk[b % 2][:, hp, :],
                       r=[oTblk[b % 2]], w=[oTd])
    cx.barrier()


def mlstm_stage(cx, dqkT, di_d, df_d, dv_d, og_d, prm, Bscr, oTd, ident, identf, consts, ntok=T):
    nb = ntok // 512
    nkt = ntok // 128
    LN8 = float(np.log(0.125))
    with ExitStack() as st:
        qk = [cx.sb(st, f"qk{i}", [128, ntok], BF16) for i in range(4)]
        cw = cx.sb(st, "cw", [128, 4, 4], F32)
        for j in range(4):
            cx.dma("sp", cw[:, :, j], prm["conv_w"][j].rearrange("(c p) -> p c", p=128), w=[cw],
                   allow_slow_non_contiguous=True)
        cb = load_col_vec(cx, st, "cb", prm["conv_b"], 512)
        masks = cx.sb(st, "masks", [128, 4, 512], BF16)
        cx.dma("sp", masks[:, :, :], consts["cmask"].t, r=[consts["cmask"]], w=[masks])
        nrm = bcast_load(cx, st, "nrm", prm["norm"], 512)
        epsc = cx.sb(st, "epsc", [128, 1], F32)
        cx.op("dve", lambda e: e.memset(epsc[:, :], EPS), w=[epsc])
        rf = cx.sb(st, "rf", [128, nkt, 4, nb], F32)
        cf = cx.sb(st, "cf", [128, nkt, 4], F32)
        with ExitStack() as st2:
            xb = cx.sb(st2, "xb", [128, ntok + 3], F32)
            yb = cx.sb(st2, "yb", [128, ntok], F32)
            cx.op("dve", lambda e: e.memset(xb[:, 0:3], 0.0), w=[xb])
            for c in range(4):
                cx.dma("sp", xb[:, 3:ntok + 3], dqkT[c, :, :ntok], r=[dqkT], w=[xb])
                cx.op("dve", lambda e: e.tensor_scalar(out=yb[:, :], in0=xb[:, 3:ntok + 3],
                                                       scalar1=cw[:, c, 3:4], scalar2=cb[:, c:c + 1],
                                                       op0=ALU.mult, op1=ALU.add), r=[xb, cw, cb], w=[yb])
                for j in range(3):
                    cx.op("dve", lambda e: e.scalar_tensor_tensor(
                        out=yb[:, :], in0=xb[:, j:ntok + j], scalar=cw[:, c, j:j + 1], in1=yb[:, :],
                        op0=ALU.mult, op1=ALU.add), r=[xb, cw, yb], w=[yb])
                cx.op("act", lambda e: e.activation(out=qk[c][:, :], in_=yb[:, :], func=AF.Silu),
                      r=[yb], w=[qk[c]])
            gi = cx.sb(st2, "gi", [4, ntok], F32)
            gf = cx.sb(st2, "gf", [4, ntok], F32)
            Bn = cx.sb(st2, "Bn", [4, ntok], F32)
            ones4 = cx.sb(st2, "ones4", [4, ntok], F32)
            gb = cx.sb(st2, "gb", [4, 4], F32)
            cx.dma("sp", gb[:, 0:1], prm["ig_b"].rearrange("(p o) -> p o", o=1), w=[gb])
            cx.dma("sp", gb[:, 1:2], prm["fg_b"].rearrange("(p o) -> p o", o=1), w=[gb])
            cx.op("dve", lambda e: e.tensor_scalar(out=gb[:, 2:3], in0=gb[:, 1:2], scalar1=-1.0,
                                                   scalar2=None, op0=ALU.mult), r=[gb], w=[gb])
            cx.dma("sp", gi[:, :], di_d[:, :ntok], r=[di_d], w=[gi])
            cx.dma("sp", gf[:, :], df_d[:, :ntok], r=[df_d], w=[gf])
            cx.op("dve", lambda e: e.memset(ones4[:, :], 1.0), w=[ones4])
            cx.op("act", lambda e: e.activation(out=gf[:, :], in_=gf[:, :], func=AF.Exp, scale=-1.0,
                                                bias=gb[:, 2:3]), r=[gf, gb], w=[gf])
            cx.op("dve", lambda e: e.tensor_scalar(out=gf[:, :], in0=gf[:, :], scalar1=1.0, scalar2=None,
                                                   op0=ALU.add), r=[gf], w=[gf])
            cx.op("act", lambda e: e.activation(out=gf[:, :], in_=gf[:, :], func=AF.Ln), r=[gf], w=[gf])
            cx.op("dve", lambda e: e.tensor_tensor_scan(out=Bn[:, :], data0=ones4[:, :], data1=gf[:, :],
                                                        initial=0.0, op0=ALU.mult, op1=ALU.add),
                  r=[ones4, gf], w=[Bn])
            cx.op("dve", lambda e: e.scalar_tensor_tensor(out=gi[:, :], in0=gi[:, :], scalar=gb[:, 0:1],
                                                          in1=Bn[:, :], op0=ALU.add, op1=ALU.add),
                  r=[gi, gb, Bn], w=[gi])
            cx.dma("pool", Bscr[:, :ntok], Bn[:, :], r=[Bn], w=[Bscr])
            bref = cx.sb(st2, "bref", [128, 4, nb], F32)
            bneg = cx.sb(st2, "bneg", [128, 4, nb], F32)
            bpos = cx.sb(st2, "bpos", [128, 4, nb], F32)
            cx.op("dve", lambda e: e.memset(bref[:, :, :], 0.0), w=[bref])
            if nb > 1:
                for h in range(4):
                    cx.dma("sp", bref[:, h, 1:nb], Bscr[h, 511:ntok - 1:512].partition_broadcast(128),
                           r=[Bscr], w=[bref], allow_slow_non_contiguous=True)
            cx.op("dve", lambda e: e.tensor_scalar(out=bneg[:, :, :], in0=bref[:, :, :], scalar1=-1.0,
                                                   scalar2=None, op0=ALU.mult), r=[bref], w=[bneg])
            cx.op("dve", lambda e: e.tensor_scalar(out=bpos[:, :, :], in0=bref[:, :, :], scalar1=LN8,
                                                   scalar2=None, op0=ALU.add), r=[bref], w=[bpos])
            uT = cx.sb(st2, "uT", [128, nkt, 4], F32)
            BnT = cx.sb(st2, "BnT", [128, nkt, 4], F32)
            ptr = cx.ps(st2, "ptr", [128, 512])
            for src, dst in ((gi, uT), (Bn, BnT)):
                for i in range(nkt):
                    cx.op("pe", lambda e, i=i: e.transpose(out=ptr[:, i * 4:(i + 1) * 4],
                                                           in_=src[0:4, i * 128:(i + 1) * 128],
                                                           identity=identf[0:4, 0:4]),
                          r=[src, identf], w=[ptr])
                cx.op("dve", lambda e: e.tensor_copy(
                    out=dst[:, :, :], in_=ptr[:, 0:nkt * 4].rearrange("p (i h) -> p i h", h=4)),
                    r=[ptr], w=[dst])
            for h in range(4):
                for j in range(nb):
                    ni = 4 * j + 4
                    cx.op("act", lambda e: e.activation(out=rf[:, 0:ni, h, j], in_=uT[:, 0:ni, h],
                                                        func=AF.Exp, bias=bneg[:, h, j:j + 1]),
                          r=[uT, bneg], w=[rf])
                    cx.op("act", lambda e: e.activation(out=cf[:, 4 * j:4 * j + 4, h],
                                                        in_=BnT[:, 4 * j:4 * j + 4, h], func=AF.Exp,
                                                        scale=-1.0, bias=bpos[:, h, j:j + 1]),
                          r=[BnT, bpos], w=[cf])
        cx.barrier()
        Vs = [cx.sb(st, f"V{i}", [128, nkt, 129], BF16) for i in range(2)]
        for v in Vs:
            cx.op("dve", lambda e, v=v: e.memset(v[:, :, 128:129], 1.0), w=[v])
        ogs = [cx.sb(st, f"og{i}", [128, 4, 128], F32) for i in range(2)]
        fs = cx.sb(st, "fs", [128, 16], F32)
        hh4 = cx.sb(st, "hh4", [128, 4, 128], F32)
        sq4 = cx.sb(st, "sq4", [128, 4, 128], F32)
        yb4 = cx.sb(st, "yb4", [128, 4, 128], BF16)
        mhalf = cx.sb(st, "mhalf", [128, 4], F32)
        cx.op("dve", lambda e: e.memset(mhalf[:, :], -0.5), w=[mhalf])
        oTs = [cx.sb(st, f"oTs{i}", [128, 512], BF16) for i in range(2)]
        tpf = cx.ps(st, "tpf", [128, 512], BF16)
        for h in range(4):
            v_sb = Vs[h % 2]
            cx.dma("sp", v_sb[:, :, 0:128],
                   dv_d[:ntok, h * 128:(h + 1) * 128].rearrange("(i p) e -> p i e", p=128),
                   r=[dv_d], w=[v_sb])
            q_sb = qk[h // 2]
            k_sb = qk[2 + h // 2]
            pr = slice((h % 2) * 64, (h % 2) * 64 + 64)
            units = []
            for j in range(nb):
                tiles = []
                for (i, mid, subs) in causal_tiles(j):
                    tiles.append(dict(k=k_sb[pr, i * 128:(i + 1) * 128], v=v_sb[:, i, :],
                                      r=[k_sb, v_sb],
                                      mask=(masks[:, mid, :], masks) if mid is not None else None,
                                      rowfac=(rf[:, i, h, j:j + 1], rf), subs=subs))

                def fin(O0, O1, j=j, h=h):
                    og = ogs[j % 2]
                    oT_sb = oTs[j % 2]
                    cx.dma("sp", og[:, :, :],
                           og_d[j * 512:(j + 1) * 512, h * 128:(h + 1) * 128].rearrange(
                               "(s p) e -> p s e", p=128), r=[og_d], w=[og])
                    for bi, O in enumerate((O0, O1)):
                        it0 = 4 * j + 2 * bi
                        cfv = cf[:, it0:it0 + 2, h]
                        f0, f1, f2 = fs[:, 0:2], fs[:, 2:4], fs[:, 4:6]
                        cx.op("dve", lambda e: e.tensor_tensor(out=f0, in0=O[:, :, 128], in1=cfv, op=ALU.mult),
                              r=[O, cf], w=[fs])
                        cx.op("dve", lambda e: e.tensor_scalar(out=f1, in0=f0, scalar1=-1.0, scalar2=1.0,
                                                               op0=ALU.mult, op1=ALU.max), r=[fs], w=[fs])
                        cx.op("dve", lambda e: e.tensor_tensor(out=f1, in0=f1, in1=f0, op=ALU.max), r=[fs], w=[fs])
                        cx.op("dve", lambda e: e.reciprocal(out=f2, in_=f1), r=[fs], w=[fs])
                        cx.op("dve", lambda e: e.tensor_tensor(out=f2, in0=f2, in1=cfv, op=ALU.mult),
                              r=[fs, cf], w=[fs])
                        cx.op("dve", lambda e: e.tensor_tensor(
                            out=hh4[:, 2 * bi:2 * bi + 2, :], in0=O[:, :, 0:128],
                            in1=f2.unsqueeze(2).to_broadcast([128, 2, 128]), op=ALU.mult), r=[O, fs], w=[hh4])
                    cx.op("pool", lambda e: e.tensor_tensor(out=sq4[:, :, :], in0=hh4[:, :, :], in1=hh4[:, :, :],
                                                            op=ALU.mult), r=[hh4], w=[sq4])
                    cx.op("dve", lambda e: e.reduce_sum(out=fs[:, 8:12], in_=sq4[:, :, :], axis=AX.X),
                          r=[sq4], w=[fs])
                    cx.op("dve", lambda e: e.tensor_scalar(out=fs[:, 8:12], in0=fs[:, 8:12], scalar1=1.0 / 128,
                                                           scalar2=EPS, op0=ALU.mult, op1=ALU.add), r=[fs], w=[fs])
                    cx.op("pool", lambda e: e.tensor_tensor(out=fs[:, 12:16], in0=fs[:, 8:12], in1=mhalf[:, 0:4],
                                                            op=ALU.pow), r=[fs, mhalf], w=[fs])
                    cx.op("dve", lambda e: e.tensor_tensor(
                        out=hh4[:, :, :], in0=hh4[:, :, :],
                        in1=fs[:, 12:16].unsqueeze(2).to_broadcast([128, 4, 128]), op=ALU.mult),
                        r=[hh4, fs], w=[hh4])
                    cx.op("pool", lambda e: e.tensor_tensor(
                        out=hh4[:, :, :], in0=hh4[:, :, :],
                        in1=nrm[:, h * 128:(h + 1) * 128].unsqueeze(1).to_broadcast([128, 4, 128]), op=ALU.mult),
                        r=[hh4, nrm], w=[hh4])
                    cx.op("dve", lambda e: e.tensor_tensor(out=yb4[:, :, :], in0=hh4[:, :, :], in1=og[:, :, :],
                                                           op=ALU.mult), r=[hh4, og], w=[yb4])
                    for s in range(4):
                        cx.op("pe", lambda e: e.transpose(out=tpf[:, s * 128:(s + 1) * 128], in_=yb4[:, s, :],
                                                          identity=ident[:, :]), r=[yb4, ident], w=[tpf])
                    cx.op("dve", lambda e: e.tensor_copy(out=oT_sb[:, :], in_=tpf[:, :]), r=[tpf], w=[oT_sb])
                    cx.dma("pool", oTd[4 + h, :, j * 512:(j + 1) * 512], oT_sb[:, :], r=[oT_sb], w=[oTd])
                units.append(dict(q=q_sb[pr, j * 512:(j + 1) * 512], qr=[q_sb], tiles=tiles, fin=fin))
            attn_run_shared(cx, st, units, 129, "scale")
    cx.barrier()


NCMP = 255


def nsa_compress_stage(cx, zT_d, pe_ap, w1_ap, w2a_ap, w2b_ap, is_k, out_d, consts):
    with ExitStack() as st:
        zT = cx.sb(st, "zT", [128, T], F32)
        cx.dma("sp", zT[:, :], zT_d[:, :], r=[zT_d], w=[zT])
        pe_c = cx.sb(st, "pe_c", [128, 32], F32)
        cx.dma("sp", pe_c[:, :], pe_ap, w=[pe_c])
        w1s = [cx.sb(st, f"w1s{i}", [128, 8, 256], F32) for i in range(2)]
        w1 = cx.sb(st, "w1", [128, 32, 256], BF16)
        w1v = w1_ap.rearrange("(j d) h -> d j h", d=64)
        for q4 in range(4):
            sg = w1s[q4 % 2]
            for g in range(2):
                cx.dma("sp", sg[g * 64:(g + 1) * 64, :, :], w1v[:, q4 * 8:(q4 + 1) * 8, :], w=[sg])
            cx.op("pool", lambda e: e.tensor_copy(out=w1[:, q4 * 8:(q4 + 1) * 8, :], in_=sg[:, :, :]),
                  r=[sg], w=[w1])
        nw2 = 128 if is_k else 64
        w2s = cx.sb(st, "w2s", [128, 2, 2, 128], F32)
        w2 = cx.sb(st, "w2", [128, 2, 2, 128], BF16)
        cx.dma("sp", w2s[:, 0, :, 0:nw2], w2a_ap.rearrange("(c p) n -> p c n", p=128), w=[w2s])
        if is_k:
            cx.dma("sp", w2s[:, 1, :, 0:nw2], w2b_ap.rearrange("(c p) n -> p c n", p=128), w=[w2s])
        else:
            cx.op("dve", lambda e: e.memset(w2s[:, 1, :, :], 0.0), w=[w2s])
            cx.op("dve", lambda e: e.memset(w2s[:, 0, :, 64:128], 0.0), w=[w2s])
        cx.op("dve", lambda e: e.tensor_copy(out=w2[:, :, :, :], in_=w2s[:, :, :, :]), r=[w2s], w=[w2])
        zpe = cx.sb(st, "zpe", [128, 32, 256], BF16)
        zv = zT[:, :].rearrange("p (n s) -> p n s", s=16)
        for j in range(32):
            src = zv[:, 0:255, j] if j < 16 else zv[:, 1:256, j - 16]
            eng = "dve" if j % 2 == 0 else "pool"
            cx.op(eng, lambda e: e.tensor_scalar(out=zpe[:, j, 0:255], in0=src, scalar1=pe_c[:, j:j + 1],
                                                 scalar2=None, op0=ALU.add), r=[zT, pe_c], w=[zpe])
        H = [[cx.ps(st, f"H{hc}{g}", [128, 512]) for g in range(2)] for hc in range(2)]
        gel = cx.sb(st, "gel", [128, 2, 2, 256], BF16)
        cx.op("dve", lambda e: e.memset(gel[:, :, :, :], 0.0), w=[gel])
        t1 = cx.sb(st, "t1", [128, 256], F32)
        t2 = cx.sb(st, "t2", [128, 256], F32)
        for g in range(2):
            pr = slice(g * 64, (g + 1) * 64)
            for hc in range(2):
                for j in range(32):
                    cx.op("pe", lambda e: e.matmul(H[hc][g][:, 0:255], lhsT=w1[pr, j, hc * 128:(hc + 1) * 128],
                                                   rhs=zpe[pr, j, 0:255], start=(j == 0), stop=(j == 31)),
                          r=[w1, zpe], w=[H[hc][g]], rows=(g * 64, 64))
                x = H[hc][g]
                cx.op("act", lambda e: e.activation(out=t1[:, 0:255], in_=x[:, 0:255], func=AF.Square),
                      r=[x], w=[t1])
                cx.op("dve", lambda e: e.tensor_scalar(out=t1[:, 0:255], in0=t1[:, 0:255], scalar1=0.044715,
                                                       scalar2=1.0, op0=ALU.mult, op1=ALU.add), r=[t1], w=[t1])
                cx.op("dve", lambda e: e.tensor_tensor(out=t1[:, 0:255], in0=x[:, 0:255], in1=t1[:, 0:255],
                                                       op=ALU.mult), r=[x, t1], w=[t1])
                cx.op("act", lambda e: e.activation(out=t2[:, 0:255], in_=t1[:, 0:255], func=AF.Tanh,
                                                    scale=0.7978845608028654), r=[t1], w=[t2])
                cx.op("dve", lambda e: e.tensor_scalar(out=t2[:, 0:255], in0=t2[:, 0:255], scalar1=0.5,
                                                       scalar2=0.5, op0=ALU.mult, op1=ALU.add), r=[t2], w=[t2])
                cx.op("dve", lambda e: e.tensor_tensor(out=gel[:, hc, g, 0:255], in0=x[:, 0:255],
                                                       in1=t2[:, 0:255], op=ALU.mult), r=[x, t2], w=[gel])
        if is_k:
            ccos = cx.sb(st, "ccos", [128, 256], F32)
            csin = cx.sb(st, "csin", [128, 256], F32)
            cx.dma("sp", ccos[:, :], consts["ccos"].t, r=[consts["ccos"]], w=[ccos])
            cx.dma("sp", csin[:, :], consts["csin"].t, r=[consts["csin"]], w=[csin])
            for g in range(2):
                A, Bp = H[0][g], H[1][g]
                for var, dst in ((0, A), (1, Bp)):
                    for hc in range(2):
                        cx.op("pe", lambda e: e.matmul(dst[:, 0:255], lhsT=w2[:, var, hc, :],
                                                       rhs=gel[:, hc, g, 0:255], start=(hc == 0),
                                                       stop=(hc == 1)), r=[w2, gel], w=[dst])
                kc = cx.sb(st, f"kc{g}", [128, 256], BF16)
                cx.op("dve", lambda e: e.memset(kc[:, :], 0.0), w=[kc])
                cx.op("dve", lambda e: e.tensor_tensor(out=t1[:, 0:255], in0=A[:, 0:255], in1=ccos[:, 0:255],
                                                       op=ALU.mult), r=[A, ccos], w=[t1])
                cx.op("dve", lambda e: e.tensor_tensor(out=t2[:, 0:255], in0=Bp[:, 0:255], in1=csin[:, 0:255],
                                                       op=ALU.mult), r=[Bp, csin], w=[t2])
                cx.op("pool", lambda e: e.tensor_tensor(out=kc[:, 0:255], in0=t1[:, 0:255], in1=t2[:, 0:255],
                                                        op=ALU.add), r=[t1, t2], w=[kc])
                cx.dma("pool", out_d[g, :, :], kc[:, :], r=[kc], w=[out_d])
        else:
            for g in range(2):
                for nt in range(2):
                    m = 128 if nt == 0 else 127
                    C = H[nt][g]
                    for hc in range(2):
                        cx.op("pe", lambda e: e.matmul(C[0:m, 0:64], lhsT=gel[:, hc, g, nt * 128:nt * 128 + m],
                                                       rhs=w2[:, 0, hc, 0:64], start=(hc == 0), stop=(hc == 1)),
                              r=[w2, gel], w=[C])
                    vc = cx.sb(st, f"vc{g}{nt}", [128, 64], BF16)
                    cx.op("dve", lambda e: e.memset(vc[:, :], 0.0), w=[vc])
                    cx.op("act", lambda e: e.copy(out=vc[0:m, :], in_=C[0:m, 0:64]), r=[C], w=[vc])
                    cx.dma("pool", out_d[g, nt * 128:(nt + 1) * 128, :], vc[:, :], r=[vc], w=[out_d])
    cx.barrier()


def _store_gated(cx, O0, O1, E, gates, gcol, j, stg, dst_d, h, extra=None):
    fs = stg["fs"]
    ob = stg["ob"][stg["k"][0] % 2]
    stg["k"][0] += 1
    for bi, O in enumerate((O0, O1)):
        it0 = 4 * j + 2 * bi
        f0, f1, f2 = fs[:, bi, 0:2], fs[:, bi, 2:4], fs[:, bi, 4:6]
        cx.op("dve", lambda e: e.tensor_scalar(out=f0, in0=O[:, :, E], scalar1=1e-30, scalar2=None,
                                               op0=ALU.max), r=[O], w=[fs])
        cx.op("dve", lambda e: e.reciprocal(out=f1, in_=f0), r=[fs], w=[fs])
        cx.op("dve", lambda e: e.tensor_tensor(out=f2, in0=f1, in1=gates[:, it0:it0 + 2, gcol], op=ALU.mult),
              r=[fs, gates], w=[fs])
        cx.op("dve", lambda e: e.tensor_tensor(out=ob[:, 2 * bi:2 * bi + 2, :], in0=O[:, :, 0:E],
                                               in1=f2.unsqueeze(2).to_broadcast([128, 2, E]), op=ALU.mult),
              r=[O, fs], w=[ob])
        if extra is not None:
            extra(bi, O, f1)
    cx.dma("pool", dst_d[j * 512:(j + 1) * 512, h * 64:(h + 1) * 64].rearrange("(s p) e -> p s e", p=128),
           ob[:, :, :], r=[ob], w=[dst_d])


def nsa_cmp_stage(cx, qTn, kcmp_d, vcmp_d, gates_d, ocmp_d, addm_d, ident, consts):
    nb = T // 512
    with ExitStack() as st:
        gates = cx.sb(st, "gates", [128, 32, 24], F32)
        cx.dma("sp", gates[:, :, :], gates_d[:, :].rearrange("(i p) c -> p i c", p=128), r=[gates_d], w=[gates])
        cmask = cx.sb(st, "cmpmask", [128, 2, 8, 512], BF16)
        cx.dma("sp", cmask[:, 0, :, :], consts["cmpmask"][:, 0, :, :], r=[consts["cmpmask"]], w=[cmask])
        cx.dma("sp", cmask[:, 1, :, :], consts["cmpmask"][:, 1, :, :], r=[consts["cmpmask"]], w=[cmask])
        stg = dict(fs=cx.sb(st, "fs", [128, 2, 8], F32),
                   ob=[cx.sb(st, f"ob{i}", [128, 4, 64], F32) for i in range(2)], k=[0])
        qs = [cx.sb(st, f"q{i}", [128, T], BF16) for i in range(2)]
        kc = cx.sb(st, "kc", [128, 256], BF16)
        va = cx.sb(st, "va", [128, 2, 129], BF16)
        impacc = cx.sb(st, "impacc", [128, 4, 64], F32)
        imptmp = cx.sb(st, "imptmp", [128, 2, 64], F32)
        selb = [cx.sb(st, f"selb{i}", [128, 4, 64], F32) for i in range(2)]
        sc = cx.sb(st, "sc", [128, 64], F32)
        sc2 = cx.sb(st, "sc2", [128, 64], F32)
        m8 = cx.sb(st, "m8", [128, 16], F32)
        am = cx.sb(st, "am", [128, 128], BF16)
        amT = [cx.sb(st, f"amT{i}", [128, 512], BF16) for i in range(2)]
        tpf = cx.ps(st, "tpf", [128, 512], BF16)
        for g in range(2):
            cx.dma("sp", kc[:, :], kcmp_d[g, :, :], r=[kcmp_d], w=[kc])
            cx.op("dve", lambda e: e.memset(va[:, :, 64:65], 1.0), w=[va])
            cx.dma("sp", va[:, :, 0:64], vcmp_d[g, :, :].rearrange("(i p) e -> p i e", p=128), r=[vcmp_d], w=[va])
            cx.dma("sp", va[:, :, 65:129], consts["selmap"][:, :].rearrange("(i p) e -> p i e", p=128),
                   r=[consts["selmap"]], w=[va])
            for c in range(2):
                cx.dma("sp", qs[c][:, :], qTn[2 * g + c, :, :], r=[qTn], w=[qs[c]])
            units = []
            for j in range(nb):
                for hg in range(4):
                    h = 4 * g + hg
                    pr = slice((h % 2) * 64, (h % 2) * 64 + 64)
                    q_sb = qs[hg // 2]
                    tiles = []
                    for nt in range(2):
                        if nt == 1 and j < 4:
                            continue
                        tiles.append(dict(k=kc[pr, nt * 128:(nt + 1) * 128], v=va[:, nt, :], r=[kc, va],
                                          mask=(cmask[:, nt, j, :], cmask), subs=[True] * 4))

                    def fin(O0, O1, j=j, hg=hg, h=h, g=g):
                        if hg == 0:
                            sb_ = selb[j % 2]
                            cx.dma("sp", sb_[:, :, :],
                                   consts["selbias"][j * 512:(j + 1) * 512, :].rearrange("(s p) b -> p s b", p=128),
                                   r=[consts["selbias"]], w=[sb_])

                        def extra(bi, O, f1):
                            dstv = impacc[:, 2 * bi:2 * bi + 2, :]
                            rb = f1.unsqueeze(2).to_broadcast([128, 2, 64])
                            if hg == 0:
                                cx.op("dve", lambda e: e.tensor_tensor(out=dstv, in0=O[:, :, 65:129], in1=rb,
                                                                       op=ALU.mult), r=[O, stg["fs"]], w=[impacc])
                            else:
                                cx.op("dve", lambda e: e.tensor_tensor(out=imptmp[:, :, :], in0=O[:, :, 65:129],
                                                                       in1=rb, op=ALU.mult),
                                      r=[O, stg["fs"]], w=[imptmp])
                                cx.op("pool", lambda e: e.tensor_tensor(out=dstv, in0=dstv, in1=imptmp[:, :, :],
                                                                        op=ALU.add), r=[impacc, imptmp], w=[impacc])
                        _store_gated(cx, O0, O1, 64, gates, h * 3 + 0, j, stg, ocmp_d, h, extra=extra)
                        if hg == 3:
                            sb_ = selb[j % 2]
                            aT = amT[j % 2]
                            for s in range(4):
                                cx.op("dve", lambda e: e.tensor_tensor(out=sc[:, :], in0=impacc[:, s, :],
                                                                       in1=sb_[:, s, :], op=ALU.add),
                                      r=[impacc, sb_], w=[sc])
                                cx.op("dve", lambda e: e.max(out=m8[:, 0:8], in_=sc[:, :]), r=[sc], w=[m8])
                                cx.op("dve", lambda e: e.match_replace(out=sc2[:, :], in_to_replace=m8[:, 0:8],
                                                                       in_values=sc[:, :], imm_value=-3e38),
                                      r=[sc, m8], w=[sc2])
                                cx.op("dve", lambda e: e.max(out=m8[:, 8:16], in_=sc2[:, :]), r=[sc2], w=[m8])
                                cx.op("dve", lambda e: e.tensor_scalar(out=sc2[:, :], in0=sc[:, :],
                                                                       scalar1=m8[:, 15:16], scalar2=30000.0,
                                                                       op0=ALU.is_ge, op1=ALU.mult),
                                      r=[sc, m8], w=[sc2])
                                for dpl in range(2):
                                    cx.op("dve", lambda e: e.tensor_scalar(
                                        out=am[:, dpl * 64:(dpl + 1) * 64], in0=sc2[:, :], scalar1=-30000.0,
                                        scalar2=None, op0=ALU.add), r=[sc2], w=[am])
                                cx.op("pe", lambda e: e.transpose(out=tpf[:, s * 128:(s + 1) * 128], in_=am[:, :],
                                                                  identity=ident[:, :]), r=[am, ident], w=[tpf])
                            cx.op("act", lambda e: e.copy(out=aT[:, :], in_=tpf[:, :]), r=[tpf], w=[aT])
                            cx.dma("pool", addm_d[g, :, j * 512:(j + 1) * 512], aT[:, :], r=[aT], w=[addm_d])
                    units.append(dict(q=q_sb[pr, j * 512:(j + 1) * 512], qr=[q_sb], tiles=tiles, fin=fin))
            attn_run_shared(cx, st, units, 129, "exp", scale=0.125)
    cx.barrier()


def nsa_kv_stage(cx, qTn, kT_d, v_d, gates_d, out_d, kind, consts, addm_d=None):
    nb = T // 512
    br = 1 if kind == "slc" else 2
    with ExitStack() as st:
        gates = cx.sb(st, "gates", [128, 32, 24], F32)
        cx.dma("sp", gates[:, :, :], gates_d[:, :].rearrange("(i p) c -> p i c", p=128), r=[gates_d], w=[gates])
        masks = cx.sb(st, "masks", [128, 4, 512], BF16)
        cx.dma("sp", masks[:, :, :], consts["cmask"].t, r=[consts["cmask"]], w=[masks])
        if kind == "win":
            wmasks = cx.sb(st, "wmasks", [128, 4, 512], BF16)
            cx.dma("sp", wmasks[:, :, :], consts["wmask"].t, r=[consts["wmask"]], w=[wmasks])
        else:
            eexp = cx.sb(st, "eexp", [128, T], BF16)
            cx.dma("sp", eexp[:, :], consts["eexp"].t, r=[consts["eexp"]], w=[eexp])
            addm = cx.sb(st, "addm", [128, T], BF16)
        stg = dict(fs=cx.sb(st, "fs", [128, 2, 8], F32),
                   ob=[cx.sb(st, f"ob{i}", [128, 4, 64], F32) for i in range(2)], k=[0])
        qs = [cx.sb(st, f"q{i}", [128, T], BF16) for i in range(2)]
        kT = cx.sb(st, "kT", [128, T], BF16)
        va = cx.sb(st, "va", [128, 32, 65], BF16)
        cx.op("dve", lambda e: e.memset(va[:, :, 64:65], 1.0), w=[va])
        for g in range(2):
            cx.dma("sp", kT[:, :], kT_d[g, :, :], r=[kT_d], w=[kT])
            cx.dma("sp", va[:, :, 0:64], v_d[:, g * 64:(g + 1) * 64].rearrange("(i p) e -> p i e", p=128),
                   r=[v_d], w=[va])
            if kind == "slc":
                cx.dma("sp", addm[:, :], addm_d[g, :, :], r=[addm_d], w=[addm])
            for c in range(2):
                cx.dma("sp", qs[c][:, :], qTn[2 * g + c, :, :], r=[qTn], w=[qs[c]])
            units = []
            for hg in range(4):
                h = 4 * g + hg
                pr = slice((h % 2) * 64, (h % 2) * 64 + 64)
                q_sb = qs[hg // 2]
                for j in range(nb):
                    tiles = []
                    if kind == "slc":
                        for (i, mid, subs) in causal_tiles(j):
                            tiles.append(dict(
                                k=kT[pr, i * 128:(i + 1) * 128], v=va[:, i, :], r=[kT, va],
                                mask=(masks[:, mid, :], masks) if mid is not None else None,
                                add=(eexp[pr, i * 128:(i + 1) * 128], addm[pr, j * 512:(j + 1) * 512],
                                     [eexp, addm]), subs=subs))
                    else:
                        for i in range(max(0, 4 * j - 4), 4 * j + 4):
                            dlt = i - 4 * j
                            if dlt >= 0:
                                mk, subs = masks[:, dlt, :], [s >= dlt for s in range(4)]
                                mb = masks
                            else:
                                mk, subs = wmasks[:, dlt + 4, :], [s <= dlt + 4 for s in range(4)]
                                mb = wmasks
                            tiles.append(dict(k=kT[pr, i * 128:(i + 1) * 128], v=va[:, i, :], r=[kT, va],
                                              mask=(mk, mb), subs=subs))

                    def fin(O0, O1, j=j, h=h):
                        _store_gated(cx, O0, O1, 64, gates, h * 3 + br, j, stg, out_d, h)
                    units.append(dict(q=q_sb[pr, j * 512:(j + 1) * 512], qr=[q_sb], tiles=tiles, fin=fin,
                                      rows=(pr.start, 64)))
            attn_run_shared(cx, st, units, 65, "exp", scale=0.125)
    cx.barrier()


def nsa_combine_stage(cx, parts, oTd, ident):
    with ExitStack() as st:
        acc = [cx.sb(st, f"acc{i}", [128, 4, 512], F32) for i in range(2)]
        tmp = [cx.sb(st, f"tmp{i}", [128, 4, 512], F32) for i in range(2)]
        ab = cx.sb(st, "ab", [128, 4, 512], BF16)
        oTs = [cx.sb(st, f"oTs{i}", [128, 4, 512], BF16) for i in range(2)]
        tpf = [cx.ps(st, f"tpf{i}", [128, 1024], BF16) for i in range(2)]
        for b in range(T // 512):
            a = acc[b % 2]
            tb = b * 512
            cx.dma("sp", a[:, :, :], parts[0][tb:tb + 512, :].rearrange("(s p) e -> p s e", p=128),
                   r=[parts[0]], w=[a])
            for k in (1, 2):
                t = tmp[k % 2]
                cx.dma("sp", t[:, :, :], parts[k][tb:tb + 512, :].rearrange("(s p) e -> p s e", p=128),
                       r=[parts[k]], w=[t])
                eng = "dve" if k == 1 else "pool"
                cx.op(eng, lambda e: e.tensor_tensor(out=a[:, :, :], in0=a[:, :, :], in1=t[:, :, :], op=ALU.add),
                      r=[a, t], w=[a])
            cx.op("act", lambda e: e.copy(out=ab[:, :, :], in_=a[:, :, :]), r=[a], w=[ab])
            o = oTs[b % 2]
            for c in range(4):
                tp = tpf[c % 2]
                for s in range(4):
                    cx.op("pe", lambda e: e.transpose(out=tp[:, s * 128:(s + 1) * 128],
                                                      in_=ab[:, s, c * 128:(c + 1) * 128], identity=ident[:, :]),
                          r=[ab, ident], w=[tp])
                cx.op("act", lambda e: e.copy(out=o[:, c, :], in_=tp[:, 0:512]), r=[tp], w=[o])
            for c in range(4):
                cx.dma("pool", oTd[c, :, tb:tb + 512], o[:, c, :], r=[o], w=[oTd])
    cx.barrier()


def final_norm_stage(cx, x_in, w_ap, out_d):
    with ExitStack() as st:
        wrow = bcast_load(cx, st, "fnw", w_ap, D)
        xts = [cx.sb(st, f"xt{i}", [128, D], F32) for i in range(3)]
        junk = cx.sb(st, "junk", [128, D], BF16)
        sss = [cx.sb(st, f"ss{i}", [128, 4], F32) for i in range(2)]
        for s in sss:
            cx.op("dve", lambda e, s=s: e.memset(s[:, :], EPS), w=[s])
        for i in range(T // 128):
            xt = xts[i % 3]
            ss = sss[i % 2]
            cx.dma("sp", xt[:, :], x_in[i * 128:(i + 1) * 128, :], r=[x_in], w=[xt])
            cx.op("act", lambda e: e.activation(out=junk[:, :], in_=xt[:, :], func=AF.Square,
                                                accum_out=ss[:, 0:1]), r=[xt], w=[junk, ss])
            cx.op("act", lambda e: e.activation(out=ss[:, 1:2], in_=ss[:, 0:1], func=AF.Sqrt,
                                                scale=1.0 / D, bias=ss[:, 3:4]), r=[ss], w=[ss])
            cx.op("dve", lambda e: e.reciprocal(out=ss[:, 2:3], in_=ss[:, 1:2]), r=[ss], w=[ss])
            cx.op("dve", lambda e: e.scalar_tensor_tensor(out=xt[:, :], in0=xt[:, :], scalar=ss[:, 2:3],
                                                          in1=wrow[:, :], op0=ALU.mult, op1=ALU.mult),
                  r=[xt, ss, wrow], w=[xt])
            cx.dma("pool", out_d[i * 128:(i + 1) * 128, :], xt[:, :], r=[xt], w=[out_d])
    cx.barrier()


IN_SPECS = {
    "x": [T, D], "ffa_norm": [2, D], "ffa_gate": [2, D, DFF], "ffa_up": [2, D, DFF], "ffa_down": [2, DFF, D],
    "mix_norm": [2, D], "ffb_norm": [2, D], "ffb_gate": [2, D, DFF], "ffb_up": [2, D, DFF],
    "ffb_down": [2, DFF, D], "ab_w_in": [D, 3328], "ab_w_rot": [D, 1024], "ab_w_out": [D, D],
    "diff_lam": [256], "diff_subln": [128], "rwkv_mu": [1792], "rwkv_w0": [512], "rwkv_w2": [64, 512],
    "rwkv_a0": [512], "rwkv_a2": [64, 512], "rwkv_g2": [128, 512], "rwkv_kk": [512], "rwkv_ka": [512],
    "rwkv_rk": [512], "rwkv_lnw": [512], "rwkv_lnb": [512], "cd_w_in": [D, 2848], "cd_w_ext": [D, 1536],
    "cd_w_out": [D, D], "nsa_pe_kT": [128, 32], "nsa_w1_k": [2048, 256], "nsa_w2_k_dup": [256, 128],
    "nsa_w2_k_rot": [256, 128], "nsa_pe_vT": [128, 32], "nsa_w1_v": [2048, 256], "nsa_w2_v": [256, 64],
    "mlstm_conv_w": [4, 512], "mlstm_conv_b": [512], "mlstm_ig_b": [4], "mlstm_fg_b": [4],
    "mlstm_norm": [512], "final_norm": [D],
}
CONST_SPECS = {
    "c_cos": ([128, T], F32), "c_sin": ([128, T], F32), "c_cmask": ([128, 4, 512], BF16),
    "c_wmask": ([128, 4, 512], BF16), "c_ccos": ([128, 256], F32), "c_csin": ([128, 256], F32),
    "c_cmpmask": ([128, 2, 8, 512], BF16), "c_selmap": ([256, 64], BF16), "c_selbias": ([T, 64], F32),
    "c_eexp": ([128, T], BF16),
}
LAM_INIT0 = 0.8 - 0.6 * float(np.exp(-0.3 * 0))


def build_program(stop_after=None, only=None):
    nc = bass.Bass("TRN2", target_bir_lowering=False)
    with ExitStack() as es:
        cx = Ctx(nc, es)
        I = {k: cx.dram(k, v, F32, kind="ExternalInput") for k, v in IN_SPECS.items()}
        consts = {k[2:]: cx.dram(k, sh, dt, kind="ExternalInput") for k, (sh, dt) in CONST_SPECS.items()}
        out = cx.dram("out", [T, D], F32, kind="ExternalOutput")
        xA = cx.dram("xA", [T, D], F32)
        xB = cx.dram("xB", [T, D], F32)
        ident, identf = make_ident(cx, es)

        def done(x_cur):
            if only is None or "final" in only:
                final_norm_stage(cx, x_cur, I["final_norm"].t, out)
            print("nins", cx.nins)
            return nc

        cnt = {}

        def en(name):
            cnt[name] = cnt.get(name, 0) + 1
            full = f"{name}{cnt[name]}"
            return only is None or full in only

        if en("ffn"):
          ffn_stage(cx, I["x"], xA, I["ffa_norm"].t[0], I["ffa_gate"].t[0], I["ffa_up"].t[0],
                  I["ffa_down"].t[0], ident)
        qTa = cx.dram("qTa", [4, 128, T], BF16)
        kTa = cx.dram("kTa", [4, 128, T], BF16)
        Va = cx.dram("Va", [T, 512], BF16)
        pTb = cx.dram("pTb", [14, 128, T], F32)
        oTab = cx.dram("oTab", [8, 128, T], BF16)
        specs = []
        for h in range(4):
            specs.append(dict(kind="rope", a=h * 128, b=3328 + h * 128, dst=qTa.t[h]))
            specs.append(dict(kind="rope", a=512 + h * 128, b=3328 + 512 + h * 128, dst=kTa.t[h]))
        specs.append(dict(kind="tm", a=1024, n=512, dst=Va.t))
        for c in range(14):
            specs.append(dict(kind="raw", a=1536 + c * 128, n=128, dst=pTb.t[c]))
        if en("proj"):
          proj_stage(cx, xA, I["mix_norm"].t[0], [(I["ab_w_in"].t, 3328), (I["ab_w_rot"].t, 1024)], specs,
                   ident, [qTa, kTa, Va, pTb], consts)
        if en("diffattn"):
          diffattn_stage(cx, qTa, kTa, Va, I["diff_lam"].t, I["diff_subln"].t, oTab, ident, consts, LAM_INIT0)
        prm = {n: I["rwkv_" + n].t for n in ("mu", "w0", "w2", "a0", "a2", "g2", "kk", "ka", "rk", "lnw", "lnb")}
        if en("rwkv"):
          rwkv_stage(cx, pTb, prm, oTab, ident, identf)
        if en("outproj"):
          outproj_stage(cx, oTab, I["ab_w_out"].t, xA, xB)
        if en("ffn"):
          ffn_stage(cx, xB, xA, I["ffb_norm"].t[0], I["ffb_gate"].t[0], I["ffb_up"].t[0],
                  I["ffb_down"].t[0], ident)
        if stop_after == "layer0":
            return done(xA)
        if en("ffn"):
          ffn_stage(cx, xA, xB, I["ffa_norm"].t[1], I["ffa_gate"].t[1], I["ffa_up"].t[1],
                  I["ffa_down"].t[1], ident)
        qTn = cx.dram("qTn", [4, 128, T], BF16)
        ksT = cx.dram("ksT", [2, 128, T], BF16)
        kwT = cx.dram("kwT", [2, 128, T], BF16)
        kcT = cx.dram("kcT", [128, T], F32)
        vcT = cx.dram("vcT", [128, T], F32)
        vs_d = cx.dram("vs_d", [T, 128], BF16)
        vw_d = cx.dram("vw_d", [T, 128], BF16)
        gates_d = cx.dram("gates_d", [T, 24], F32)
        dqkT = cx.dram("dqkT", [4, 128, T], F32)
        dv_d = cx.dram("dv_d", [T, 512], BF16)
        di_d = cx.dram("di_d", [4, T], F32)
        df_d = cx.dram("df_d", [4, T], F32)
        og_d = cx.dram("og_d", [T, 512], F32)
        oTcd = cx.dram("oTcd", [8, 128, T], BF16)
        X0 = 2848
        specs = []
        for c in range(4):
            specs.append(dict(kind="rope", a=c * 128, b=X0 + c * 128, dst=qTn.t[c]))
        for g in range(2):
            specs.append(dict(kind="rope", a=X0 + 512 + g * 128, b=X0 + 768 + g * 128, dst=ksT.t[g]))
            specs.append(dict(kind="rope", a=X0 + 1024 + g * 128, b=X0 + 1280 + g * 128, dst=kwT.t[g]))
        specs.append(dict(kind="raw", a=512, n=128, dst=kcT.t))
        specs.append(dict(kind="raw", a=640, n=128, dst=vcT.t))
        specs.append(dict(kind="tm", a=896, n=128, dst=vs_d.t))
        specs.append(dict(kind="tm", a=1152, n=128, dst=vw_d.t))
        specs.append(dict(kind="tm", a=1280, n=24, dst=gates_d.t, act="sigmoid"))
        for c in range(4):
            specs.append(dict(kind="raw", a=1304 + c * 128, n=128, dst=dqkT.t[c]))
        specs.append(dict(kind="tm", a=1816, n=512, dst=dv_d.t))
        specs.append(dict(kind="raw", a=2328, n=4, dst=di_d.t))
        specs.append(dict(kind="raw", a=2332, n=4, dst=df_d.t))
        specs.append(dict(kind="tm", a=2336, n=512, dst=og_d.t, act="sigmoid"))
        if en("proj"):
          proj_stage(cx, xB, I["mix_norm"].t[1], [(I["cd_w_in"].t, 2848), (I["cd_w_ext"].t, 1536)], specs,
                   ident, [qTn, ksT, kwT, kcT, vcT, vs_d, vw_d, gates_d, dqkT, dv_d, di_d, df_d, og_d], consts)
        kcmp_d = cx.dram("kcmp_d", [2, 128, 256], BF16)
        vcmp_d = cx.dram("vcmp_d", [2, 256, 64], BF16)
        addm_d = cx.dram("addm_d", [2, 128, T], BF16)
        ocmp_d = cx.dram("ocmp_d", [T, 512], F32)
        oslc_d = cx.dram("oslc_d", [T, 512], F32)
        owin_d = cx.dram("owin_d", [T, 512], F32)
        Bscr = cx.dram("Bscr", [4, T], F32)
        if en("nsa_compress"):
          nsa_compress_stage(cx, kcT, I["nsa_pe_kT"].t, I["nsa_w1_k"].t, I["nsa_w2_k_dup"].t,
                           I["nsa_w2_k_rot"].t, True, kcmp_d, consts)
        if en("nsa_compress"):
          nsa_compress_stage(cx, vcT, I["nsa_pe_vT"].t, I["nsa_w1_v"].t, I["nsa_w2_v"].t, None, False,
                           vcmp_d, consts)
        if en("nsa_cmp"):
          nsa_cmp_stage(cx, qTn, kcmp_d, vcmp_d, gates_d, ocmp_d, addm_d, ident, consts)
        if en("nsa_kv"):
          nsa_kv_stage(cx, qTn, ksT, vs_d, gates_d, oslc_d, "slc", consts, addm_d=addm_d)
        if en("nsa_kv"):
          nsa_kv_stage(cx, qTn, kwT, vw_d, gates_d, owin_d, "win", consts)
        if en("nsa_combine"):
          nsa_combine_stage(cx, [ocmp_d, oslc_d, owin_d], oTcd, ident)
        mprm = {n: I["mlstm_" + n].t for n in ("conv_w", "conv_b", "ig_b", "fg_b", "norm")}
        if en("mlstm"):
          mlstm_stage(cx, dqkT, di_d, df_d, dv_d, og_d, mprm, Bscr, oTcd, ident, identf, consts)
        if en("outproj"):
          outproj_stage(cx, oTcd, I["cd_w_out"].t, xB, xA)
        if en("ffn"):
          ffn_stage(cx, xA, xB, I["ffb_norm"].t[1], I["ffb_gate"].t[1], I["ffb_up"].t[1],
                  I["ffb_down"].t[1], ident)
        return done(xB)


def _dup(w):
    return np.concatenate([w, w], axis=1)


def host_layout(inp):
    f = lambda a: np.ascontiguousarray(np.asarray(a, dtype=np.float32))
    o = {}
    for k in ("ffa_norm", "ffa_gate", "ffa_up", "ffa_down", "mix_norm", "ffb_norm", "ffb_gate", "ffb_up",
              "ffb_down", "final_norm"):
        o[k] = f(inp[k])
    wab = f(inp["ab_w_in"][0])
    o["ab_w_in"] = wab
    o["ab_w_rot"] = rot_cols(wab[:, 0:1024])
    o["ab_w_out"] = f(inp["ab_w_out"][0])
    o["diff_lam"] = f(inp["diff_lam"][0]).reshape(256)
    o["diff_subln"] = f(inp["diff_subln"][0])
    for n in ("mu", "w0", "w2", "a0", "a2", "g2", "kk", "ka", "lnw", "lnb"):
        o["rwkv_" + n] = f(inp["rwkv_" + n][0])
    o["rwkv_rk"] = f(inp["rwkv_rk"][0]).reshape(512)
    wcd = f(inp["cd_w_in"][0])
    o["cd_w_in"] = wcd
    ks = [_dup(wcd[:, 768 + g * 64:768 + (g + 1) * 64]) for g in range(2)]
    kw = [_dup(wcd[:, 1024 + g * 64:1024 + (g + 1) * 64]) for g in range(2)]
    ext = [rot_cols(wcd[:, 0:512])] + ks + [rot_cols(k) for k in ks] + kw + [rot_cols(k) for k in kw]
    o["cd_w_ext"] = np.ascontiguousarray(np.concatenate(ext, axis=1))
    o["cd_w_out"] = f(inp["cd_w_out"][0])
    o["nsa_pe_kT"] = np.ascontiguousarray(_dup(f(inp["nsa_pe_k"][0])).T)
    o["nsa_pe_vT"] = np.ascontiguousarray(_dup(f(inp["nsa_pe_v"][0])).T)
    o["nsa_w1_k"] = f(inp["nsa_w1_k"][0])
    o["nsa_w1_v"] = f(inp["nsa_w1_v"][0])
    w2k = f(inp["nsa_w2_k"][0])
    o["nsa_w2_k_dup"] = np.ascontiguousarray(_dup(w2k))
    o["nsa_w2_k_rot"] = np.ascontiguousarray(_dup(rot_cols(w2k)))
    o["nsa_w2_v"] = f(inp["nsa_w2_v"][0])
    for n in ("conv_w", "conv_b", "ig_b", "fg_b", "norm"):
        o["mlstm_" + n] = f(inp["mlstm_" + n][0])
    return o


def host_consts_full():
    c = host_consts(T)
    bf = ml_dtypes.bfloat16
    key = np.arange(128)[:, None, None]
    q = np.arange(512)[None, None, :]
    dl = np.arange(4)[None, :, None]
    c["c_wmask"] = ((128 * dl + key) > q).astype(np.float32).astype(bf)
    inv = 10000.0 ** (-np.arange(0, 64, 2, dtype=np.float64) / 64)
    d = np.arange(128) % 64
    cpos = (16.0 * np.arange(256) + 31.0)
    ang = cpos[None, :] * inv[d % 32][:, None]
    sgn = np.where(d < 32, -1.0, 1.0)[:, None]
    c["c_ccos"] = np.cos(ang).astype(np.float32)
    c["c_csin"] = (np.sin(ang) * sgn).astype(np.float32)
    n = np.arange(128)[:, None, None, None] + 128 * np.arange(2)[None, :, None, None]
    t = 512 * np.arange(8)[None, None, :, None] + np.arange(512)[None, None, None, :]
    c["c_cmpmask"] = ((16 * n + 31) <= t).astype(np.float32).astype(bf)
    c0 = 16 * np.arange(256)[:, None]
    s0 = 64 * np.arange(64)[None, :]
    shared = np.clip(np.minimum(c0 + 32, s0 + 64) - np.maximum(c0, s0), 0, None) / 32.0
    shared[255:, :] = 0.0
    c["c_selmap"] = shared.astype(np.float32).astype(bf)
    tt = np.arange(T)[:, None]
    blk = np.arange(64)[None, :]
    cur = tt // 64
    forced = (blk == 0) | (blk == cur) | (blk == cur - 1)
    valid = blk * 64 <= tt
    c["c_selbias"] = np.where(valid, np.where(forced, 1e4, 0.0), -1e30).astype(np.float32)
    c["c_eexp"] = (np.arange(64)[:, None] == (np.arange(T)[None, :] // 64)).astype(np.float32)
    c["c_eexp"] = np.concatenate([c["c_eexp"], c["c_eexp"]], 0).astype(bf)
    return c


_PROG = {}


def kernel(**inputs):
    if "nc" not in _PROG:
        _PROG["nc"] = build_program()
    nc = _PROG["nc"]
    shared = host_layout(inputs)
    shared.update(host_consts_full())
    x = np.asarray(inputs["x"], dtype=np.float32)
    in_maps = []
    for b in range(8):
        m = dict(shared)
        m["x"] = np.ascontiguousarray(x[b])
        in_maps.append(m)
    res = run_bass_kernel_spmd(nc, in_maps, core_ids=list(range(8)))
    return np.stack([np.asarray(r["out"], dtype=np.float32) for r in res.results], axis=0)
```

```python
from contextlib import ExitStack
import numpy as np
import ml_dtypes
import concourse.bass as bass
import concourse.mybir as mybir
from concourse.bass_utils import run_bass_kernel_spmd

F32 = mybir.dt.float32
BF16 = mybir.dt.bfloat16
ALU = mybir.AluOpType
AF = mybir.ActivationFunctionType
AX = mybir.AxisListType

T = 4096
D = 1024
DFF = 2816
NFC = DFF // 128
EPS = 1e-6


class Buf:
    __slots__ = ("t", "w", "r", "name")

    def __init__(self, t, name=""):
        self.t = t
        self.w = None
        self.r = {}
        self.name = name

    def __getitem__(self, k):
        return self.t[k]


class Ctx:
    COMPUTE = ("pe", "act", "dve", "pool")
    NDQ = 6

    def __init__(self, nc, es):
        self.nc = nc
        self.es = es
        self.engs = {"pe": nc.tensor, "act": nc.scalar, "dve": nc.vector,
                     "pool": nc.gpsimd, "sp": nc.sync}
        self.sem = {}
        self.cnt = {}
        for e in self.COMPUTE:
            self.sem[e] = es.enter_context(nc.semaphore("s_" + e))
            self.cnt[e] = 0
        self.dq = {}
        self.dq_cnt = {}
        self.dq_i = {}
        for q in ("sp", "pool", "act"):
            self.dq[q] = [es.enter_context(nc.semaphore(f"d_{q}{i}")) for i in range(self.NDQ)]
            self.dq_cnt[q] = [0] * self.NDQ
            self.dq_i[q] = 0
        self.known = {e: {} for e in self.engs}
        self.nins = 0

    def _nm(self, name):
        self.uid = getattr(self, "uid", 0) + 1
        return f"{name}_{self.uid}"

    def sb(self, st, name, shape, dt):
        name = self._nm("sb_" + name)
        return Buf(st.enter_context(self.nc.sbuf_tensor(name, list(shape), dt)), name)

    def ps(self, st, name, shape, dt=F32):
        name = self._nm("ps_" + name)
        return Buf(st.enter_context(self.nc.psum_tensor(name, list(shape), dt)), name)

    def dram(self, name, shape, dt, kind=None):
        if kind is None:
            t = self.nc.dram_tensor(name, list(shape), dt)
        else:
            t = self.nc.dram_tensor(name, list(shape), dt, kind=kind)
        return Buf(t.ap(), name)

    def _need(self, r, w):
        need = {}

        def add(tok):
            if tok is None:
                return
            s, v = tok
            cur = need.get(s.num)
            if cur is None or cur[1] < v:
                need[s.num] = (s, v)
        for b in r:
            add(b.w)
        for b in w:
            add(b.w)
            for tk in b.r.values():
                add(tk)
        return need

    def _emit_waits(self, e, need):
        eng = self.engs[e]
        kn = self.known[e]
        for num, (s, v) in need.items():
            if e == "pe" and s is self.sem["pe"]:
                continue
            if kn.get(num, 0) >= v:
                continue
            eng.wait_ge(s, v)
            kn[num] = v
            self.nins += 1

    def _record(self, tok, r, w):
        for b in r:
            b.r[tok[0].num] = tok
        for b in w:
            b.w = tok
            b.r = {}

    def op(self, e, fn, r=(), w=(), rows=(0, 128)):
        need = self._need(r, w)
        self._emit_waits(e, need)
        if e == "pe":
            lw = getattr(self, "_pe_lw", None)
            if lw is None:
                lw = self._pe_lw = {}
            wnums = [id(b) for b in w]
            if rows[1] >= 128:
                lw.clear()
            else:
                conflict = any(rg != rows and (rows[0] >= rg[0] + rg[1] or rg[0] >= rows[0] + rows[1])
                               and any(x in bs for x in wnums) for rg, bs in lw.items())
                if conflict and self.cnt["pe"] > 0:
                    if self.known["pe"].get(self.sem["pe"].num, 0) < self.cnt["pe"]:
                        self.engs["pe"].wait_ge(self.sem["pe"], self.cnt["pe"])
                        self.known["pe"][self.sem["pe"].num] = self.cnt["pe"]
                        self.nins += 1
                    lw.clear()
                lw.setdefault(rows, set()).update(wnums)
        ins = fn(self.engs[e])
        self.cnt[e] += 1
        tok = (self.sem[e], self.cnt[e])
        ins.then_inc(self.sem[e], 1)
        self.nins += 1
        self._record(tok, r, w)
        return tok

    def dma(self, q, out_ap, in_ap, r=(), w=(), **kw):
        ring = self.dq[q]
        i = self.dq_i[q]
        self.dq_i[q] = (i + 1) % len(ring)
        s = ring[i]
        prev = self.dq_cnt[q][i]
        need = self._need(r, w)
        if prev > 0:
            need[s.num] = (s, prev)
        self._emit_waits(q, need)
        ins = self.engs[q].dma_start(out=out_ap, in_=in_ap, **kw)
        ins.then_inc(s, 16)
        self.nins += 1
        self.dq_cnt[q][i] = prev + 16
        tok = (s, prev + 16)
        self._record(tok, r, w)
        return tok

    def barrier(self):
        toks = []
        for e in self.COMPUTE:
            if self.cnt[e] > 0:
                toks.append((self.sem[e], self.cnt[e]))
        for q in self.dq:
            for i, s in enumerate(self.dq[q]):
                if self.dq_cnt[q][i] > 0:
                    toks.append((s, self.dq_cnt[q][i]))
        for e in self.engs:
            need = {s.num: (s, v) for s, v in toks}
            kn = self.known[e]
            for num, (s, v) in need.items():
                if kn.get(num, 0) >= v:
                    continue
                self.engs[e].wait_ge(s, v)
                kn[num] = v
                self.nins += 1


def load_col_vec(cx, st, name, src_ap, n):
    c = n // 128
    b = cx.sb(st, name, [128, c], F32)
    cx.dma("sp", b[:, :], src_ap.rearrange("(c p) -> p c", p=128), w=[b],
           allow_slow_non_contiguous=True)
    return b


def load_weight_bf16(cx, st, wsb, w_dram_ap, kc, ncols, stage, scale_col=None, cast_eng="pool"):
    half = stage[0].t.shape[1]
    si = 0
    for c in range(kc):
        for c0 in range(0, ncols, half):
            cw = min(half, ncols - c0)
            sg = stage[si % len(stage)]
            si += 1
            cx.dma("sp", sg[:, :cw], w_dram_ap[c * 128:(c + 1) * 128, c0:c0 + cw], w=[sg])
            if scale_col is not None:
                cx.op(cast_eng, lambda e, sg=sg, c=c, c0=c0, cw=cw: e.tensor_scalar(
                    out=wsb[:, c, c0:c0 + cw], in0=sg[:, :cw], scalar1=scale_col[:, c:c + 1],
                    scalar2=None, op0=ALU.mult), r=[sg, scale_col], w=[wsb])
            else:
                cx.op(cast_eng, lambda e, sg=sg, c=c, c0=c0, cw=cw: e.tensor_copy(
                    out=wsb[:, c, c0:c0 + cw], in_=sg[:, :cw]), r=[sg], w=[wsb])


def norm_transpose_tile(cx, xt, hT, col0, ident, junk, ss, xn, tp, nw_b):
    cx.op("act", lambda e: e.activation(out=junk[:, :], in_=xt[:, :], func=AF.Square,
                                        accum_out=ss[:, 0:1]), r=[xt], w=[junk, ss])
    cx.op("act", lambda e: e.activation(out=ss[:, 1:2], in_=ss[:, 0:1], func=AF.Sqrt,
                                        scale=1.0 / D, bias=ss[:, 3:4]), r=[ss], w=[ss])
    cx.op("dve", lambda e: e.reciprocal(out=ss[:, 2:3], in_=ss[:, 1:2]), r=[ss], w=[ss])
    cx.op("dve", lambda e: e.scalar_tensor_tensor(out=xn[:, :], in0=xt[:, :], scalar=ss[:, 2:3],
                                                  in1=nw_b[:, :], op0=ALU.mult, op1=ALU.mult),
          r=[xt, ss, nw_b], w=[xn])
    for c in range(8):
        cx.op("pe", lambda e, c=c: e.transpose(out=tp[:, c * 128:(c + 1) * 128],
                                                in_=xn[:, c * 128:(c + 1) * 128],
                                                identity=ident[:, :]), r=[xn, ident], w=[tp])
    cx.op("act", lambda e: e.copy(out=hT[:, :, col0:col0 + 128],
                                  in_=tp[:, :].rearrange("p (c t) -> p c t", c=8)),
          r=[tp], w=[hT])


class WStream:
    def __init__(self, cx, stage, jobs):
        self.cx, self.stage, self.jobs, self.n = cx, stage, list(jobs), 0

    def emit(self, k=1):
        cx = self.cx
        for _ in range(k):
            if self.n >= len(self.jobs):
                return
            n = self.n
            self.n += 1
            dst, src, buf = self.jobs[n]
            sg = self.stage[n % len(self.stage)]
            shp = list(dst.shape)
            if len(shp) == 2:
                view = sg[:, 0:shp[1]]
            else:
                view = sg[:, 0:shp[1] * shp[2]].rearrange("p (c n) -> p c n", c=shp[1])
            cx.dma("sp" if n % 2 == 0 else "pool", view, src, w=[sg])
            if n % 2 == 0:
                cx.op("dve", lambda e: e.tensor_copy(out=dst, in_=view), r=[sg], w=[buf])
            else:
                cx.op("act", lambda e: e.copy(out=dst, in_=view), r=[sg], w=[buf])

    def rest(self):
        self.emit(len(self.jobs))


def stream_weight_chunks(cx, jobs, stage):
    WStream(cx, stage, jobs).rest()


def ffn_stage(cx, x_in, x_out, nw_ap, wg_ap, wu_ap, wd_ap, ident, ntok=T, final_w=None):
    with ExitStack() as st:
        wg = cx.sb(st, "wg", [128, 8, DFF], BF16)
        wu = cx.sb(st, "wu", [128, 8, DFF], BF16)
        wd = cx.sb(st, "wd", [128, NFC, D], BF16)
        wgp = [Buf(wg.t, f"wg{i}") for i in range(NFC // 2)]
        wup = [Buf(wu.t, f"wu{i}") for i in range(NFC // 2)]
        wdp = [Buf(wd.t, f"wd{i}") for i in range(NFC // 2)]
        wgf = [wgp[f // 2] for f in range(NFC)]
        wuf = [wup[f // 2] for f in range(NFC)]
        wdf = [wdp[f // 2] for f in range(NFC)]
        stage = [cx.sb(st, f"wstage{i}", [128, 2048], F32) for i in range(2)]
        nw_b = bcast_load(cx, st, "nw_b", nw_ap, D)
        xts = [cx.sb(st, f"xt{i}", [128, D], F32) for i in range(2)]
        xrs = [cx.sb(st, f"xr{i}", [128, D], F32) for i in range(1)]
        xn = cx.sb(st, "xn", [128, D], BF16)
        junk = xn
        sss = [cx.sb(st, f"ss{i}", [128, 4], F32) for i in range(2)]
        fss = cx.sb(st, "fss", [128, 4], F32)
        hTs = [cx.sb(st, f"hT{i}", [128, 8, 512], BF16) for i in range(2)]
        actb = cx.sb(st, "actb", [128, NFC, 512], BF16)
        sg_sb = [cx.sb(st, f"sg{i}", [128, 512], BF16) for i in range(2)]
        tp = cx.ps(st, "tp", [128, 1024], BF16)
        pg = [cx.ps(st, f"pg{i}", [128, 512]) for i in range(2)]
        pu = [cx.ps(st, f"pu{i}", [128, 512]) for i in range(2)]
        py = [cx.ps(st, f"py{i}", [128, 512]) for i in range(2)]
        for s in sss:
            cx.op("dve", lambda e, s=s: e.memset(s[:, :], EPS), w=[s])
        wgv = wg_ap.rearrange("(c p) n -> p c n", p=128)
        wuv = wu_ap.rearrange("(c p) n -> p c n", p=128)
        wdv = wd_ap.rearrange("(f p) n -> p f n", p=128)
        jobs = []
        for i in range(NFC // 2):
            cs = slice(i * 256, (i + 1) * 256)
            jobs.append((wg[:, :, cs], wgv[:, :, cs], wgp[i]))
            jobs.append((wu[:, :, cs], wuv[:, :, cs], wup[i]))
            if i >= 1:
                jobs.append((wd[:, 2 * (i - 1):2 * i, :], wdv[:, 2 * (i - 1):2 * i, :], wdp[i - 1]))
        jobs.append((wd[:, NFC - 2:NFC, :], wdv[:, NFC - 2:NFC, :], wdp[NFC // 2 - 1]))
        ws = WStream(cx, stage, jobs)

        def norm_tile(b, j):
            hT_ = hTs[b % 2]
            t0_ = b * 512 + j * 128
            k_ = b * 4 + j
            xt = xts[k_ % 2]
            ss = sss[k_ % 2]
            cx.dma("sp", xt[:, :], x_in[t0_:t0_ + 128, :], r=[x_in], w=[xt])
            norm_transpose_tile(cx, xt, hT_, j * 128, ident, junk, ss, xn, tp, nw_b)

        nblk = ntok // 512
        for b in range(nblk):
            hT = hTs[b % 2]
            if b == 0:
                for j in range(4):
                    norm_tile(0, j)
                ws.emit(2)
            for f in range(NFC):
                if b == 0 and f % 2 == 0:
                    ws.emit(3)
                g = pg[f % 2]
                u = pu[f % 2]
                sg = sg_sb[f % 2]
                for c in range(8):
                    cx.op("pe", lambda e, c=c, f=f, g=g: e.matmul(
                        g[:, :], lhsT=wg[:, c, f * 128:(f + 1) * 128], rhs=hT[:, c, :],
                        start=(c == 0), stop=(c == 7)), r=[wgf[f], hT], w=[g])
                for c in range(8):
                    cx.op("pe", lambda e, c=c, f=f, u=u: e.matmul(
                        u[:, :], lhsT=wu[:, c, f * 128:(f + 1) * 128], rhs=hT[:, c, :],
                        start=(c == 0), stop=(c == 7)), r=[wuf[f], hT], w=[u])
                cx.op("act", lambda e, g=g, sg=sg: e.activation(out=sg[:, :], in_=g[:, :],
                                                                 func=AF.Silu), r=[g], w=[sg])
                cx.op("dve", lambda e, f=f, u=u, sg=sg: e.tensor_tensor(
                    out=actb[:, f, :], in0=u[:, :], in1=sg[:, :], op=ALU.mult),
                    r=[u, sg], w=[actb])
                if b + 1 < nblk and f in (3, 8, 13, 18):
                    norm_tile(b + 1, (3, 8, 13, 18).index(f))
            if b == 0:
                ws.rest()
            for j in range(4):
                t0 = b * 512 + j * 128
                xr = xrs[0]
                cx.dma("sp", xr[:, :], x_in[t0:t0 + 128, :], r=[x_in], w=[xr])
                for h in range(2):
                    y = py[h]
                    for f in range(NFC):
                        cx.op("pe", lambda e, f=f, h=h, y=y, j=j: e.matmul(
                            y[:, :], lhsT=actb[:, f, j * 128:(j + 1) * 128],
                            rhs=wd[:, f, h * 512:(h + 1) * 512],
                            start=(f == 0), stop=(f == NFC - 1)), r=[actb, wdf[f]], w=[y])
                    cx.op("dve", lambda e, h=h, y=y, xr=xr: e.scalar_tensor_tensor(
                        out=xr[:, h * 512:(h + 1) * 512], in0=y[:, :], scalar=0.5,
                        in1=xr[:, h * 512:(h + 1) * 512], op0=ALU.mult, op1=ALU.add),
                        r=[y, xr], w=[xr])
                if final_w is not None:
                    fw = stage[0]
                    jk2 = stage[1]
                    if b == 0 and j == 0:
                        cx.dma("sp", fw[:, 0:D], final_w.partition_broadcast(128), w=[fw])
                        cx.op("dve", lambda e: e.memset(fss[:, :], EPS), w=[fss])
                    cx.op("act", lambda e: e.activation(out=jk2[:, 0:D], in_=xr[:, :], func=AF.Square,
                                                        accum_out=fss[:, 0:1]), r=[xr], w=[jk2, fss])
                    cx.op("act", lambda e: e.activation(out=fss[:, 1:2], in_=fss[:, 0:1], func=AF.Sqrt,
                                                        scale=1.0 / D, bias=fss[:, 3:4]), r=[fss], w=[fss])
                    cx.op("dve", lambda e: e.reciprocal(out=fss[:, 2:3], in_=fss[:, 1:2]), r=[fss], w=[fss])
                    cx.op("dve", lambda e: e.scalar_tensor_tensor(out=xr[:, :], in0=xr[:, :], scalar=fss[:, 2:3],
                                                                  in1=fw[:, 0:D], op0=ALU.mult, op1=ALU.mult),
                          r=[xr, fss, fw], w=[xr])
                    cx.dma("pool", x_out[t0:t0 + 128, :], xr[:, :], r=[xr], w=[x_out])
                else:
                    cx.dma("pool", x_out[t0:t0 + 128, :], xr[:, :], r=[xr], w=[x_out])
    cx.barrier()


def make_ident(cx, st):
    identf = cx.sb(st, "identf", [128, 128], F32)
    ident = cx.sb(st, "ident", [128, 128], BF16)
    cx.op("pool", lambda e: e.memset(identf[:, :], 1.0), w=[identf])
    cx.op("pool", lambda e: e.affine_select(out=identf[:, :], in_=identf[:, :],
                                            pattern=[[-1, 128]], compare_op=ALU.is_equal,
                                            fill=0.0, base=0, channel_multiplier=1),
          r=[identf], w=[identf])
    cx.op("dve", lambda e: e.tensor_copy(out=ident[:, :], in_=identf[:, :]), r=[identf], w=[ident])
    return ident, identf


def proj_stage(cx, x_in, nw_ap, wsrcs, specs, ident, dst_buf, consts, ntok=T):
    ncols = sum(n for _, n in wsrcs)
    with ExitStack() as st:
        wsb = cx.sb(st, "wp", [128, 8, ncols], BF16)
        stage = [cx.sb(st, f"wstage{i}", [128, 1024], F32) for i in range(4)]
        nw_b = bcast_load(cx, st, "nw_b", nw_ap, D)
        pieces = []
        c0 = 0
        for ap, n in wsrcs:
            apv = ap.rearrange("(c p) n -> p c n", p=128)
            for lo in range(0, n, 128):
                hi = min(n, lo + 128)
                pieces.append((c0 + lo, c0 + hi, Buf(wsb.t, f"wp{c0 + lo}"), apv[:, :, lo:hi]))
            c0 += n

        def wb(a, n):
            return [p[2] for p in pieces if p[0] < a + n and p[1] > a]
        order = []
        for sp in specs:
            rng = [(sp["a"], sp.get("n", 128))]
            if sp["kind"] == "rope":
                rng.append((sp["b"], 128))
            for (a, n) in rng:
                for p in pieces:
                    if p[0] < a + n and p[1] > a and p not in order:
                        order.append(p)
        for p in pieces:
            if p not in order:
                order.append(p)
        jobs = [(wsb[:, :, p[0]:p[1]], p[3], p[2]) for p in order]
        ws = WStream(cx, stage, jobs)

        def need(sp):
            rng = [(sp["a"], sp.get("n", 128))]
            if sp["kind"] == "rope":
                rng.append((sp["b"], 128))
            idx = 0
            for (a_, n_) in rng:
                for k_, p in enumerate(order):
                    if p[0] < a_ + n_ and p[1] > a_:
                        idx = max(idx, k_ + 1)
            return idx
        needs = [need(sp) for sp in specs]
        xts = [cx.sb(st, f"xt{i}", [128, D], F32) for i in range(2)]
        junk = cx.sb(st, "junk", [128, D], BF16)
        xn = cx.sb(st, "xn", [128, D], BF16)
        sss = [cx.sb(st, f"ss{i}", [128, 4], F32) for i in range(2)]
        hTs = [cx.sb(st, f"hT{i}", [128, 8, 512], BF16) for i in range(2)]
        cosb = [cx.sb(st, f"cosb{i}", [128, 512], F32) for i in range(2)]
        sinb = [cx.sb(st, f"sinb{i}", [128, 512], F32) for i in range(2)]
        t1 = [cx.sb(st, f"t1_{i}", [128, 512], F32) for i in range(2)]
        t2 = [cx.sb(st, f"t2_{i}", [128, 512], F32) for i in range(2)]
        ob = [cx.sb(st, f"ob{i}", [128, 512], BF16) for i in range(3)]
        of = [cx.sb(st, f"of{i}", [128, 512], F32) for i in range(3)]
        tp = cx.ps(st, "tp", [128, 1024], BF16)
        pa = [cx.ps(st, f"pa{i}", [128, 512]) for i in range(2)]
        pb = [cx.ps(st, f"pb{i}", [128, 512]) for i in range(2)]
        pc = [cx.ps(st, f"pc{i}", [128, 512]) for i in range(2)]
        for s in sss:
            cx.op("dve", lambda e, s=s: e.memset(s[:, :], EPS), w=[s])
        nblk = ntok // 512
        it = 0
        k_ob = 0
        k_of = 0
        k_r = 0
        k_c = 0
        for b in range(nblk):
            hT = hTs[b % 2]
            tb = b * 512
            for j in range(4):
                t0 = tb + j * 128
                xt = xts[it % 2]
                ss = sss[it % 2]
                it += 1
                cx.dma("sp", xt[:, :], x_in[t0:t0 + 128, :], r=[x_in], w=[xt])
                norm_transpose_tile(cx, xt, hT, j * 128, ident, junk, ss, xn, tp, nw_b)
            cb = cosb[b % 2]
            sb_ = sinb[b % 2]
            if any(s["kind"] == "rope" for s in specs):
                cx.dma("sp", cb[:, :], consts["cos"][:, tb:tb + 512], r=[consts["cos"]], w=[cb])
                cx.dma("sp", sb_[:, :], consts["sin"][:, tb:tb + 512], r=[consts["sin"]], w=[sb_])
            for si, sp in enumerate(specs):
                kind = sp["kind"]
                if b == 0:
                    tgt = needs[min(si + 2, len(specs) - 1)] if si + 2 < len(specs) else len(jobs)
                    tgt = max(tgt, needs[si])
                    ws.emit(max(0, tgt - ws.n))
                if kind == "rope":
                    A = pa[k_r % 2]
                    B = pb[k_r % 2]
                    u1 = t1[k_r % 2]
                    u2 = t2[k_r % 2]
                    k_r += 1
                    for c in range(8):
                        cx.op("pe", lambda e, c=c, A=A, a=sp["a"]: e.matmul(
                            A[:, :], lhsT=wsb[:, c, a:a + 128], rhs=hT[:, c, :],
                            start=(c == 0), stop=(c == 7)), r=wb(sp["a"], 128) + [hT], w=[A])
                    for c in range(8):
                        cx.op("pe", lambda e, c=c, B=B, a=sp["b"]: e.matmul(
                            B[:, :], lhsT=wsb[:, c, a:a + 128], rhs=hT[:, c, :],
                            start=(c == 0), stop=(c == 7)), r=wb(sp["b"], 128) + [hT], w=[B])
                    cx.op("dve", lambda e, A=A, u1=u1: e.tensor_tensor(
                        out=u1[:, :], in0=A[:, :], in1=cb[:, :], op=ALU.mult), r=[A, cb], w=[u1])
                    cx.op("dve", lambda e, B=B, u2=u2: e.tensor_tensor(
                        out=u2[:, :], in0=B[:, :], in1=sb_[:, :], op=ALU.mult), r=[B, sb_], w=[u2])
                    o = ob[k_ob % 3]
                    k_ob += 1
                    cx.op("pool", lambda e, o=o, u1=u1, u2=u2: e.tensor_tensor(
                        out=o[:, :], in0=u1[:, :], in1=u2[:, :], op=ALU.add), r=[u1, u2], w=[o])
                    cx.dma("pool", sp["dst"][:, tb:tb + 512], o[:, :], r=[o], w=dst_buf)
                elif kind == "raw":
                    n = sp["n"]
                    C = pc[k_c % 2]
                    k_c += 1
                    for c in range(8):
                        cx.op("pe", lambda e, c=c, C=C, a=sp["a"], n=n: e.matmul(
                            C[:n, :], lhsT=wsb[:, c, a:a + n], rhs=hT[:, c, :],
                            start=(c == 0), stop=(c == 7)), r=wb(sp["a"], n) + [hT], w=[C])
                    o = of[k_of % 3]
                    k_of += 1
                    cx.op("act", lambda e, o=o, C=C, n=n: e.copy(out=o[:n, :], in_=C[:n, :]),
                          r=[C], w=[o])
                    cx.dma("pool", sp["dst"][:, tb:tb + 512], o[:n, :], r=[o], w=dst_buf)
                elif kind == "tm":
                    n = sp["n"]
                    isbf = sp["dst"].dtype == BF16
                    for j in range(4):
                        C = pc[k_c % 2]
                        k_c += 1
                        for c in range(8):
                            cx.op("pe", lambda e, c=c, C=C, a=sp["a"], n=n, j=j: e.matmul(
                                C[:, :n], lhsT=hT[:, c, j * 128:(j + 1) * 128],
                                rhs=wsb[:, c, a:a + n],
                                start=(c == 0), stop=(c == 7)), r=wb(sp["a"], n) + [hT], w=[C])
                        if isbf:
                            o = ob[k_ob % 3]
                            k_ob += 1
                        else:
                            o = of[k_of % 3]
                            k_of += 1
                        fn = AF.Sigmoid if sp.get("act") == "sigmoid" else AF.Copy
                        cx.op("act", lambda e, o=o, C=C, n=n, fn=fn: e.activation(
                            out=o[:, :n], in_=C[:, :n], func=fn), r=[C], w=[o])
                        cx.dma("pool", sp["dst"][tb + j * 128:tb + (j + 1) * 128, :], o[:, :n],
                               r=[o], w=dst_buf)
    cx.barrier()


class _ColView:
    def __init__(self, buf, c0):
        self.buf = buf
        self.c0 = c0
        self.t = buf.t

    def __getitem__(self, k):
        p, c, cols = k
        return self.buf.t[p, c, slice(cols.start + self.c0, cols.stop + self.c0)]

    @property
    def w(self):
        return self.buf.w

    @w.setter
    def w(self, v):
        self.buf.w = v

    @property
    def r(self):
        return self.buf.r

    @r.setter
    def r(self, v):
        self.buf.r = v


def attn_run(cx, st, units, E1, mode, scale=1.0, bufs=None):
    sS, Os, Pb = bufs["sS"], bufs["Os"], bufs["Pb"]
    npb = len(Pb)
    flat = []
    for ui, u in enumerate(units):
        nt = len(u["tiles"])
        first = [None] * 4
        last = [None] * 4
        for ti, t in enumerate(u["tiles"]):
            for s in range(4):
                if t["subs"][s]:
                    if first[s] is None:
                        first[s] = ti
                    last[s] = ti
        for ti, t in enumerate(u["tiles"]):
            flat.append((ui, ti, first, last))

    def emit_S(n):
        ui, ti, _, _ = flat[n]
        u = units[ui]
        t = u["tiles"][ti]
        S = sS[n % len(sS)]
        add = t.get("add")
        rows = u.get("rows", (0, 128))
        cx.op("pe", lambda e: e.matmul(S[:, :], lhsT=t["k"], rhs=u["q"], start=True,
                                       stop=(add is None)), r=list(u["qr"]) + list(t["r"]), w=[S],
              rows=rows)
        if add is not None:
            cx.op("pe", lambda e: e.matmul(S[:, :], lhsT=add[0], rhs=add[1], start=False,
                                           stop=True), r=list(add[2]), w=[S], rows=rows)

    def emit_rest(n):
        ui, ti, first, last = flat[n]
        u = units[ui]
        t = u["tiles"][ti]
        S = sS[n % len(sS)]
        P = Pb[n % npb]
        O = Os[ui % 2]
        if mode == "exp":
            cx.op("act", lambda e: e.activation(out=P[:, :], in_=S[:, :], func=AF.Exp,
                                                scale=scale), r=[S], w=[P])
        else:
            rf = t["rowfac"]
            cx.op("act", lambda e: e.activation(out=P[:, :], in_=S[:, :], func=AF.Copy,
                                                scale=rf[0]), r=[S, rf[1]], w=[P])
        if t.get("mask") is not None:
            m = t["mask"]
            cx.op("dve", lambda e: e.tensor_tensor(out=P[:, :], in0=P[:, :], in1=m[0],
                                                   op=ALU.mult), r=[P, m[1]], w=[P])
        for s in range(4):
            if not t["subs"][s]:
                continue
            bk = (ui, s // 2)
            st_flag = bk not in started
            started.add(bk)
            cx.op("pe", lambda e, s=s: e.matmul(
                O[s // 2][:, s % 2, :E1], lhsT=P[:, s * 128:(s + 1) * 128], rhs=t["v"],
                start=st_flag, stop=(ti == last[s]), skip_group_check=True),
                r=[P] + list(t["r"]), w=[O[s // 2]])
        if ti == len(u["tiles"]) - 1:
            u["fin"](O[0], O[1])

    n = len(flat)
    if n == 0:
        return
    started = set()
    depth = len(sS) - 1
    for k in range(min(depth, n)):
        emit_S(k)
    for i in range(n):
        if i + depth < n:
            emit_S(i + depth)
        emit_rest(i)


def causal_tiles(j):
    out = []
    for i in range(4 * j + 4):
        dlt = i - 4 * j
        if dlt < 0:
            out.append((i, None, [True] * 4))
        else:
            out.append((i, dlt, [s >= dlt for s in range(4)]))
    return out


def bcast_load(cx, st, name, src_ap, n, dt=F32, parts=128):
    b = cx.sb(st, name, [parts, n], dt)
    cx.dma("sp", b[:, :], src_ap.partition_broadcast(parts), w=[b])
    return b


def rstd_from_ss(cx, ss_ap, out_ap, bufs, n, eps_ap):
    cx.op("act", lambda e: e.activation(out=out_ap, in_=ss_ap, func=AF.Sqrt, scale=1.0 / n,
                                        bias=eps_ap), r=bufs, w=bufs)
    cx.op("dve", lambda e: e.reciprocal(out=out_ap, in_=out_ap), r=bufs, w=bufs)


def diffattn_stage(cx, qTd, kTd, Vd, lam_ap, subln_ap, oTd, ident, consts, lam_init, ntok=T):
    nb = ntok // 512
    nkt = ntok // 128
    with ExitStack() as st:
        kT = [cx.sb(st, f"kT{i}", [128, ntok], BF16) for i in range(2)]
        qT = [cx.sb(st, f"qT{i}", [128, ntok], BF16) for i in range(2)]
        Vs = [cx.sb(st, f"V{i}", [128, nkt, 129], BF16) for i in range(2)]
        masks = cx.sb(st, "masks", [128, 4, 512], BF16)
        cx.dma("sp", masks[:, :, :], consts["cmask"].t, r=[consts["cmask"]], w=[masks])
        for v in Vs:
            cx.op("dve", lambda e, v=v: e.memset(v[:, :, 128:129], 1.0), w=[v])
        lam = bcast_load(cx, st, "lam", lam_ap, 256)
        sub = bcast_load(cx, st, "subln", subln_ap, 128)
        cx.op("dve", lambda e: e.tensor_scalar(out=sub[:, :], in0=sub[:, :], scalar1=1.0 - lam_init,
                                               scalar2=None, op0=ALU.mult), r=[sub], w=[sub])
        sm = cx.sb(st, "sm", [128, 8], F32)
        lprod = cx.sb(st, "lprod", [128, 2, 64], F32)
        lv = lam[:, :].rearrange("p (a b d) -> p a b d", a=2, b=2)
        cx.op("dve", lambda e: e.tensor_tensor(out=lprod[:, :, :], in0=lv[:, :, 0, :],
                                               in1=lv[:, :, 1, :], op=ALU.mult), r=[lam], w=[lprod])
        cx.op("dve", lambda e: e.reduce_sum(out=sm[:, 0:2], in_=lprod[:, :, :], axis=AX.X),
              r=[lprod], w=[sm])
        cx.op("act", lambda e: e.activation(out=sm[:, 2:4], in_=sm[:, 0:2], func=AF.Exp),
              r=[sm], w=[sm])
        cx.op("dve", lambda e: e.tensor_tensor(out=sm[:, 4:5], in0=sm[:, 3:4], in1=sm[:, 2:3],
                                               op=ALU.subtract), r=[sm], w=[sm])
        cx.op("dve", lambda e: e.tensor_scalar(out=sm[:, 5:6], in0=sm[:, 4:5], scalar1=-lam_init,
                                               scalar2=None, op0=ALU.add), r=[sm], w=[sm])
        cx.op("dve", lambda e: e.memset(sm[:, 6:7], EPS), r=[], w=[sm])
        oc0 = [cx.sb(st, f"oc0_{i}", [128, 4, 128], F32) for i in range(2)]
        dd4 = cx.sb(st, "dd4", [128, 4, 128], F32)
        sq4 = cx.sb(st, "sq4", [128, 4, 128], F32)
        yb4 = cx.sb(st, "yb4", [128, 4, 128], BF16)
        fs = cx.sb(st, "fs", [128, 12], F32)
        mhalf = cx.sb(st, "mhalf", [128, 4], F32)
        cx.op("dve", lambda e: e.memset(mhalf[:, :], -0.5), w=[mhalf])
        oTs = [cx.sb(st, f"oTs{i}", [128, 512], BF16) for i in range(2)]
        tpf = cx.ps(st, "tpf", [128, 512], BF16)

        for h in range(4):
            k_sb = kT[h % 2]
            q_sb = qT[h % 2]
            v_sb = Vs[h % 2]
            cx.dma("sp", k_sb[:, :], kTd[h, :, :ntok], r=[kTd], w=[k_sb])
            cx.dma("sp", q_sb[:, :], qTd[h, :, :ntok], r=[qTd], w=[q_sb])
            cx.dma("sp", v_sb[:, :, 0:128],
                   Vd[:ntok, h * 128:(h + 1) * 128].rearrange("(i p) e -> p i e", p=128),
                   r=[Vd], w=[v_sb])
            units = []
            for j in range(nb):
                for c in range(2):
                    pr = slice(c * 64, (c + 1) * 64)
                    tiles = []
                    for (i, mid, subs) in causal_tiles(j):
                        tiles.append(dict(k=k_sb[pr, i * 128:(i + 1) * 128], v=v_sb[:, i, :],
                                          r=[k_sb, v_sb],
                                          mask=(masks[:, mid, :], masks) if mid is not None else None,
                                          subs=subs))

                    def fin(O0, O1, c=c, j=j, h=h):
                        oc = oc0[j % 2]
                        oT_sb = oTs[j % 2]
                        tgt = oc if c == 0 else dd4
                        for bi, O in enumerate((O0, O1)):
                            cx.op("dve", lambda e: e.reciprocal(out=fs[:, bi * 2:bi * 2 + 2], in_=O[:, :, 128]),
                                  r=[O], w=[fs])
                            cx.op("dve", lambda e: e.tensor_tensor(
                                out=tgt[:, bi * 2:bi * 2 + 2, :], in0=O[:, :, 0:128],
                                in1=fs[:, bi * 2:bi * 2 + 2].unsqueeze(2).to_broadcast([128, 2, 128]),
                                op=ALU.mult), r=[O, fs], w=[tgt])
                        if c == 1:
                            cx.op("dve", lambda e: e.scalar_tensor_tensor(
                                out=dd4[:, :, :], in0=dd4[:, :, :], scalar=sm[:, 5:6], in1=oc[:, :, :],
                                op0=ALU.mult, op1=ALU.add), r=[dd4, sm, oc], w=[dd4])
                            cx.op("pool", lambda e: e.tensor_tensor(out=sq4[:, :, :], in0=dd4[:, :, :],
                                                                    in1=dd4[:, :, :], op=ALU.mult),
                                  r=[dd4], w=[sq4])
                            cx.op("dve", lambda e: e.reduce_sum(out=fs[:, 4:8], in_=sq4[:, :, :], axis=AX.X),
                                  r=[sq4], w=[fs])
                            cx.op("dve", lambda e: e.tensor_scalar(out=fs[:, 4:8], in0=fs[:, 4:8], scalar1=1.0 / 128,
                                                                   scalar2=EPS, op0=ALU.mult, op1=ALU.add),
                                  r=[fs], w=[fs])
                            cx.op("pool", lambda e: e.tensor_tensor(out=fs[:, 8:12], in0=fs[:, 4:8], in1=mhalf[:, 0:4],
                                                                    op=ALU.pow), r=[fs, mhalf], w=[fs])
                            cx.op("dve", lambda e: e.tensor_tensor(
                                out=dd4[:, :, :], in0=dd4[:, :, :],
                                in1=fs[:, 8:12].unsqueeze(2).to_broadcast([128, 4, 128]), op=ALU.mult),
                                r=[dd4, fs], w=[dd4])
                            cx.op("pool", lambda e: e.tensor_tensor(
                                out=yb4[:, :, :], in0=dd4[:, :, :],
                                in1=sub[:, :].unsqueeze(1).to_broadcast([128, 4, 128]), op=ALU.mult),
                                r=[dd4, sub], w=[yb4])
                            for s in range(4):
                                cx.op("pe", lambda e: e.transpose(
                                    out=tpf[:, s * 128:(s + 1) * 128], in_=yb4[:, s, :], identity=ident[:, :]),
                                    r=[yb4, ident], w=[tpf])
                            cx.op("dve", lambda e: e.tensor_copy(out=oT_sb[:, :], in_=tpf[:, :]),
                                  r=[tpf], w=[oT_sb])
                            cx.dma("pool", oTd[h, :, j * 512:(j + 1) * 512], oT_sb[:, :],
                                   r=[oT_sb], w=[oTd])
                    units.append(dict(q=q_sb[pr, j * 512:(j + 1) * 512], qr=[q_sb], tiles=tiles, fin=fin))
            attn_run_shared(cx, st, units, 129, "exp", scale=0.125)
    cx.barrier()


def attn_run_shared(cx, st, units, E1, mode, scale=1.0):
    key = "_attn_bufs"
    if not hasattr(st, key):
        setattr(st, key, dict(
            sS=[cx.ps(st, f"sS{i}", [128, 512]) for i in range(3)],
            Os=[[cx.ps(st, f"O{i}_{h}", [128, 2, 256]) for h in range(2)] for i in range(2)],
            Pb=[cx.sb(st, f"Pb{i}", [128, 512], BF16) for i in range(4)]))
    attn_run(cx, st, units, E1, mode, scale=scale, bufs=getattr(st, key))


def rot_cols(w):
    n = w.shape[1]
    idx = np.arange(n)
    g = idx // 64
    d = idx % 64
    src = g * 64 + (d + 32) % 64
    return np.ascontiguousarray(w[:, src])


def host_consts(ntok=T):
    pos = np.arange(ntok, dtype=np.float64)
    inv = 10000.0 ** (-np.arange(0, 64, 2, dtype=np.float64) / 64)
    d = np.arange(128) % 64
    ang = pos[None, :] * inv[d % 32][:, None]
    cos = np.cos(ang).astype(np.float32)
    sgn = np.where(d < 32, -1.0, 1.0)[:, None]
    sin = (np.sin(ang) * sgn).astype(np.float32)
    key = np.arange(128)[:, None, None]
    dl = np.arange(4)[None, :, None]
    q = np.arange(512)[None, None, :]
    cmask = ((128 * dl + key) <= q).astype(np.float32).astype(ml_dtypes.bfloat16)
    return {"c_cos": cos, "c_sin": sin, "c_cmask": cmask}


def declare_consts(cx, ntok=T):
    return {
        "cos": cx.dram("c_cos", [128, ntok], F32, kind="ExternalInput"),
        "sin": cx.dram("c_sin", [128, ntok], F32, kind="ExternalInput"),
        "cmask": cx.dram("c_cmask", [128, 4, 512], BF16, kind="ExternalInput"),
    }


def outproj_stage(cx, oTd, w_ap, x_in, x_out, ntok=T):
    with ExitStack() as st:
        wsb = cx.sb(st, "wo", [128, 8, D], BF16)
        stage = [cx.sb(st, f"wstage{i}", [128, 1024], F32) for i in range(2)]
        load_weight_bf16(cx, st, wsb, w_ap, 8, D, stage)
        oTs = [cx.sb(st, f"oTb{i}", [128, 8, 512], BF16) for i in range(2)]
        xrs = [cx.sb(st, f"xr{i}", [128, D], F32) for i in range(2)]
        py = [cx.ps(st, f"py{i}", [128, 512]) for i in range(2)]
        k = 0
        for b in range(ntok // 512):
            tb = b * 512
            o = oTs[b % 2]
            cx.dma("sp", o[:, :, :], oTd[:, :, tb:tb + 512].rearrange("c p t -> p c t"),
                   r=[oTd], w=[o])
            for j in range(4):
                t0 = tb + j * 128
                xr = xrs[j % 2]
                cx.dma("sp", xr[:, :], x_in[t0:t0 + 128, :], r=[x_in], w=[xr])
                for h in range(2):
                    y = py[k % 2]
                    k += 1
                    for c in range(8):
                        cx.op("pe", lambda e, c=c: e.matmul(
                            y[:, :], lhsT=o[:, c, j * 128:(j + 1) * 128],
                            rhs=wsb[:, c, h * 512:(h + 1) * 512], start=(c == 0), stop=(c == 7)),
                            r=[o, wsb], w=[y])
                    cx.op("dve", lambda e: e.tensor_tensor(
                        out=xr[:, h * 512:(h + 1) * 512], in0=y[:, :],
                        in1=xr[:, h * 512:(h + 1) * 512], op=ALU.add), r=[y, xr], w=[xr])
                cx.dma("pool", x_out[t0:t0 + 128, :], xr[:, :], r=[xr], w=[x_out])
    cx.barrier()


def rwkv_stage(cx, pTb, prm, oTd, ident, identf, ntok=T):
    NB = ntok // 512
    with ExitStack() as st:
        G = [cx.ps(st, f"G{i}", [128, 512]) for i in range(8)]

        def g3(i, n=64):
            return G[i].t[0:64, :].rearrange("p (h t) -> p h t", h=8)[:, :, 0:n]

        mu_c = load_col_vec(cx, st, "mu", prm["mu"], 1792)
        w0c = load_col_vec(cx, st, "w0", prm["w0"], 512)
        a0c = load_col_vec(cx, st, "a0", prm["a0"], 512)
        kkc = load_col_vec(cx, st, "kk", prm["kk"], 512)
        kac = load_col_vec(cx, st, "ka", prm["ka"], 512)
        rkc = load_col_vec(cx, st, "rk", prm["rk"], 512)
        lnw_b = bcast_load(cx, st, "lnw", prm["lnw"], 512, parts=64)
        lnb_b = bcast_load(cx, st, "lnb", prm["lnb"], 512, parts=64)
        wst = cx.sb(st, "wst", [128, 512], F32)
        w2_sb = cx.sb(st, "w2", [64, 512], BF16)
        a2_sb = cx.sb(st, "a2", [128, 512], BF16)
        g2_sb = cx.sb(st, "g2", [128, 512], BF16)
        cx.dma("sp", wst[0:64, :], prm["w2"], w=[wst])
        cx.op("dve", lambda e: e.tensor_copy(out=w2_sb[:, :], in_=wst[0:64, :]), r=[wst], w=[w2_sb])
        cx.dma("sp", wst[64:128, :], prm["a2"], r=[], w=[wst])
        cx.op("dve", lambda e: e.tensor_copy(out=a2_sb[64:128, :], in_=wst[64:128, :]), r=[wst], w=[a2_sb])
        cx.dma("sp", wst[:, :], prm["g2"], w=[wst])
        cx.op("dve", lambda e: e.tensor_copy(out=g2_sb[:, :], in_=wst[:, :]), r=[wst], w=[g2_sb])
        bones = cx.sb(st, "bones", [128, 128], BF16)
        cx.op("dve", lambda e: e.memset(bones[:, :], 0.0), w=[bones])
        cx.op("dve", lambda e: e.memset(bones[0:64, 0:64], 1.0), w=[bones])
        cx.op("dve", lambda e: e.memset(bones[64:128, 64:128], 1.0), w=[bones])
        hsel = cx.sb(st, "hsel", [128, 2], BF16)
        cx.op("dve", lambda e: e.memset(hsel[:, :], 0.0), w=[hsel])
        cx.op("dve", lambda e: e.memset(hsel[0:64, 0:1], 1.0), w=[hsel])
        cx.op("dve", lambda e: e.memset(hsel[64:128, 1:2], 1.0), w=[hsel])
        rmask = cx.sb(st, "rmask", [128, 512], F32)
        cx.op("dve", lambda e: e.memset(rmask[:, :], 1.0), w=[rmask])
        cx.op("dve", lambda e: e.memset(
            rmask[:, :].rearrange("p (c t) -> p c t", t=64)[:, :, 0:1], 0.0), w=[rmask])
        epsc = cx.sb(st, "epsc", [128, 2], F32)
        cx.op("dve", lambda e: e.memset(epsc[:, 0:1], 1e-12), w=[epsc])
        cx.op("dve", lambda e: e.memset(epsc[:, 1:2], 64e-5), w=[epsc])

        def mk_mask(name, step, cm, cmp, val):
            m = cx.sb(st, name, [64, 8, 64], F32)
            cx.op("pool", lambda e: e.memset(m[:, :, :], val), w=[m])
            cx.op("pool", lambda e: e.affine_select(out=m[:, :, :], in_=m[:, :, :],
                                                    pattern=[[0, 8], [step, 64]], compare_op=cmp,
                                                    fill=0.0, base=0, channel_multiplier=cm),
                  r=[m], w=[m])
            return m
        Mlt = mk_mask("Mlt", 1, -1, ALU.is_gt, 1.0)
        Mle = mk_mask("Mle", 1, -1, ALU.is_ge, 1.0)
        nMlt = mk_mask("nMlt", 1, -1, ALU.is_gt, -1.0)
        nMle = mk_mask("nMle", 1, -1, ALU.is_ge, -1.0)
        nMltT = mk_mask("nMltT", -1, 1, ALU.is_gt, -1.0)
        Ieye = mk_mask("Ieye", 1, -1, ALU.is_equal, 1.0)

        S32 = cx.sb(st, "S32", [128, 4, 64], F32)
        Sbf = cx.sb(st, "Sbf", [128, 4, 64], BF16)
        cx.op("dve", lambda e: e.memset(S32[:, :, :], 0.0), w=[S32])
        cx.op("dve", lambda e: e.memset(Sbf[:, :, :], 0.0), w=[Sbf])

        def dbl(name, shape, dt):
            return [cx.sb(st, f"{name}{i}", shape, dt) for i in range(2)]
        kb = dbl("kb", [128, 4, 8, 128], BF16)
        kr = dbl("kr", [128, 4, 8, 128], BF16)
        khT = dbl("khT", [64, 4, 8, 128], BF16)
        nbhT = dbl("nbhT", [64, 4, 8, 128], BF16)
        vT = dbl("vT", [64, 4, 8, 128], BF16)
        WLt = dbl("WLt", [128, 4, 8], F32)
        bon = dbl("bon", [64, 8, 8], F32)
        sgT = dbl("sgT", [128, 512], BF16)
        oTblk = dbl("oTblk", [128, 4, 512], BF16)

        raws = [cx.sb(st, f"raw{i}", [128, 513], F32) for i in range(3)]
        dtmp = [cx.sb(st, f"dtmp{i}", [128, 512], F32) for i in range(2)]

        def f32t(name):
            return cx.sb(st, name, [128, 512], F32)
        xm12, xm13, xr, xk, xv = f32t("xm12"), f32t("xm13"), f32t("xr"), f32t("xk"), f32t("xv")
        tw_bf = cx.sb(st, "tw_bf", [64, 512], BF16)
        al_bf = cx.sb(st, "al_bf", [128, 512], BF16)
        sigw, av, kkv, kap, k2, bv, cum = (f32t("sigw"), f32t("av"), f32t("kkv"), f32t("kap"),
                                           f32t("k2"), f32t("bv"), f32t("cum"))
        epos, eneg, eprev, ehat, tt = f32t("epos"), f32t("eneg"), f32t("eprev"), f32t("ehat"), f32t("tt")
        sq_bf = cx.sb(st, "sq_bf", [128, 512], BF16)
        khat_bf = cx.sb(st, "khat_bf", [128, 512], BF16)
        nbhat_bf = cx.sb(st, "nbhat_bf", [128, 512], BF16)
        v_bf = cx.sb(st, "v_bf", [128, 512], BF16)
        rkk_bf = cx.sb(st, "rkk_bf", [128, 512], BF16)
        kctr = [0]

        def load_xm(c, b, dst):
            raw = raws[kctr[0] % 3]
            d = dtmp[kctr[0] % 2]
            kctr[0] += 1
            tb = b * 512
            if b == 0:
                cx.op("dve", lambda e: e.memset(raw[:, 0:1], 0.0), w=[raw])
                cx.dma("sp", raw[:, 1:513], pTb[c, :, 0:512], r=[pTb], w=[raw])
            else:
                cx.dma("sp", raw[:, 0:513], pTb[c, :, tb - 1:tb + 512], r=[pTb], w=[raw])
            cx.op("pool", lambda e: e.tensor_tensor(out=d[:, :], in0=raw[:, 0:512], in1=raw[:, 1:513],
                                                    op=ALU.subtract), r=[raw], w=[d])
            cx.op("dve", lambda e: e.scalar_tensor_tensor(
                out=dst[:, :], in0=d[:, :], scalar=mu_c[:, c:c + 1], in1=raw[:, 1:513],
                op0=ALU.mult, op1=ALU.add), r=[d, raw, mu_c], w=[dst])

        def v3(buf):
            return buf[:, :].rearrange("p (c t) -> p c t", t=64)

        def phase1(b):
            par = b % 2
            load_xm(12, b, xm12)
            load_xm(13, b, xm13)
            cx.op("act", lambda e: e.activation(out=tw_bf[:, :], in_=xm12[0:64, :], func=AF.Tanh),
                  r=[xm12], w=[tw_bf])
            cx.op("dve", lambda e: e.tensor_copy(out=al_bf[64:128, :], in_=xm12[64:128, :]),
                  r=[xm12], w=[al_bf])
            cx.op("act", lambda e: e.activation(out=sgT[par][:, :], in_=xm13[:, :], func=AF.Sigmoid),
                  r=[xm13], w=[sgT[par]])
            for hp in range(4):
                yield
                cs = slice(hp * 128, (hp + 1) * 128)
                load_xm(hp, b, xr)
                load_xm(4 + hp, b, xk)
                load_xm(8 + hp, b, xv)
                pw, pa, pss = G[5], G[6], G[7]
                cx.op("pe", lambda e: e.matmul(pw[:, :], lhsT=w2_sb[0:64, cs], rhs=tw_bf[0:64, :],
                                               start=True, stop=True), r=[w2_sb, tw_bf], w=[pw])
                cx.op("pe", lambda e: e.matmul(pa[:, :], lhsT=a2_sb[64:128, cs], rhs=al_bf[64:128, :],
                                               start=True, stop=True), r=[a2_sb, al_bf], w=[pa])
                cx.op("act", lambda e: e.activation(out=sigw[:, :], in_=pw[:, :], func=AF.Sigmoid,
                                                    bias=w0c[:, hp:hp + 1]), r=[pw, w0c], w=[sigw])
                cx.op("act", lambda e: e.activation(out=av[:, :], in_=pa[:, :], func=AF.Sigmoid,
                                                    bias=a0c[:, hp:hp + 1]), r=[pa, a0c], w=[av])
                cx.op("dve", lambda e: e.tensor_scalar(out=sigw[:, :], in0=sigw[:, :],
                                                       scalar1=-0.6065306597126334, scalar2=None,
                                                       op0=ALU.mult), r=[sigw], w=[sigw])
                cx.op("dve", lambda e: e.tensor_tensor_scan(out=cum[:, :], data0=rmask[:, :],
                                                            data1=sigw[:, :], initial=0.0,
                                                            op0=ALU.mult, op1=ALU.add),
                      r=[rmask, sigw], w=[cum])
                yield
                cx.op("dve", lambda e: e.tensor_scalar(out=kkv[:, :], in0=xk[:, :],
                                                       scalar1=kkc[:, hp:hp + 1], scalar2=None,
                                                       op0=ALU.mult), r=[xk, kkc], w=[kkv])
                cx.op("act", lambda e: e.activation(out=sq_bf[:, :], in_=kkv[:, :], func=AF.Square),
                      r=[kkv], w=[sq_bf])
                cx.op("pe", lambda e: e.matmul(pss[:, :], lhsT=bones[:, :], rhs=sq_bf[:, :],
                                               start=True, stop=True), r=[bones, sq_bf], w=[pss])
                cx.op("act", lambda e: e.activation(out=tt[:, :], in_=pss[:, :], func=AF.Sqrt,
                                                    bias=epsc[:, 0:1]), r=[pss, epsc], w=[tt])
                cx.op("dve", lambda e: e.reciprocal(out=tt[:, :], in_=tt[:, :]), r=[tt], w=[tt])
                cx.op("dve", lambda e: e.tensor_tensor(out=kap[:, :], in0=kkv[:, :], in1=tt[:, :],
                                                       op=ALU.mult), r=[kkv, tt], w=[kap])
                yield
                cx.op("dve", lambda e: e.tensor_scalar(out=tt[:, :], in0=av[:, :], scalar1=-1.0,
                                                       scalar2=kac[:, hp:hp + 1], op0=ALU.add,
                                                       op1=ALU.mult), r=[av, kac], w=[tt])
                cx.op("dve", lambda e: e.scalar_tensor_tensor(out=k2[:, :], in0=tt[:, :], scalar=1.0,
                                                              in1=xk[:, :], op0=ALU.add, op1=ALU.mult),
                      r=[tt, xk], w=[k2])
                cx.op("pool", lambda e: e.tensor_tensor(out=bv[:, :], in0=kap[:, :], in1=av[:, :],
                                                        op=ALU.mult), r=[kap, av], w=[bv])
                yield
                cx.op("act", lambda e: e.activation(out=epos[:, :], in_=cum[:, :], func=AF.Exp),
                      r=[cum], w=[epos])
                cx.op("act", lambda e: e.activation(out=eneg[:, :], in_=cum[:, :], func=AF.Exp,
                                                    scale=-1.0), r=[cum], w=[eneg])
                cx.op("pool", lambda e: e.tensor_tensor(out=tt[:, :], in0=cum[:, :], in1=sigw[:, :],
                                                        op=ALU.subtract), r=[cum, sigw], w=[tt])
                cx.op("act", lambda e: e.activation(out=eprev[:, :], in_=tt[:, :], func=AF.Exp),
                      r=[tt], w=[eprev])
                cx.op("pool", lambda e: e.tensor_tensor(
                    out=v3(tt), in0=v3(cum)[:, :, 63:64].to_broadcast([128, 8, 64]), in1=v3(cum),
                    op=ALU.subtract), r=[cum], w=[tt])
                cx.op("act", lambda e: e.activation(out=ehat[:, :], in_=tt[:, :], func=AF.Exp),
                      r=[tt], w=[ehat])
                cx.op("act", lambda e: e.activation(out=WLt[par][:, hp, :], in_=v3(cum)[:, :, 63],
                                                    func=AF.Exp), r=[cum], w=[WLt[par]])
                yield
                cx.op("dve", lambda e: e.tensor_tensor(out=kb[par][:, hp, :, 0:64], in0=v3(k2),
                                                       in1=v3(eneg), op=ALU.mult), r=[k2, eneg], w=[kb[par]])
                cx.op("dve", lambda e: e.tensor_tensor(out=kb[par][:, hp, :, 64:128], in0=v3(bv),
                                                       in1=v3(eneg), op=ALU.mult), r=[bv, eneg], w=[kb[par]])
                cx.op("dve", lambda e: e.tensor_tensor(out=kr[par][:, hp, :, 0:64], in0=v3(kap),
                                                       in1=v3(eprev), op=ALU.mult), r=[kap, eprev], w=[kr[par]])
                cx.op("dve", lambda e: e.tensor_tensor(out=kr[par][:, hp, :, 64:128], in0=v3(xr),
                                                       in1=v3(epos), op=ALU.mult), r=[xr, epos], w=[kr[par]])
                cx.op("pool", lambda e: e.tensor_tensor(out=khat_bf[:, :], in0=k2[:, :], in1=ehat[:, :],
                                                        op=ALU.mult), r=[k2, ehat], w=[khat_bf])
                cx.op("dve", lambda e: e.scalar_tensor_tensor(
                    out=nbhat_bf[:, :], in0=bv[:, :], scalar=-1.0, in1=ehat[:, :], op0=ALU.mult,
                    op1=ALU.mult), r=[bv, ehat], w=[nbhat_bf])
                cx.op("act", lambda e: e.copy(out=v_bf[:, :], in_=xv[:, :]), r=[xv], w=[v_bf])
                cx.op("dve", lambda e: e.scalar_tensor_tensor(
                    out=rkk_bf[:, :], in0=xr[:, :], scalar=rkc[:, hp:hp + 1], in1=k2[:, :],
                    op0=ALU.mult, op1=ALU.mult), r=[xr, rkc, k2], w=[rkk_bf])
                yield
                for src, dst, bank in ((khat_bf, khT[par], G[0]), (nbhat_bf, nbhT[par], G[1]),
                                       (v_bf, vT[par], G[2])):
                    pt = bank.t[0:64, :].bitcast(BF16)
                    for ch in range(8):
                        cx.op("pe", lambda e, ch=ch: e.transpose(
                            out=pt[:, ch * 128:(ch + 1) * 128], in_=src[:, ch * 64:(ch + 1) * 64],
                            identity=ident[:, :]), r=[src, ident], w=[bank])
                    cx.op("act", lambda e: e.copy(out=dst[:, hp, :, :],
                                                  in_=pt.rearrange("p (c k) -> p c k", c=8)),
                          r=[bank], w=[dst])
                yield
                pbn = G[3]
                for ch in range(8):
                    cx.op("pe", lambda e, ch=ch: e.matmul(
                        pbn[0:64, ch * 2:(ch + 1) * 2], lhsT=rkk_bf[:, ch * 64:(ch + 1) * 64],
                        rhs=hsel[:, :], start=True, stop=True), r=[rkk_bf, hsel], w=[pbn])
                cx.op("act", lambda e: e.copy(
                    out=bon[par][:, :, hp * 2:(hp + 1) * 2],
                    in_=pbn[0:64, 0:16].rearrange("p (c e) -> p c e", e=2)), r=[pbn], w=[bon[par]])

        def t64(name, dt):
            return cx.sb(st, name, [64, 8, 64], dt)
        Avk_sb, Bkr_sb, nBbr_sb, Z_sb = t64("Avk", BF16), t64("Bkr", BF16), t64("nBbr", BF16), t64("Zsb", BF16)
        N_sb, NT_sb, RZ_sb = t64("Nsb", BF16), t64("NTsb", BF16), t64("RZsb", F32)
        Pbf = t64("Pbf", BF16)
        Ysbs = [t64("Ysb0", F32), t64("Ysb1", F32)]
        yc, ysq = t64("yc", F32), t64("ysq", F32)
        bvv = t64("bvv", F32)
        st8 = cx.sb(st, "st8", [64, 4, 8], F32)
        out_bf = cx.sb(st, "out_bf", [64, 512], BF16)
        Stmp = cx.sb(st, "Stmp", [128, 4, 64], F32)

        Pinv = [t64("Pinv0", F32), t64("Pinv1", F32)]
        HS = [(h, h // 2, slice((h % 2) * 64, (h % 2) * 64 + 64)) for h in (0, 2, 4, 6, 1, 3, 5, 7)]

        def opsof(par, ch, h, hp, pr):
            return (kb[par][pr, hp, ch, 0:64], kb[par][pr, hp, ch, 64:128],
                    kr[par][pr, hp, ch, 0:64], kr[par][pr, hp, ch, 64:128])

        def inv_gen(b, ch):
            par = b % 2
            P_sb = Pinv[ch % 2]
            rd = [kb[par], kr[par]]
            for half in (HS[0:4], HS[4:8]):
                for (bank, li, ri) in ((5, 1, 2), (6, 2, 1)):
                    for (h, hp, pr) in half:
                        o = opsof(par, ch, h, hp, pr)
                        cx.op("pe", lambda e: e.matmul(g3(bank)[:, h, :], lhsT=o[li], rhs=o[ri],
                                                       start=True, stop=True), r=rd, w=[G[bank]],
                              rows=(pr.start, 64))
            cx.op("dve", lambda e: e.tensor_tensor(out=N_sb[:, :, :], in0=g3(5), in1=nMlt[:, :, :],
                                                   op=ALU.mult), r=[G[5], nMlt], w=[N_sb])
            cx.op("dve", lambda e: e.tensor_tensor(out=NT_sb[:, :, :], in0=g3(6), in1=nMltT[:, :, :],
                                                   op=ALU.mult), r=[G[6], nMltT], w=[NT_sb])
            cx.op("pool", lambda e: e.tensor_tensor(out=P_sb[:, :, :], in0=N_sb[:, :, :],
                                                    in1=Ieye[:, :, :], op=ALU.add), r=[N_sb, Ieye], w=[P_sb])
            cx.op("pool", lambda e: e.tensor_copy(out=Pbf[:, :, :], in_=P_sb[:, :, :]), r=[P_sb], w=[Pbf])
            yield
            for lev in range(5):
                for (h, hp, pr) in (HS if lev < 4 else []):
                    cx.op("pe", lambda e: e.matmul(g3(5)[:, h, :], lhsT=NT_sb[:, h, :], rhs=N_sb[:, h, :],
                                                   start=True, stop=True), r=[NT_sb, N_sb], w=[G[5]], rows=(0, 64))
                for (h, hp, pr) in HS:
                    cx.op("pe", lambda e: e.matmul(g3(6)[:, h, :], lhsT=N_sb[:, h, :], rhs=NT_sb[:, h, :],
                                                   start=True, stop=True), r=[NT_sb, N_sb], w=[G[6]], rows=(0, 64))
                if lev < 4:
                    cx.op("act", lambda e: e.copy(out=N_sb[:, :, :], in_=g3(5)), r=[G[5]], w=[N_sb])
                cx.op("dve", lambda e: e.tensor_copy(out=NT_sb[:, :, :], in_=g3(6)), r=[G[6]], w=[NT_sb])
                yield
                for (h, hp, pr) in HS:
                    cx.op("pe", lambda e: e.matmul(g3(7)[:, h, :], lhsT=NT_sb[:, h, :], rhs=Pbf[:, h, :],
                                                   start=True, stop=True), r=[NT_sb, Pbf], w=[G[7]], rows=(0, 64))
                cx.op("dve", lambda e: e.tensor_tensor(out=P_sb[:, :, :], in0=g3(7), in1=P_sb[:, :, :],
                                                       op=ALU.add), r=[G[7], P_sb], w=[P_sb])
                if lev < 4:
                    cx.op("act", lambda e: e.copy(out=Pbf[:, :, :], in_=P_sb[:, :, :]), r=[P_sb], w=[Pbf])
                yield

        def state_gen(b, ch):
            par = b % 2
            P_sb = Pinv[ch % 2]
            rd = [kb[par], kr[par]]
            for half in (HS[0:4], HS[4:8]):
                for (bank, li, ri) in ((0, 0, 2), (2, 0, 3), (3, 1, 3)):
                    for (h, hp, pr) in half:
                        o = opsof(par, ch, h, hp, pr)
                        cx.op("pe", lambda e: e.matmul(g3(bank)[:, h, :], lhsT=o[li], rhs=o[ri],
                                                       start=True, stop=True), r=rd, w=[G[bank]],
                              rows=(pr.start, 64))
            for (bank, m, dst) in ((0, Mlt, Avk_sb), (2, Mle, Bkr_sb), (3, nMle, nBbr_sb)):
                cx.op("dve", lambda e: e.tensor_tensor(out=dst[:, :, :], in0=g3(bank), in1=m[:, :, :],
                                                       op=ALU.mult), r=[G[bank], m], w=[dst])
            yield
            for (h, hp, pr) in HS[4:8] + HS[0:4]:
                o = opsof(par, ch, h, hp, pr)
                cx.op("pe", lambda e: e.matmul(g3(0)[:, h, :], lhsT=o[2], rhs=Sbf[pr, hp, :],
                                               start=(h == 1), stop=False, skip_group_check=True),
                      r=[kr[par], Sbf], w=[G[0]], rows=(pr.start, 64))
            for (h, hp, pr) in HS:
                e_ = h % 2
                cx.op("pe", lambda e: e.matmul(g3(0)[:, h, :], lhsT=Avk_sb[:, h, :],
                                               rhs=vT[par][:, hp, ch, e_ * 64:(e_ + 1) * 64],
                                               start=False, stop=True, skip_group_check=True),
                      r=[Avk_sb, vT[par]], w=[G[0]], rows=(0, 64))
            cx.op("act", lambda e: e.copy(out=RZ_sb[:, :, :], in_=g3(0)), r=[G[0]], w=[RZ_sb])
            yield
            for (h, hp, pr) in HS:
                cx.op("pe", lambda e: e.matmul(g3(1)[:, h, :], lhsT=P_sb[:, h, :], rhs=RZ_sb[:, h, :],
                                               start=True, stop=True), r=[P_sb, RZ_sb], w=[G[1]], rows=(0, 64))
            cx.op("act", lambda e: e.copy(out=Z_sb[:, :, :], in_=g3(1)), r=[G[1]], w=[Z_sb])
            yield
            for (h, hp, pr) in HS[4:8] + HS[0:4]:
                o = opsof(par, ch, h, hp, pr)
                cx.op("pe", lambda e: e.matmul(g3(2)[:, h, :], lhsT=o[3], rhs=Sbf[pr, hp, :],
                                               start=(h == 1), stop=False, skip_group_check=True),
                      r=[kr[par], Sbf], w=[G[2]], rows=(pr.start, 64))
            for (h, hp, pr) in HS:
                e_ = h % 2
                vh = vT[par][:, hp, ch, e_ * 64:(e_ + 1) * 64]
                cx.op("pe", lambda e: e.matmul(g3(2)[:, h, :], lhsT=Bkr_sb[:, h, :], rhs=vh,
                                               start=False, stop=False, skip_group_check=True),
                      r=[Bkr_sb, vT[par]], w=[G[2]], rows=(0, 64))
            for (h, hp, pr) in HS:
                cx.op("pe", lambda e: e.matmul(g3(2)[:, h, :], lhsT=nBbr_sb[:, h, :], rhs=Z_sb[:, h, :],
                                               start=False, stop=True, skip_group_check=True),
                      r=[nBbr_sb, Z_sb], w=[G[2]], rows=(0, 64))
            SU = G[3].t[:, :].rearrange("p (a c) -> p a c", a=4)
            for hp in range(4):
                cx.op("pe", lambda e: e.matmul(SU[:, hp, :], lhsT=khT[par][:, hp, ch, :],
                                               rhs=vT[par][:, hp, ch, :], start=(hp == 0), stop=False,
                                               skip_group_check=True), r=[khT[par], vT[par]], w=[G[3]],
                      rows=(0, 64))
                cx.op("pe", lambda e: e.matmul(
                    SU[:, hp, :], lhsT=nbhT[par][:, hp, ch, :],
                    rhs=Z_sb[:, 2 * hp:2 * hp + 2, :].rearrange("p a v -> p (a v)"),
                    start=False, stop=True, skip_group_check=True), r=[nbhT[par], Z_sb], w=[G[3]],
                    rows=(0, 64))
            cx.op("pool", lambda e: e.tensor_tensor(
                out=Stmp[:, :, :], in0=S32[:, :, :],
                in1=WLt[par][:, :, ch:ch + 1].to_broadcast([128, 4, 64]), op=ALU.mult),
                r=[S32, WLt[par]], w=[Stmp])
            for e_ in range(2):
                pr = slice(e_ * 64, (e_ + 1) * 64)
                cx.op("dve", lambda e: e.tensor_tensor(
                    out=S32[pr, :, :], in0=SU[pr, :, e_ * 64:(e_ + 1) * 64], in1=Stmp[pr, :, :],
                    op=ALU.add), r=[G[3], Stmp], w=[S32])
            cx.op("act", lambda e: e.copy(out=Sbf[:, :, :], in_=S32[:, :, :]), r=[S32], w=[Sbf])
            Ysb = Ysbs[ch % 2]
            cx.op("act", lambda e: e.copy(out=Ysb[:, :, :], in_=g3(2)), r=[G[2]], w=[Ysb])

        def out_gen(b, ch):
            par = b % 2
            Ysb = Ysbs[ch % 2]
            cx.op("pe", lambda e: e.matmul(G[4].t[0:64, :], lhsT=sgT[par][:, ch * 64:(ch + 1) * 64],
                                           rhs=g2_sb[:, :], start=True, stop=True),
                  r=[sgT[par], g2_sb], w=[G[4]])
            cx.op("dve", lambda e: e.reduce_sum(out=st8[:, 0, :], in_=Ysb[:, :, :], axis=AX.X),
                  r=[Ysb], w=[st8])
            cx.op("dve", lambda e: e.tensor_scalar(out=st8[:, 1, :], in0=st8[:, 0, :], scalar1=1.0 / 64,
                                                   scalar2=None, op0=ALU.mult), r=[st8], w=[st8])
            cx.op("dve", lambda e: e.tensor_tensor(
                out=yc[:, :, :], in0=Ysb[:, :, :],
                in1=st8[:, 1, :].unsqueeze(2).to_broadcast([64, 8, 64]), op=ALU.subtract),
                r=[Ysb, st8], w=[yc])
            cx.op("pool", lambda e: e.tensor_tensor(out=ysq[:, :, :], in0=yc[:, :, :], in1=yc[:, :, :],
                                                    op=ALU.mult), r=[yc], w=[ysq])
            yield
            cx.op("dve", lambda e: e.reduce_sum(out=st8[:, 2, :], in_=ysq[:, :, :], axis=AX.X),
                  r=[ysq], w=[st8])
            cx.op("dve", lambda e: e.tensor_scalar(out=st8[:, 2, :], in0=st8[:, 2, :], scalar1=1.0 / 64,
                                                   scalar2=64e-5, op0=ALU.mult, op1=ALU.add), r=[st8], w=[st8])
            cx.op("pool", lambda e: e.tensor_tensor(out=st8[:, 3, :], in0=st8[:, 2, :], in1=mhalf[:, :],
                                                    op=ALU.pow), r=[st8, mhalf], w=[st8])
            cx.op("dve", lambda e: e.tensor_tensor(
                out=yc[:, :, :], in0=yc[:, :, :],
                in1=st8[:, 3, :].unsqueeze(2).to_broadcast([64, 8, 64]), op=ALU.mult),
                r=[yc, st8], w=[yc])
            ycf = yc[:, :, :].rearrange("p h v -> p (h v)")
            cx.op("pool", lambda e: e.tensor_tensor(out=ycf, in0=ycf, in1=lnw_b[:, :], op=ALU.mult),
                  r=[yc, lnw_b], w=[yc])
            yield
            cx.op("pool", lambda e: e.tensor_tensor(out=ycf, in0=ycf, in1=lnb_b[:, :], op=ALU.add),
                  r=[yc, lnb_b], w=[yc])
            vview = vT[par][:, :, ch, :].rearrange("p a (e v) -> p a e v", e=2)
            cx.op("dve", lambda e: e.tensor_tensor(
                out=bvv[:, :, :].rearrange("p (a e) v -> p a e v", e=2), in0=vview,
                in1=bon[par][:, ch, :].rearrange("p (a e) -> p a e", e=2).unsqueeze(3).to_broadcast([64, 4, 2, 64]),
                op=ALU.mult), r=[vT[par], bon[par]], w=[bvv])
            cx.op("pool", lambda e: e.tensor_tensor(out=yc[:, :, :], in0=yc[:, :, :], in1=bvv[:, :, :],
                                                    op=ALU.add), r=[yc, bvv], w=[yc])
            cx.op("dve", lambda e: e.tensor_tensor(out=out_bf[:, :], in0=G[4].t[0:64, :], in1=ycf,
                                                   op=ALU.mult), r=[G[4], yc], w=[out_bf])
            yield
            ptT = G[4].t[:, :].bitcast(BF16)
            for hp in range(4):
                cx.op("pe", lambda e: e.transpose(out=ptT[:, hp * 64:(hp + 1) * 64],
                                                  in_=out_bf[:, hp * 128:(hp + 1) * 128],
                                                  identity=ident[0:64, 0:64]), r=[out_bf, ident], w=[G[4]])
            cx.op("act", lambda e: e.copy(
                out=oTblk[par][:, :, ch * 64:(ch + 1) * 64],
                in_=ptT[:, 0:256].rearrange("p (a t) -> p a t", a=4)), r=[G[4]], w=[oTblk[par]])

        def run_interleaved(gens):
            gens = [g for g in gens if g is not None]
            while gens:
                for g in list(gens):
                    try:
                        next(g)
                    except StopIteration:
                        gens.remove(g)

        mhalf = cx.sb(st, "mhalf", [64, 8], F32)
        cx.op("dve", lambda e: e.memset(mhalf[:, :], -0.5), w=[mhalf])

        def step(g, k):
            if g is None:
                return None
            for _ in range(k):
                try:
                    next(g)
                except StopIteration:
                    return None
            return g

        run_interleaved([phase1(0)])
        for b in range(NB):
            p1 = phase1(b + 1) if b + 1 < NB else None
            run_interleaved([inv_gen(b, 0)])
            for ch in range(8):
                gens = [state_gen(b, ch), inv_gen(b, ch + 1) if ch < 7 else None,
                        out_gen(b, ch - 1) if ch > 0 else None]
                gens = [g for g in gens if g is not None]
                while gens:
                    for g in list(gens):
                        try:
                            next(g)
                        except StopIteration:
                            gens.remove(g)
                    p1 = step(p1, 1)
            run_interleaved([out_gen(b, 7)])
            if p1 is not None:
                run_interleaved([p1])
            for hp in range(4):
                cx.dma("pool", oTd[4 + hp, :, b * 512:(b + 1) * 512], oTblk[b % 2][:, hp, :],
                       r=[oTblk[b % 2]], w=[oTd])
    cx.barrier()


def mlstm_stage(cx, dqkT, di_d, df_d, dv_d, og_d, prm, Bscr, oTd, ident, identf, consts, ntok=T):
    nb = ntok // 512
    nkt = ntok // 128
    LN8 = float(np.log(0.125))
    with ExitStack() as st:
        qk = [cx.sb(st, f"qk{i}", [128, ntok], BF16) for i in range(4)]
        cw = cx.sb(st, "cw", [128, 4, 4], F32)
        for j in range(4):
            cx.dma("sp", cw[:, :, j], prm["conv_w"][j].rearrange("(c p) -> p c", p=128), w=[cw],
                   allow_slow_non_contiguous=True)
        cb = load_col_vec(cx, st, "cb", prm["conv_b"], 512)
        masks = cx.sb(st, "masks", [128, 4, 512], BF16)
        cx.dma("sp", masks[:, :, :], consts["cmask"].t, r=[consts["cmask"]], w=[masks])
        nrm = bcast_load(cx, st, "nrm", prm["norm"], 512)
        epsc = cx.sb(st, "epsc", [128, 1], F32)
        cx.op("dve", lambda e: e.memset(epsc[:, :], EPS), w=[epsc])
        rf = cx.sb(st, "rf", [128, nkt, 4, nb], F32)
        cf = cx.sb(st, "cf", [128, nkt, 4], F32)
        with ExitStack() as st2:
            xb = cx.sb(st2, "xb", [128, ntok + 3], F32)
            yb = cx.sb(st2, "yb", [128, ntok], F32)
            cx.op("dve", lambda e: e.memset(xb[:, 0:3], 0.0), w=[xb])
            for c in range(4):
                cx.dma("sp", xb[:, 3:ntok + 3], dqkT[c, :, :ntok], r=[dqkT], w=[xb])
                cx.op("dve", lambda e: e.tensor_scalar(out=yb[:, :], in0=xb[:, 3:ntok + 3],
                                                       scalar1=cw[:, c, 3:4], scalar2=cb[:, c:c + 1],
                                                       op0=ALU.mult, op1=ALU.add), r=[xb, cw, cb], w=[yb])
                for j in range(3):
                    cx.op("dve", lambda e: e.scalar_tensor_tensor(
                        out=yb[:, :], in0=xb[:, j:ntok + j], scalar=cw[:, c, j:j + 1], in1=yb[:, :],
                        op0=ALU.mult, op1=ALU.add), r=[xb, cw, yb], w=[yb])
                cx.op("act", lambda e: e.activation(out=qk[c][:, :], in_=yb[:, :], func=AF.Silu),
                      r=[yb], w=[qk[c]])
            gi = cx.sb(st2, "gi", [4, ntok], F32)
            gf = cx.sb(st2, "gf", [4, ntok], F32)
            Bn = cx.sb(st2, "Bn", [4, ntok], F32)
            ones4 = cx.sb(st2, "ones4", [4, ntok], F32)
            gb = cx.sb(st2, "gb", [4, 4], F32)
            cx.dma("sp", gb[:, 0:1], prm["ig_b"].rearrange("(p o) -> p o", o=1), w=[gb])
            cx.dma("sp", gb[:, 1:2], prm["fg_b"].rearrange("(p o) -> p o", o=1), w=[gb])
            cx.op("dve", lambda e: e.tensor_scalar(out=gb[:, 2:3], in0=gb[:, 1:2], scalar1=-1.0,
                                                   scalar2=None, op0=ALU.mult), r=[gb], w=[gb])
            cx.dma("sp", gi[:, :], di_d[:, :ntok], r=[di_d], w=[gi])
            cx.dma("sp", gf[:, :], df_d[:, :ntok], r=[df_d], w=[gf])
            cx.op("dve", lambda e: e.memset(ones4[:, :], 1.0), w=[ones4])
            cx.op("act", lambda e: e.activation(out=gf[:, :], in_=gf[:, :], func=AF.Exp, scale=-1.0,
                                                bias=gb[:, 2:3]), r=[gf, gb], w=[gf])
            cx.op("dve", lambda e: e.tensor_scalar(out=gf[:, :], in0=gf[:, :], scalar1=1.0, scalar2=None,
                                                   op0=ALU.add), r=[gf], w=[gf])
            cx.op("act", lambda e: e.activation(out=gf[:, :], in_=gf[:, :], func=AF.Ln), r=[gf], w=[gf])
            cx.op("dve", lambda e: e.tensor_tensor_scan(out=Bn[:, :], data0=ones4[:, :], data1=gf[:, :],
                                                        initial=0.0, op0=ALU.mult, op1=ALU.add),
                  r=[ones4, gf], w=[Bn])
            cx.op("dve", lambda e: e.scalar_tensor_tensor(out=gi[:, :], in0=gi[:, :], scalar=gb[:, 0:1],
                                                          in1=Bn[:, :], op0=ALU.add, op1=ALU.add),
                  r=[gi, gb, Bn], w=[gi])
            cx.dma("pool", Bscr[:, :ntok], Bn[:, :], r=[Bn], w=[Bscr])
            bref = cx.sb(st2, "bref", [128, 4, nb], F32)
            bneg = cx.sb(st2, "bneg", [128, 4, nb], F32)
            bpos = cx.sb(st2, "bpos", [128, 4, nb], F32)
            cx.op("dve", lambda e: e.memset(bref[:, :, :], 0.0), w=[bref])
            if nb > 1:
                for h in range(4):
                    cx.dma("sp", bref[:, h, 1:nb], Bscr[h, 511:ntok - 1:512].partition_broadcast(128),
                           r=[Bscr], w=[bref], allow_slow_non_contiguous=True)
            cx.op("dve", lambda e: e.tensor_scalar(out=bneg[:, :, :], in0=bref[:, :, :], scalar1=-1.0,
                                                   scalar2=None, op0=ALU.mult), r=[bref], w=[bneg])
            cx.op("dve", lambda e: e.tensor_scalar(out=bpos[:, :, :], in0=bref[:, :, :], scalar1=LN8,
                                                   scalar2=None, op0=ALU.add), r=[bref], w=[bpos])
            uT = cx.sb(st2, "uT", [128, nkt, 4], F32)
            BnT = cx.sb(st2, "BnT", [128, nkt, 4], F32)
            ptr = cx.ps(st2, "ptr", [128, 512])
            for src, dst in ((gi, uT), (Bn, BnT)):
                for i in range(nkt):
                    cx.op("pe", lambda e, i=i: e.transpose(out=ptr[:, i * 4:(i + 1) * 4],
                                                           in_=src[0:4, i * 128:(i + 1) * 128],
                                                           identity=identf[0:4, 0:4]),
                          r=[src, identf], w=[ptr])
                cx.op("dve", lambda e: e.tensor_copy(
                    out=dst[:, :, :], in_=ptr[:, 0:nkt * 4].rearrange("p (i h) -> p i h", h=4)),
                    r=[ptr], w=[dst])
            for h in range(4):
                for j in range(nb):
                    ni = 4 * j + 4
                    cx.op("act", lambda e: e.activation(out=rf[:, 0:ni, h, j], in_=uT[:, 0:ni, h],
                                                        func=AF.Exp, bias=bneg[:, h, j:j + 1]),
                          r=[uT, bneg], w=[rf])
                    cx.op("act", lambda e: e.activation(out=cf[:, 4 * j:4 * j + 4, h],
                                                        in_=BnT[:, 4 * j:4 * j + 4, h], func=AF.Exp,
                                                        scale=-1.0, bias=bpos[:, h, j:j + 1]),
                          r=[BnT, bpos], w=[cf])
        cx.barrier()
        Vs = [cx.sb(st, f"V{i}", [128, nkt, 129], BF16) for i in range(2)]
        for v in Vs:
            cx.op("dve", lambda e, v=v: e.memset(v[:, :, 128:129], 1.0), w=[v])
        ogs = [cx.sb(st, f"og{i}", [128, 4, 128], F32) for i in range(2)]
        fs = cx.sb(st, "fs", [128, 16], F32)
        hh4 = cx.sb(st, "hh4", [128, 4, 128], F32)
        sq4 = cx.sb(st, "sq4", [128, 4, 128], F32)
        yb4 = cx.sb(st, "yb4", [128, 4, 128], BF16)
        mhalf = cx.sb(st, "mhalf", [128, 4], F32)
        cx.op("dve", lambda e: e.memset(mhalf[:, :], -0.5), w=[mhalf])
        oTs = [cx.sb(st, f"oTs{i}", [128, 512], BF16) for i in range(2)]
        tpf = cx.ps(st, "tpf", [128, 512], BF16)
        for h in range(4):
            v_sb = Vs[h % 2]
            cx.dma("sp", v_sb[:, :, 0:128],
                   dv_d[:ntok, h * 128:(h + 1) * 128].rearrange("(i p) e -> p i e", p=128),
                   r=[dv_d], w=[v_sb])
            q_sb = qk[h // 2]
            k_sb = qk[2 + h // 2]
            pr = slice((h % 2) * 64, (h % 2) * 64 + 64)
            units = []
            for j in range(nb):
                tiles = []
                for (i, mid, subs) in causal_tiles(j):
                    tiles.append(dict(k=k_sb[pr, i * 128:(i + 1) * 128], v=v_sb[:, i, :],
                                      r=[k_sb, v_sb],
                                      mask=(masks[:, mid, :], masks) if mid is not None else None,
                                      rowfac=(rf[:, i, h, j:j + 1], rf), subs=subs))

                def fin(O0, O1, j=j, h=h):
                    og = ogs[j % 2]
                    oT_sb = oTs[j % 2]
                    cx.dma("sp", og[:, :, :],
                           og_d[j * 512:(j + 1) * 512, h * 128:(h + 1) * 128].rearrange(
                               "(s p) e -> p s e", p=128), r=[og_d], w=[og])
                    for bi, O in enumerate((O0, O1)):
                        it0 = 4 * j + 2 * bi
                        cfv = cf[:, it0:it0 + 2, h]
                        f0, f1, f2 = fs[:, 0:2], fs[:, 2:4], fs[:, 4:6]
                        cx.op("dve", lambda e: e.tensor_tensor(out=f0, in0=O[:, :, 128], in1=cfv, op=ALU.mult),
                              r=[O, cf], w=[fs])
                        cx.op("dve", lambda e: e.tensor_scalar(out=f1, in0=f0, scalar1=-1.0, scalar2=1.0,
                                                               op0=ALU.mult, op1=ALU.max), r=[fs], w=[fs])
                        cx.op("dve", lambda e: e.tensor_tensor(out=f1, in0=f1, in1=f0, op=ALU.max), r=[fs], w=[fs])
                        cx.op("dve", lambda e: e.reciprocal(out=f2, in_=f1), r=[fs], w=[fs])
                        cx.op("dve", lambda e: e.tensor_tensor(out=f2, in0=f2, in1=cfv, op=ALU.mult),
                              r=[fs, cf], w=[fs])
                        cx.op("dve", lambda e: e.tensor_tensor(
                            out=hh4[:, 2 * bi:2 * bi + 2, :], in0=O[:, :, 0:128],
                            in1=f2.unsqueeze(2).to_broadcast([128, 2, 128]), op=ALU.mult), r=[O, fs], w=[hh4])
                    cx.op("pool", lambda e: e.tensor_tensor(out=sq4[:, :, :], in0=hh4[:, :, :], in1=hh4[:, :, :],
                                                            op=ALU.mult), r=[hh4], w=[sq4])
                    cx.op("dve", lambda e: e.reduce_sum(out=fs[:, 8:12], in_=sq4[:, :, :], axis=AX.X),
                          r=[sq4], w=[fs])
                    cx.op("dve", lambda e: e.tensor_scalar(out=fs[:, 8:12], in0=fs[:, 8:12], scalar1=1.0 / 128,
                                                           scalar2=EPS, op0=ALU.mult, op1=ALU.add), r=[fs], w=[fs])
                    cx.op("pool", lambda e: e.tensor_tensor(out=fs[:, 12:16], in0=fs[:, 8:12], in1=mhalf[:, 0:4],
                                                            op=ALU.pow), r=[fs, mhalf], w=[fs])
                    cx.op("dve", lambda e: e.tensor_tensor(
                        out=hh4[:, :, :], in0=hh4[:, :, :],
                        in1=fs[:, 12:16].unsqueeze(2).to_broadcast([128, 4, 128]), op=ALU.mult),
                        r=[hh4, fs], w=[hh4])
                    cx.op("pool", lambda e: e.tensor_tensor(
                        out=hh4[:, :, :], in0=hh4[:, :, :],
                        in1=nrm[:, h * 128:(h + 1) * 128].unsqueeze(1).to_broadcast([128, 4, 128]), op=ALU.mult),
                        r=[hh4, nrm], w=[hh4])
                    cx.op("dve", lambda e: e.tensor_tensor(out=yb4[:, :, :], in0=hh4[:, :, :], in1=og[:, :, :],
                                                           op=ALU.mult), r=[hh4, og], w=[yb4])
                    for s in range(4):
                        cx.op("pe", lambda e: e.transpose(out=tpf[:, s * 128:(s + 1) * 128], in_=yb4[:, s, :],
                                                          identity=ident[:, :]), r=[yb4, ident], w=[tpf])
                    cx.op("dve", lambda e: e.tensor_copy(out=oT_sb[:, :], in_=tpf[:, :]), r=[tpf], w=[oT_sb])
                    cx.dma("pool", oTd[4 + h, :, j * 512:(j + 1) * 512], oT_sb[:, :], r=[oT_sb], w=[oTd])
                units.append(dict(q=q_sb[pr, j * 512:(j + 1) * 512], qr=[q_sb], tiles=tiles, fin=fin))
            attn_run_shared(cx, st, units, 129, "scale")
    cx.barrier()


NCMP = 255


def nsa_compress_stage(cx, zT_d, pe_ap, w1_ap, w2a_ap, w2b_ap, is_k, out_d, consts):
    with ExitStack() as st:
        zT = cx.sb(st, "zT", [128, T], F32)
        cx.dma("sp", zT[:, :], zT_d[:, :], r=[zT_d], w=[zT])
        pe_c = cx.sb(st, "pe_c", [128, 32], F32)
        cx.dma("sp", pe_c[:, :], pe_ap, w=[pe_c])
        w1s = [cx.sb(st, f"w1s{i}", [128, 8, 256], F32) for i in range(2)]
        w1 = cx.sb(st, "w1", [128, 32, 256], BF16)
        w1v = w1_ap.rearrange("(j d) h -> d j h", d=64)
        for q4 in range(4):
            sg = w1s[q4 % 2]
            for g in range(2):
                cx.dma("sp", sg[g * 64:(g + 1) * 64, :, :], w1v[:, q4 * 8:(q4 + 1) * 8, :], w=[sg])
            cx.op("pool", lambda e: e.tensor_copy(out=w1[:, q4 * 8:(q4 + 1) * 8, :], in_=sg[:, :, :]),
                  r=[sg], w=[w1])
        nw2 = 128 if is_k else 64
        w2s = cx.sb(st, "w2s", [128, 2, 2, 128], F32)
        w2 = cx.sb(st, "w2", [128, 2, 2, 128], BF16)
        cx.dma("sp", w2s[:, 0, :, 0:nw2], w2a_ap.rearrange("(c p) n -> p c n", p=128), w=[w2s])
        if is_k:
            cx.dma("sp", w2s[:, 1, :, 0:nw2], w2b_ap.rearrange("(c p) n -> p c n", p=128), w=[w2s])
        else:
            cx.op("dve", lambda e: e.memset(w2s[:, 1, :, :], 0.0), w=[w2s])
            cx.op("dve", lambda e: e.memset(w2s[:, 0, :, 64:128], 0.0), w=[w2s])
        cx.op("dve", lambda e: e.tensor_copy(out=w2[:, :, :, :], in_=w2s[:, :, :, :]), r=[w2s], w=[w2])
        zpe = cx.sb(st, "zpe", [128, 32, 256], BF16)
        zv = zT[:, :].rearrange("p (n s) -> p n s", s=16)
        for j in range(32):
            src = zv[:, 0:255, j] if j < 16 else zv[:, 1:256, j - 16]
            eng = "dve" if j % 2 == 0 else "pool"
            cx.op(eng, lambda e: e.tensor_scalar(out=zpe[:, j, 0:255], in0=src, scalar1=pe_c[:, j:j + 1],
                                                 scalar2=None, op0=ALU.add), r=[zT, pe_c], w=[zpe])
        H = [[cx.ps(st, f"H{hc}{g}", [128, 512]) for g in range(2)] for hc in range(2)]
        gel = cx.sb(st, "gel", [128, 2, 2, 256], BF16)
        cx.op("dve", lambda e: e.memset(gel[:, :, :, :], 0.0), w=[gel])
        t1 = cx.sb(st, "t1", [128, 256], F32)
        t2 = cx.sb(st, "t2", [128, 256], F32)
        for g in range(2):
            pr = slice(g * 64, (g + 1) * 64)
            for hc in range(2):
                for j in range(32):
                    cx.op("pe", lambda e: e.matmul(H[hc][g][:, 0:255], lhsT=w1[pr, j, hc * 128:(hc + 1) * 128],
                                                   rhs=zpe[pr, j, 0:255], start=(j == 0), stop=(j == 31)),
                          r=[w1, zpe], w=[H[hc][g]], rows=(g * 64, 64))
                x = H[hc][g]
                cx.op("act", lambda e: e.activation(out=t1[:, 0:255], in_=x[:, 0:255], func=AF.Square),
                      r=[x], w=[t1])
                cx.op("dve", lambda e: e.tensor_scalar(out=t1[:, 0:255], in0=t1[:, 0:255], scalar1=0.044715,
                                                       scalar2=1.0, op0=ALU.mult, op1=ALU.add), r=[t1], w=[t1])
                cx.op("dve", lambda e: e.tensor_tensor(out=t1[:, 0:255], in0=x[:, 0:255], in1=t1[:, 0:255],
                                                       op=ALU.mult), r=[x, t1], w=[t1])
                cx.op("act", lambda e: e.activation(out=t2[:, 0:255], in_=t1[:, 0:255], func=AF.Tanh,
                                                    scale=0.7978845608028654), r=[t1], w=[t2])
                cx.op("dve", lambda e: e.tensor_scalar(out=t2[:, 0:255], in0=t2[:, 0:255], scalar1=0.5,
                                                       scalar2=0.5, op0=ALU.mult, op1=ALU.add), r=[t2], w=[t2])
                cx.op("dve", lambda e: e.tensor_tensor(out=gel[:, hc, g, 0:255], in0=x[:, 0:255],
                                                       in1=t2[:, 0:255], op=ALU.mult), r=[x, t2], w=[gel])
        if is_k:
            ccos = cx.sb(st, "ccos", [128, 256], F32)
            csin = cx.sb(st, "csin", [128, 256], F32)
            cx.dma("sp", ccos[:, :], consts["ccos"].t, r=[consts["ccos"]], w=[ccos])
            cx.dma("sp", csin[:, :], consts["csin"].t, r=[consts["csin"]], w=[csin])
            for g in range(2):
                A, Bp = H[0][g], H[1][g]
                for var, dst in ((0, A), (1, Bp)):
                    for hc in range(2):
                        cx.op("pe", lambda e: e.matmul(dst[:, 0:255], lhsT=w2[:, var, hc, :],
                                                       rhs=gel[:, hc, g, 0:255], start=(hc == 0),
                                                       stop=(hc == 1)), r=[w2, gel], w=[dst])
                kc = cx.sb(st, f"kc{g}", [128, 256], BF16)
                cx.op("dve", lambda e: e.memset(kc[:, :], 0.0), w=[kc])
                cx.op("dve", lambda e: e.tensor_tensor(out=t1[:, 0:255], in0=A[:, 0:255], in1=ccos[:, 0:255],
                                                       op=ALU.mult), r=[A, ccos], w=[t1])
                cx.op("dve", lambda e: e.tensor_tensor(out=t2[:, 0:255], in0=Bp[:, 0:255], in1=csin[:, 0:255],
                                                       op=ALU.mult), r=[Bp, csin], w=[t2])
                cx.op("pool", lambda e: e.tensor_tensor(out=kc[:, 0:255], in0=t1[:, 0:255], in1=t2[:, 0:255],
                                                        op=ALU.add), r=[t1, t2], w=[kc])
                cx.dma("pool", out_d[g, :, :], kc[:, :], r=[kc], w=[out_d])
        else:
            for g in range(2):
                for nt in range(2):
                    m = 128 if nt == 0 else 127
                    C = H[nt][g]
                    for hc in range(2):
                        cx.op("pe", lambda e: e.matmul(C[0:m, 0:64], lhsT=gel[:, hc, g, nt * 128:nt * 128 + m],
                                                       rhs=w2[:, 0, hc, 0:64], start=(hc == 0), stop=(hc == 1)),
                              r=[w2, gel], w=[C])
                    vc = cx.sb(st, f"vc{g}{nt}", [128, 64], BF16)
                    cx.op("dve", lambda e: e.memset(vc[:, :], 0.0), w=[vc])
                    cx.op("act", lambda e: e.copy(out=vc[0:m, :], in_=C[0:m, 0:64]), r=[C], w=[vc])
                    cx.dma("pool", out_d[g, nt * 128:(nt + 1) * 128, :], vc[:, :], r=[vc], w=[out_d])
    cx.barrier()


def _store_gated(cx, O0, O1, E, gates, gcol, j, stg, dst_d, h, extra=None):
    fs = stg["fs"]
    ob = stg["ob"][stg["k"][0] % 2]
    stg["k"][0] += 1
    for bi, O in enumerate((O0, O1)):
        it0 = 4 * j + 2 * bi
        f0, f1, f2 = fs[:, bi, 0:2], fs[:, bi, 2:4], fs[:, bi, 4:6]
        cx.op("dve", lambda e: e.tensor_scalar(out=f0, in0=O[:, :, E], scalar1=1e-30, scalar2=None,
                                               op0=ALU.max), r=[O], w=[fs])
        cx.op("dve", lambda e: e.reciprocal(out=f1, in_=f0), r=[fs], w=[fs])
        cx.op("dve", lambda e: e.tensor_tensor(out=f2, in0=f1, in1=gates[:, it0:it0 + 2, gcol], op=ALU.mult),
              r=[fs, gates], w=[fs])
        cx.op("dve", lambda e: e.tensor_tensor(out=ob[:, 2 * bi:2 * bi + 2, :], in0=O[:, :, 0:E],
                                               in1=f2.unsqueeze(2).to_broadcast([128, 2, E]), op=ALU.mult),
              r=[O, fs], w=[ob])
        if extra is not None:
            extra(bi, O, f1)
    cx.dma("pool", dst_d[j * 512:(j + 1) * 512, h * 64:(h + 1) * 64].rearrange("(s p) e -> p s e", p=128),
           ob[:, :, :], r=[ob], w=[dst_d])


def nsa_cmp_stage(cx, qTn, kcmp_d, vcmp_d, gates_d, ocmp_d, addm_d, ident, consts):
    nb = T // 512
    with ExitStack() as st:
        gates = cx.sb(st, "gates", [128, 32, 24], F32)
        cx.dma("sp", gates[:, :, :], gates_d[:, :].rearrange("(i p) c -> p i c", p=128), r=[gates_d], w=[gates])
        cmask = cx.sb(st, "cmpmask", [128, 2, 8, 512], BF16)
        cx.dma("sp", cmask[:, 0, :, :], consts["cmpmask"][:, 0, :, :], r=[consts["cmpmask"]], w=[cmask])
        cx.dma("sp", cmask[:, 1, :, :], consts["cmpmask"][:, 1, :, :], r=[consts["cmpmask"]], w=[cmask])
        stg = dict(fs=cx.sb(st, "fs", [128, 2, 8], F32),
                   ob=[cx.sb(st, f"ob{i}", [128, 4, 64], F32) for i in range(2)], k=[0])
        qs = [cx.sb(st, f"q{i}", [128, T], BF16) for i in range(2)]
        kc = cx.sb(st, "kc", [128, 256], BF16)
        va = cx.sb(st, "va", [128, 2, 129], BF16)
        impacc = cx.sb(st, "impacc", [128, 4, 64], F32)
        imptmp = cx.sb(st, "imptmp", [128, 2, 64], F32)
        selb = [cx.sb(st, f"selb{i}", [128, 4, 64], F32) for i in range(2)]
        sc = cx.sb(st, "sc", [128, 64], F32)
        sc2 = cx.sb(st, "sc2", [128, 64], F32)
        m8 = cx.sb(st, "m8", [128, 16], F32)
        am = cx.sb(st, "am", [128, 128], BF16)
        amT = [cx.sb(st, f"amT{i}", [128, 512], BF16) for i in range(2)]
        tpf = cx.ps(st, "tpf", [128, 512], BF16)
        for g in range(2):
            cx.dma("sp", kc[:, :], kcmp_d[g, :, :], r=[kcmp_d], w=[kc])
            cx.op("dve", lambda e: e.memset(va[:, :, 64:65], 1.0), w=[va])
            cx.dma("sp", va[:, :, 0:64], vcmp_d[g, :, :].rearrange("(i p) e -> p i e", p=128), r=[vcmp_d], w=[va])
            cx.dma("sp", va[:, :, 65:129], consts["selmap"][:, :].rearrange("(i p) e -> p i e", p=128),
                   r=[consts["selmap"]], w=[va])
            for c in range(2):
                cx.dma("sp", qs[c][:, :], qTn[2 * g + c, :, :], r=[qTn], w=[qs[c]])
            units = []
            for j in range(nb):
                for hg in range(4):
                    h = 4 * g + hg
                    pr = slice((h % 2) * 64, (h % 2) * 64 + 64)
                    q_sb = qs[hg // 2]
                    tiles = []
                    for nt in range(2):
                        if nt == 1 and j < 4:
                            continue
                        tiles.append(dict(k=kc[pr, nt * 128:(nt + 1) * 128], v=va[:, nt, :], r=[kc, va],
                                          mask=(cmask[:, nt, j, :], cmask), subs=[True] * 4))

                    def fin(O0, O1, j=j, hg=hg, h=h, g=g):
                        if hg == 0:
                            sb_ = selb[j % 2]
                            cx.dma("sp", sb_[:, :, :],
                                   consts["selbias"][j * 512:(j + 1) * 512, :].rearrange("(s p) b -> p s b", p=128),
                                   r=[consts["selbias"]], w=[sb_])

                        def extra(bi, O, f1):
                            dstv = impacc[:, 2 * bi:2 * bi + 2, :]
                            rb = f1.unsqueeze(2).to_broadcast([128, 2, 64])
                            if hg == 0:
                                cx.op("dve", lambda e: e.tensor_tensor(out=dstv, in0=O[:, :, 65:129], in1=rb,
                                                                       op=ALU.mult), r=[O, stg["fs"]], w=[impacc])
                            else:
                                cx.op("dve", lambda e: e.tensor_tensor(out=imptmp[:, :, :], in0=O[:, :, 65:129],
                                                                       in1=rb, op=ALU.mult),
                                      r=[O, stg["fs"]], w=[imptmp])
                                cx.op("pool", lambda e: e.tensor_tensor(out=dstv, in0=dstv, in1=imptmp[:, :, :],
                                                                        op=ALU.add), r=[impacc, imptmp], w=[impacc])
                        _store_gated(cx, O0, O1, 64, gates, h * 3 + 0, j, stg, ocmp_d, h, extra=extra)
                        if hg == 3:
                            sb_ = selb[j % 2]
                            aT = amT[j % 2]
                            for s in range(4):
                                cx.op("dve", lambda e: e.tensor_tensor(out=sc[:, :], in0=impacc[:, s, :],
                                                                       in1=sb_[:, s, :], op=ALU.add),
                                      r=[impacc, sb_], w=[sc])
                                cx.op("dve", lambda e: e.max(out=m8[:, 0:8], in_=sc[:, :]), r=[sc], w=[m8])
                                cx.op("dve", lambda e: e.match_replace(out=sc2[:, :], in_to_replace=m8[:, 0:8],
                                                                       in_values=sc[:, :], imm_value=-3e38),
                                      r=[sc, m8], w=[sc2])
                                cx.op("dve", lambda e: e.max(out=m8[:, 8:16], in_=sc2[:, :]), r=[sc2], w=[m8])
                                cx.op("dve", lambda e: e.tensor_scalar(out=sc2[:, :], in0=sc[:, :],
                                                                       scalar1=m8[:, 15:16], scalar2=30000.0,
                                                                       op0=ALU.is_ge, op1=ALU.mult),
                                      r=[sc, m8], w=[sc2])
                                for dpl in range(2):
                                    cx.op("dve", lambda e: e.tensor_scalar(
                                        out=am[:, dpl * 64:(dpl + 1) * 64], in0=sc2[:, :], scalar1=-30000.0,
                                        scalar2=None, op0=ALU.add), r=[sc2], w=[am])
                                cx.op("pe", lambda e: e.transpose(out=tpf[:, s * 128:(s + 1) * 128], in_=am[:, :],
                                                                  identity=ident[:, :]), r=[am, ident], w=[tpf])
                            cx.op("act", lambda e: e.copy(out=aT[:, :], in_=tpf[:, :]), r=[tpf], w=[aT])
                            cx.dma("pool", addm_d[g, :, j * 512:(j + 1) * 512], aT[:, :], r=[aT], w=[addm_d])
                    units.append(dict(q=q_sb[pr, j * 512:(j + 1) * 512], qr=[q_sb], tiles=tiles, fin=fin))
            attn_run_shared(cx, st, units, 129, "exp", scale=0.125)
    cx.barrier()


def nsa_kv_stage(cx, qTn, kT_d, v_d, gates_d, out_d, kind, consts, addm_d=None):
    nb = T // 512
    br = 1 if kind == "slc" else 2
    with ExitStack() as st:
        gates = cx.sb(st, "gates", [128, 32, 24], F32)
        cx.dma("sp", gates[:, :, :], gates_d[:, :].rearrange("(i p) c -> p i c", p=128), r=[gates_d], w=[gates])
        masks = cx.sb(st, "masks", [128, 4, 512], BF16)
        cx.dma("sp", masks[:, :, :], consts["cmask"].t, r=[consts["cmask"]], w=[masks])
        if kind == "win":
            wmasks = cx.sb(st, "wmasks", [128, 4, 512], BF16)
            cx.dma("sp", wmasks[:, :, :], consts["wmask"].t, r=[consts["wmask"]], w=[wmasks])
        else:
            eexp = cx.sb(st, "eexp", [128, T], BF16)
            cx.dma("sp", eexp[:, :], consts["eexp"].t, r=[consts["eexp"]], w=[eexp])
            addm = cx.sb(st, "addm", [128, T], BF16)
        stg = dict(fs=cx.sb(st, "fs", [128, 2, 8], F32),
                   ob=[cx.sb(st, f"ob{i}", [128, 4, 64], F32) for i in range(2)], k=[0])
        qs = [cx.sb(st, f"q{i}", [128, T], BF16) for i in range(2)]
        kT = cx.sb(st, "kT", [128, T], BF16)
        va = cx.sb(st, "va", [128, 32, 65], BF16)
        cx.op("dve", lambda e: e.memset(va[:, :, 64:65], 1.0), w=[va])
        for g in range(2):
            cx.dma("sp", kT[:, :], kT_d[g, :, :], r=[kT_d], w=[kT])
            cx.dma("sp", va[:, :, 0:64], v_d[:, g * 64:(g + 1) * 64].rearrange("(i p) e -> p i e", p=128),
                   r=[v_d], w=[va])
            if kind == "slc":
                cx.dma("sp", addm[:, :], addm_d[g, :, :], r=[addm_d], w=[addm])
            for c in range(2):
                cx.dma("sp", qs[c][:, :], qTn[2 * g + c, :, :], r=[qTn], w=[qs[c]])
            units = []
            for hg in range(4):
                h = 4 * g + hg
                pr = slice((h % 2) * 64, (h % 2) * 64 + 64)
                q_sb = qs[hg // 2]
                for j in range(nb):
                    tiles = []
                    if kind == "slc":
                        for (i, mid, subs) in causal_tiles(j):
                            tiles.append(dict(
                                k=kT[pr, i * 128:(i + 1) * 128], v=va[:, i, :], r=[kT, va],
                                mask=(masks[:, mid, :], masks) if mid is not None else None,
                                add=(eexp[pr, i * 128:(i + 1) * 128], addm[pr, j * 512:(j + 1) * 512],
                                     [eexp, addm]), subs=subs))
                    else:
                        for i in range(max(0, 4 * j - 4), 4 * j + 4):
                            dlt = i - 4 * j
                            if dlt >= 0:
                                mk, subs = masks[:, dlt, :], [s >= dlt for s in range(4)]
                                mb = masks
                            else:
                                mk, subs = wmasks[:, dlt + 4, :], [s <= dlt + 4 for s in range(4)]
                                mb = wmasks
                            tiles.append(dict(k=kT[pr, i * 128:(i + 1) * 128], v=va[:, i, :], r=[kT, va],
                                              mask=(mk, mb), subs=subs))

                    def fin(O0, O1, j=j, h=h):
                        _store_gated(cx, O0, O1, 64, gates, h * 3 + br, j, stg, out_d, h)
                    units.append(dict(q=q_sb[pr, j * 512:(j + 1) * 512], qr=[q_sb], tiles=tiles, fin=fin,
                                      rows=(pr.start, 64)))
            attn_run_shared(cx, st, units, 65, "exp", scale=0.125)
    cx.barrier()


def nsa_combine_stage(cx, parts, oTd, ident):
    with ExitStack() as st:
        acc = [cx.sb(st, f"acc{i}", [128, 4, 512], F32) for i in range(2)]
        tmp = [cx.sb(st, f"tmp{i}", [128, 4, 512], F32) for i in range(2)]
        ab = cx.sb(st, "ab", [128, 4, 512], BF16)
        oTs = [cx.sb(st, f"oTs{i}", [128, 4, 512], BF16) for i in range(2)]
        tpf = [cx.ps(st, f"tpf{i}", [128, 1024], BF16) for i in range(2)]
        for b in range(T // 512):
            a = acc[b % 2]
            tb = b * 512
            cx.dma("sp", a[:, :, :], parts[0][tb:tb + 512, :].rearrange("(s p) e -> p s e", p=128),
                   r=[parts[0]], w=[a])
            for k in (1, 2):
                t = tmp[k % 2]
                cx.dma("sp", t[:, :, :], parts[k][tb:tb + 512, :].rearrange("(s p) e -> p s e", p=128),
                       r=[parts[k]], w=[t])
                eng = "dve" if k == 1 else "pool"
                cx.op(eng, lambda e: e.tensor_tensor(out=a[:, :, :], in0=a[:, :, :], in1=t[:, :, :], op=ALU.add),
                      r=[a, t], w=[a])
            cx.op("act", lambda e: e.copy(out=ab[:, :, :], in_=a[:, :, :]), r=[a], w=[ab])
            o = oTs[b % 2]
            for c in range(4):
                tp = tpf[c % 2]
                for s in range(4):
                    cx.op("pe", lambda e: e.transpose(out=tp[:, s * 128:(s + 1) * 128],
                                                      in_=ab[:, s, c * 128:(c + 1) * 128], identity=ident[:, :]),
                          r=[ab, ident], w=[tp])
                cx.op("act", lambda e: e.copy(out=o[:, c, :], in_=tp[:, 0:512]), r=[tp], w=[o])
            for c in range(4):
                cx.dma("pool", oTd[c, :, tb:tb + 512], o[:, c, :], r=[o], w=[oTd])
    cx.barrier()


def final_norm_stage(cx, x_in, w_ap, out_d):
    with ExitStack() as st:
        wrow = bcast_load(cx, st, "fnw", w_ap, D)
        xts = [cx.sb(st, f"xt{i}", [128, D], F32) for i in range(3)]
        junk = cx.sb(st, "junk", [128, D], BF16)
        sss = [cx.sb(st, f"ss{i}", [128, 4], F32) for i in range(2)]
        for s in sss:
            cx.op("dve", lambda e, s=s: e.memset(s[:, :], EPS), w=[s])
        for i in range(T // 128):
            xt = xts[i % 3]
            ss = sss[i % 2]
            cx.dma("sp", xt[:, :], x_in[i * 128:(i + 1) * 128, :], r=[x_in], w=[xt])
            cx.op("act", lambda e: e.activation(out=junk[:, :], in_=xt[:, :], func=AF.Square,
                                                accum_out=ss[:, 0:1]), r=[xt], w=[junk, ss])
            cx.op("act", lambda e: e.activation(out=ss[:, 1:2], in_=ss[:, 0:1], func=AF.Sqrt,
                                                scale=1.0 / D, bias=ss[:, 3:4]), r=[ss], w=[ss])
            cx.op("dve", lambda e: e.reciprocal(out=ss[:, 2:3], in_=ss[:, 1:2]), r=[ss], w=[ss])
            cx.op("dve", lambda e: e.scalar_tensor_tensor(out=xt[:, :], in0=xt[:, :], scalar=ss[:, 2:3],
                                                          in1=wrow[:, :], op0=ALU.mult, op1=ALU.mult),
                  r=[xt, ss, wrow], w=[xt])
            cx.dma("pool", out_d[i * 128:(i + 1) * 128, :], xt[:, :], r=[xt], w=[out_d])
    cx.barrier()


IN_SPECS = {
    "x": [T, D], "ffa_norm": [2, D], "ffa_gate": [2, D, DFF], "ffa_up": [2, D, DFF], "ffa_down": [2, DFF, D],
    "mix_norm": [2, D], "ffb_norm": [2, D], "ffb_gate": [2, D, DFF], "ffb_up": [2, D, DFF],
    "ffb_down": [2, DFF, D], "ab_w_in": [D, 3328], "ab_w_rot": [D, 1024], "ab_w_out": [D, D],
    "diff_lam": [256], "diff_subln": [128], "rwkv_mu": [1792], "rwkv_w0": [512], "rwkv_w2": [64, 512],
    "rwkv_a0": [512], "rwkv_a2": [64, 512], "rwkv_g2": [128, 512], "rwkv_kk": [512], "rwkv_ka": [512],
    "rwkv_rk": [512], "rwkv_lnw": [512], "rwkv_lnb": [512], "cd_w_in": [D, 2848], "cd_w_ext": [D, 1536],
    "cd_w_out": [D, D], "nsa_pe_kT": [128, 32], "nsa_w1_k": [2048, 256], "nsa_w2_k_dup": [256, 128],
    "nsa_w2_k_rot": [256, 128], "nsa_pe_vT": [128, 32], "nsa_w1_v": [2048, 256], "nsa_w2_v": [256, 64],
    "mlstm_conv_w": [4, 512], "mlstm_conv_b": [512], "mlstm_ig_b": [4], "mlstm_fg_b": [4],
    "mlstm_norm": [512], "final_norm": [D],
}
CONST_SPECS = {
    "c_cos": ([128, T], F32), "c_sin": ([128, T], F32), "c_cmask": ([128, 4, 512], BF16),
    "c_wmask": ([128, 4, 512], BF16), "c_ccos": ([128, 256], F32), "c_csin": ([128, 256], F32),
    "c_cmpmask": ([128, 2, 8, 512], BF16), "c_selmap": ([256, 64], BF16), "c_selbias": ([T, 64], F32),
    "c_eexp": ([128, T], BF16),
}
LAM_INIT0 = 0.8 - 0.6 * float(np.exp(-0.3 * 0))


def build_program(stop_after=None, only=None):
    nc = bass.Bass("TRN2", target_bir_lowering=False)
    with ExitStack() as es:
        cx = Ctx(nc, es)
        I = {k: cx.dram(k, v, F32, kind="ExternalInput") for k, v in IN_SPECS.items()}
        consts = {k[2:]: cx.dram(k, sh, dt, kind="ExternalInput") for k, (sh, dt) in CONST_SPECS.items()}
        out = cx.dram("out", [T, D], F32, kind="ExternalOutput")
        xA = cx.dram("xA", [T, D], F32)
        xB = cx.dram("xB", [T, D], F32)
        ident, identf = make_ident(cx, es)

        def done(x_cur):
            if only is None or "final" in only:
                final_norm_stage(cx, x_cur, I["final_norm"].t, out)
            print("nins", cx.nins)
            return nc

        cnt = {}

        def en(name):
            cnt[name] = cnt.get(name, 0) + 1
            full = f"{name}{cnt[name]}"
            return only is None or full in only

        if en("ffn"):
          ffn_stage(cx, I["x"], xA, I["ffa_norm"].t[0], I["ffa_gate"].t[0], I["ffa_up"].t[0],
                  I["ffa_down"].t[0], ident)
        qTa = cx.dram("qTa", [4, 128, T], BF16)
        kTa = cx.dram("kTa", [4, 128, T], BF16)
        Va = cx.dram("Va", [T, 512], BF16)
        pTb = cx.dram("pTb", [14, 128, T], F32)
        oTab = cx.dram("oTab", [8, 128, T], BF16)
        specs = []
        for h in range(4):
            specs.append(dict(kind="rope", a=h * 128, b=3328 + h * 128, dst=qTa.t[h]))
            specs.append(dict(kind="rope", a=512 + h * 128, b=3328 + 512 + h * 128, dst=kTa.t[h]))
        specs.append(dict(kind="tm", a=1024, n=512, dst=Va.t))
        for c in range(14):
            specs.append(dict(kind="raw", a=1536 + c * 128, n=128, dst=pTb.t[c]))
        if en("proj"):
          proj_stage(cx, xA, I["mix_norm"].t[0], [(I["ab_w_in"].t, 3328), (I["ab_w_rot"].t, 1024)], specs,
                   ident, [qTa, kTa, Va, pTb], consts)
        if en("diffattn"):
          diffattn_stage(cx, qTa, kTa, Va, I["diff_lam"].t, I["diff_subln"].t, oTab, ident, consts, LAM_INIT0)
        prm = {n: I["rwkv_" + n].t for n in ("mu", "w0", "w2", "a0", "a2", "g2", "kk", "ka", "rk", "lnw", "lnb")}
        if en("rwkv"):
          rwkv_stage(cx, pTb, prm, oTab, ident, identf)
        if en("outproj"):
          outproj_stage(cx, oTab, I["ab_w_out"].t, xA, xB)
        if en("ffn"):
          ffn_stage(cx, xB, xA, I["ffb_norm"].t[0], I["ffb_gate"].t[0], I["ffb_up"].t[0],
                  I["ffb_down"].t[0], ident)
        if stop_after == "layer0":
            return done(xA)
        if en("ffn"):
          ffn_stage(cx, xA, xB, I["ffa_norm"].t[1], I["ffa_gate"].t[1], I["ffa_up"].t[1],
                  I["ffa_down"].t[1], ident)
        qTn = cx.dram("qTn", [4, 128, T], BF16)
        ksT = cx.dram("ksT", [2, 128, T], BF16)
        kwT = cx.dram("kwT", [2, 128, T], BF16)
        kcT = cx.dram("kcT", [128, T], F32)
        vcT = cx.dram("vcT", [128, T], F32)
        vs_d = cx.dram("vs_d", [T, 128], BF16)
        vw_d = cx.dram("vw_d", [T, 128], BF16)
        gates_d = cx.dram("gates_d", [T, 24], F32)
        dqkT = cx.dram("dqkT", [4, 128, T], F32)
        dv_d = cx.dram("dv_d", [T, 512], BF16)
        di_d = cx.dram("di_d", [4, T], F32)
        df_d = cx.dram("df_d", [4, T], F32)
        og_d = cx.dram("og_d", [T, 512], F32)
        oTcd = cx.dram("oTcd", [8, 128, T], BF16)
        X0 = 2848
        specs = []
        for c in range(4):
            specs.append(dict(kind="rope", a=c * 128, b=X0 + c * 128, dst=qTn.t[c]))
        for g in range(2):
            specs.append(dict(kind="rope", a=X0 + 512 + g * 128, b=X0 + 768 + g * 128, dst=ksT.t[g]))
            specs.append(dict(kind="rope", a=X0 + 1024 + g * 128, b=X0 + 1280 + g * 128, dst=kwT.t[g]))
        specs.append(dict(kind="raw", a=512, n=128, dst=kcT.t))
        specs.append(dict(kind="raw", a=640, n=128, dst=vcT.t))
        specs.append(dict(kind="tm", a=896, n=128, dst=vs_d.t))
        specs.append(dict(kind="tm", a=1152, n=128, dst=vw_d.t))
        specs.append(dict(kind="tm", a=1280, n=24, dst=gates_d.t, act="sigmoid"))
        for c in range(4):
            specs.append(dict(kind="raw", a=1304 + c * 128, n=128, dst=dqkT.t[c]))
        specs.append(dict(kind="tm", a=1816, n=512, dst=dv_d.t))
        specs.append(dict(kind="raw", a=2328, n=4, dst=di_d.t))
        specs.append(dict(kind="raw", a=2332, n=4, dst=df_d.t))
        specs.append(dict(kind="tm", a=2336, n=512, dst=og_d.t, act="sigmoid"))
        if en("proj"):
          proj_stage(cx, xB, I["mix_norm"].t[1], [(I["cd_w_in"].t, 2848), (I["cd_w_ext"].t, 1536)], specs,
                   ident, [qTn, ksT, kwT, kcT, vcT, vs_d, vw_d, gates_d, dqkT, dv_d, di_d, df_d, og_d], consts)
        kcmp_d = cx.dram("kcmp_d", [2, 128, 256], BF16)
        vcmp_d = cx.dram("vcmp_d", [2, 256, 64], BF16)
        addm_d = cx.dram("addm_d", [2, 128, T], BF16)
        ocmp_d = cx.dram("ocmp_d", [T, 512], F32)
        oslc_d = cx.dram("oslc_d", [T, 512], F32)
        owin_d = cx.dram("owin_d", [T, 512], F32)
        Bscr = cx.dram("Bscr", [4, T], F32)
        if en("nsa_compress"):
          nsa_compress_stage(cx, kcT, I["nsa_pe_kT"].t, I["nsa_w1_k"].t, I["nsa_w2_k_dup"].t,
                           I["nsa_w2_k_rot"].t, True, kcmp_d, consts)
        if en("nsa_compress"):
          nsa_compress_stage(cx, vcT, I["nsa_pe_vT"].t, I["nsa_w1_v"].t, I["nsa_w2_v"].t, None, False,
                           vcmp_d, consts)
        if en("nsa_cmp"):
          nsa_cmp_stage(cx, qTn, kcmp_d, vcmp_d, gates_d, ocmp_d, addm_d, ident, consts)
        if en("nsa_kv"):
          nsa_kv_stage(cx, qTn, ksT, vs_d, gates_d, oslc_d, "slc", consts, addm_d=addm_d)
        if en("nsa_kv"):
          nsa_kv_stage(cx, qTn, kwT, vw_d, gates_d, owin_d, "win", consts)
        if en("nsa_combine"):
          nsa_combine_stage(cx, [ocmp_d, oslc_d, owin_d], oTcd, ident)
        mprm = {n: I["mlstm_" + n].t for n in ("conv_w", "conv_b", "ig_b", "fg_b", "norm")}
        if en("mlstm"):
          mlstm_stage(cx, dqkT, di_d, df_d, dv_d, og_d, mprm, Bscr, oTcd, ident, identf, consts)
        if en("outproj"):
          outproj_stage(cx, oTcd, I["cd_w_out"].t, xB, xA)
        if en("ffn"):
          ffn_stage(cx, xA, out, I["ffb_norm"].t[1], I["ffb_gate"].t[1], I["ffb_up"].t[1],
                    I["ffb_down"].t[1], ident, final_w=I["final_norm"].t)
        print("nins", cx.nins)
        return nc


def _dup(w):
    return np.concatenate([w, w], axis=1)


def host_layout(inp):
    f = lambda a: np.ascontiguousarray(np.asarray(a, dtype=np.float32))
    o = {}
    for k in ("ffa_norm", "ffa_gate", "ffa_up", "ffa_down", "mix_norm", "ffb_norm", "ffb_gate", "ffb_up",
              "ffb_down", "final_norm"):
        o[k] = f(inp[k])
    wab = f(inp["ab_w_in"][0])
    o["ab_w_in"] = wab
    o["ab_w_rot"] = rot_cols(wab[:, 0:1024])
    o["ab_w_out"] = f(inp["ab_w_out"][0])
    o["diff_lam"] = f(inp["diff_lam"][0]).reshape(256)
    o["diff_subln"] = f(inp["diff_subln"][0])
    for n in ("mu", "w0", "w2", "a0", "a2", "g2", "kk", "ka", "lnw", "lnb"):
        o["rwkv_" + n] = f(inp["rwkv_" + n][0])
    o["rwkv_rk"] = f(inp["rwkv_rk"][0]).reshape(512)
    wcd = f(inp["cd_w_in"][0])
    o["cd_w_in"] = wcd
    ks = [_dup(wcd[:, 768 + g * 64:768 + (g + 1) * 64]) for g in range(2)]
    kw = [_dup(wcd[:, 1024 + g * 64:1024 + (g + 1) * 64]) for g in range(2)]
    ext = [rot_cols(wcd[:, 0:512])] + ks + [rot_cols(k) for k in ks] + kw + [rot_cols(k) for k in kw]
    o["cd_w_ext"] = np.ascontiguousarray(np.concatenate(ext, axis=1))
    o["cd_w_out"] = f(inp["cd_w_out"][0])
    o["nsa_pe_kT"] = np.ascontiguousarray(_dup(f(inp["nsa_pe_k"][0])).T)
    o["nsa_pe_vT"] = np.ascontiguousarray(_dup(f(inp["nsa_pe_v"][0])).T)
    o["nsa_w1_k"] = f(inp["nsa_w1_k"][0])
    o["nsa_w1_v"] = f(inp["nsa_w1_v"][0])
    w2k = f(inp["nsa_w2_k"][0])
    o["nsa_w2_k_dup"] = np.ascontiguousarray(_dup(w2k))
    o["nsa_w2_k_rot"] = np.ascontiguousarray(_dup(rot_cols(w2k)))
    o["nsa_w2_v"] = f(inp["nsa_w2_v"][0])
    for n in ("conv_w", "conv_b", "ig_b", "fg_b", "norm"):
        o["mlstm_" + n] = f(inp["mlstm_" + n][0])
    return o


def host_consts_full():
    c = host_consts(T)
    bf = ml_dtypes.bfloat16
    key = np.arange(128)[:, None, None]
    q = np.arange(512)[None, None, :]
    dl = np.arange(4)[None, :, None]
    c["c_wmask"] = ((128 * dl + key) > q).astype(np.float32).astype(bf)
    inv = 10000.0 ** (-np.arange(0, 64, 2, dtype=np.float64) / 64)
    d = np.arange(128) % 64
    cpos = (16.0 * np.arange(256) + 31.0)
    ang = cpos[None, :] * inv[d % 32][:, None]
    sgn = np.where(d < 32, -1.0, 1.0)[:, None]
    c["c_ccos"] = np.cos(ang).astype(np.float32)
    c["c_csin"] = (np.sin(ang) * sgn).astype(np.float32)
    n = np.arange(128)[:, None, None, None] + 128 * np.arange(2)[None, :, None, None]
    t = 512 * np.arange(8)[None, None, :, None] + np.arange(512)[None, None, None, :]
    c["c_cmpmask"] = ((16 * n + 31) <= t).astype(np.float32).astype(bf)
    c0 = 16 * np.arange(256)[:, None]
    s0 = 64 * np.arange(64)[None, :]
    shared = np.clip(np.minimum(c0 + 32, s0 + 64) - np.maximum(c0, s0), 0, None) / 32.0
    shared[255:, :] = 0.0
    c["c_selmap"] = shared.astype(np.float32).astype(bf)
    tt = np.arange(T)[:, None]
    blk = np.arange(64)[None, :]
    cur = tt // 64
    forced = (blk == 0) | (blk == cur) | (blk == cur - 1)
    valid = blk * 64 <= tt
    c["c_selbias"] = np.where(valid, np.where(forced, 1e4, 0.0), -1e30).astype(np.float32)
    c["c_eexp"] = (np.arange(64)[:, None] == (np.arange(T)[None, :] // 64)).astype(np.float32)
    c["c_eexp"] = np.concatenate([c["c_eexp"], c["c_eexp"]], 0).astype(bf)
    return c


_PROG = {}


def kernel(**inputs):
    if "nc" not in _PROG:
        _PROG["nc"] = build_program()
    nc = _PROG["nc"]
    shared = host_layout(inputs)
    shared.update(host_consts_full())
    x = np.asarray(inputs["x"], dtype=np.float32)
    in_maps = []
    for b in range(8):
        m = dict(shared)
        m["x"] = np.ascontiguousarray(x[b])
        in_maps.append(m)
    res = run_bass_kernel_spmd(nc, in_maps, core_ids=list(range(8)))
    return np.stack([np.asarray(r["out"], dtype=np.float32) for r in res.results], axis=0)
```
